# Optimizing a Trainium2 kernel written in Bass

```python
import jax, jax.numpy as jnp
from jax import lax
import numpy as np

D_MODEL = 1024
BATCH = 16
SEQ = 256
DEPTH = 4
DEC_BATCH = 8
DEC_SEQ = 1024
PAST_LEN = 512

GRID_W = 64
N_MIXERS = 2
N_DELTA = (DEPTH + 1) // 2
N_ATTN = DEPTH // 2
DN_DK = 128
DN_DV = 128
DN_HEADS = D_MODEL // DN_DK
DN_CONV = 3
DN_CHUNK = 64
DN_KD = DN_HEADS * DN_DK
DN_VD = DN_HEADS * DN_DV
DN_QKV = 2 * DN_KD + DN_VD
DN_PROJ = DN_QKV + DN_VD + 4 * DN_HEADS
AT_HEAD_DIM = 64
AT_HEADS = D_MODEL // AT_HEAD_DIM
AT_KV_HEADS = AT_HEADS // 4
AT_GROUPS = AT_HEADS // AT_KV_HEADS
AT_WINDOW = 128
AT_QBLK = 128
AT_PROJ = (AT_HEADS + 2 * AT_KV_HEADS) * AT_HEAD_DIM
ROPE_BASE = 10000.0
FFN_DIM = ((8 * D_MODEL // 3 + 127) // 128) * 128
FFN_CONV = 3
EPS = 1e-6

kernel_name = "hybrid_deltanet_swa_diffusion_step"

F32 = jnp.float32


def rmsnorm(x, g):
    x32 = x.astype(F32)
    y = x32 * lax.rsqrt(jnp.mean(x32 * x32, axis=-1, keepdims=True) + EPS)
    return (y * g.astype(F32)).astype(x.dtype)


def l2norm(x):
    x32 = x.astype(F32)
    return x32 * lax.rsqrt(jnp.sum(x32 * x32, axis=-1, keepdims=True) + EPS)


def dwconv_centred(x, w, b=None):
    K = w.shape[0]
    pad = K // 2
    T = x.shape[1]
    xp = jnp.pad(x, ((0, 0), (pad, pad), (0, 0)))
    y = sum(xp[:, k:k + T] * w[k] for k in range(K))
    if b is not None:
        y = y + b
    return y


def modulation(cvec, w_ada, b_ada):
    m = (jax.nn.silu(cvec) @ w_ada + b_ada)[:, None, :]
    return jnp.split(m, 6, axis=-1)


def pre(x, g, shift, scale):
    return rmsnorm(x, g) * (1 + scale) + shift


def chunk_gated_delta(q, k, v, g, beta, s0):
    B, T, H, DK = q.shape
    DV = v.shape[-1]
    C = DN_CHUNK
    n = T // C

    def blocks(a):
        a = jnp.moveaxis(a.astype(F32), 2, 1)
        return a.reshape((a.shape[0], a.shape[1], n, C) + a.shape[3:])

    qc = blocks(q) * (DK ** -0.5)
    kc = blocks(k)
    vc = blocks(v)
    bc = blocks(beta)
    gc = jnp.cumsum(blocks(g), axis=-1)
    idx = jnp.arange(C)
    lower = idx[:, None] >= idx[None, :]
    strict = idx[:, None] > idx[None, :]
    decay = jnp.exp(jnp.where(lower, gc[..., :, None] - gc[..., None, :], -jnp.inf))
    kk = jnp.einsum('bhnik,bhnjk->bhnij', kc * bc[..., None], kc) * decay
    a_mat = jnp.where(strict, kk, 0.0) + jnp.eye(C, dtype=F32)
    rhs = jnp.concatenate([vc * bc[..., None], kc * (bc * jnp.exp(gc))[..., None]], axis=-1)
    sol = lax.linalg.triangular_solve(a_mat, rhs, left_side=True, lower=True, unit_diagonal=True)
    u, w = sol[..., :DV], sol[..., DV:]
    qk = jnp.where(lower, jnp.einsum('bhnik,bhnjk->bhnij', qc, kc) * decay, 0.0)

    def step(S, xs):
        q_i, k_i, u_i, w_i, g_i, qk_i = xs
        v_new = u_i - jnp.einsum('bhck,bhkv->bhcv', w_i, S)
        o_i = (jnp.einsum('bhck,bhkv->bhcv', q_i * jnp.exp(g_i)[..., None], S)
               + jnp.einsum('bhij,bhjv->bhiv', qk_i, v_new))
        g_last = g_i[..., -1]
        S = (S * jnp.exp(g_last)[..., None, None]
             + jnp.einsum('bhck,bhcv->bhkv', k_i * jnp.exp(g_last[..., None] - g_i)[..., None], v_new))
        return S, o_i

    xs = tuple(jnp.moveaxis(a, 2, 0) for a in (qc, kc, u, w, gc, qk))
    s_fin, o = lax.scan(step, s0.astype(F32), xs)
    o = jnp.transpose(o, (1, 0, 3, 2, 4)).reshape(B, T, H, DV)
    return o, s_fin


def gated_delta_mixer(h, s0, w_in, conv_w, a_log, dt_bias, onorm_g, w_out):
    B, T, _ = h.shape
    proj = h @ w_in
    qkv = jax.nn.silu(dwconv_centred(proj[..., :DN_QKV], conv_w))
    q = l2norm(qkv[..., :DN_KD].reshape(B, T, DN_HEADS, DN_DK))
    k = l2norm(qkv[..., DN_KD:2 * DN_KD].reshape(B, T, DN_HEADS, DN_DK))
    v = qkv[..., 2 * DN_KD:].reshape(B, T, DN_HEADS, DN_DV)
    z = proj[..., DN_QKV:DN_QKV + DN_VD].reshape(B, T, DN_HEADS, DN_DV)
    ab = proj[..., DN_QKV + DN_VD:].astype(F32).reshape(B, T, 2, 2, DN_HEADS)
    g = -jnp.exp(a_log.astype(F32)) * jax.nn.softplus(ab[:, :, 0] + dt_bias.astype(F32))
    beta = jax.nn.sigmoid(ab[:, :, 1])
    flip = lambda a: jnp.flip(a, axis=1)
    o_f, s_f = chunk_gated_delta(q, k, v, g[:, :, 0], beta[:, :, 0], s0[:, 0])
    o_b, s_b = chunk_gated_delta(flip(q), flip(k), flip(v), flip(g[:, :, 1]), flip(beta[:, :, 1]), s0[:, 1])
    o = o_f + flip(o_b)
    o = rmsnorm(o, onorm_g) * jax.nn.silu(z.astype(F32))
    y = o.reshape(B, T, DN_VD).astype(h.dtype) @ w_out
    return y, jnp.stack([s_f, s_b], axis=1).astype(h.dtype)


def attn_proj(h, w_qkv):
    B, T, _ = h.shape
    p = h @ w_qkv
    qd = AT_HEADS * AT_HEAD_DIM
    kd = AT_KV_HEADS * AT_HEAD_DIM
    q = p[..., :qd].reshape(B, T, AT_KV_HEADS, AT_GROUPS, AT_HEAD_DIM)
    k = p[..., qd:qd + kd].reshape(B, T, AT_KV_HEADS, AT_HEAD_DIM)
    v = p[..., qd + kd:].reshape(B, T, AT_KV_HEADS, AT_HEAD_DIM)
    return q, k, v


def rope2d(x):
    T = x.shape[1]
    rows = T // GRID_W
    pos = jnp.arange(rows * GRID_W)
    row = (pos // GRID_W).astype(F32)
    col = (pos % GRID_W).astype(F32)
    half = AT_HEAD_DIM // 2
    inv = jnp.power(ROPE_BASE, -jnp.arange(0, half, 2, dtype=F32) / half)

    def rot(xh, p):
        ang = (p[:, None] * inv).reshape((1, T) + (1,) * (x.ndim - 3) + (-1,))
        cos, sin = jnp.cos(ang), jnp.sin(ang)
        x1, x2 = jnp.split(xh.astype(F32), 2, axis=-1)
        return jnp.concatenate([x1 * cos - x2 * sin, x1 * sin + x2 * cos], axis=-1)

    return jnp.concatenate([rot(x[..., :half], row), rot(x[..., half:], col)], axis=-1).astype(x.dtype)


def softmax_with_sink(s, sink):
    sk = jnp.broadcast_to(sink.astype(F32).reshape(1, AT_KV_HEADS, AT_GROUPS, 1, 1), s.shape[:-1] + (1,))
    return jax.nn.softmax(jnp.concatenate([s, sk], axis=-1), axis=-1)[..., :-1]


def context_attention(q, k, v, sink):
    B, T = q.shape[:2]
    n = T // AT_QBLK
    scale = AT_HEAD_DIM ** -0.5
    qb = jnp.moveaxis(q.reshape(B, n, AT_QBLK, AT_KV_HEADS, AT_GROUPS, AT_HEAD_DIM), 1, 0)

    def blk(q_i):
        s = jnp.einsum('bqkgd,bskd->bkgqs', q_i, k, preferred_element_type=F32) * scale
        p = softmax_with_sink(s, sink)
        return jnp.einsum('bkgqs,bskd->bqkgd', p.astype(v.dtype), v)

    o = lax.map(blk, qb)
    return jnp.moveaxis(o, 0, 1).reshape(B, T, AT_HEADS * AT_HEAD_DIM)


def latent_attention(q, k, v, k_ctx, v_ctx, sink):
    B, T = q.shape[:2]
    n = T // AT_QBLK
    W = AT_WINDOW
    band = AT_QBLK + 2 * W
    scale = AT_HEAD_DIM ** -0.5
    kp = jnp.pad(k, ((0, 0), (W, W), (0, 0), (0, 0)))
    vp = jnp.pad(v, ((0, 0), (W, W), (0, 0), (0, 0)))
    qb = jnp.moveaxis(q.reshape(B, n, AT_QBLK, AT_KV_HEADS, AT_GROUPS, AT_HEAD_DIM), 1, 0)
    qi_idx = jnp.arange(AT_QBLK)
    kj_idx = jnp.arange(band)

    def blk(xs):
        q_i, b = xs
        start = b * AT_QBLK
        kb = lax.dynamic_slice_in_dim(kp, start, band, axis=1)
        vb = lax.dynamic_slice_in_dim(vp, start, band, axis=1)
        qpos = start + qi_idx
        kpos = start - W + kj_idx
        valid = (jnp.abs(qpos[:, None] - kpos[None, :]) <= W) & (kpos[None, :] >= 0) & (kpos[None, :] < T)
        s_loc = jnp.einsum('bqkgd,bskd->bkgqs', q_i, kb, preferred_element_type=F32) * scale
        s_loc = jnp.where(valid, s_loc, -jnp.inf)
        s_ctx = jnp.einsum('bqkgd,bskd->bkgqs', q_i, k_ctx, preferred_element_type=F32) * scale
        p = softmax_with_sink(jnp.concatenate([s_loc, s_ctx], axis=-1), sink).astype(v.dtype)
        return (jnp.einsum('bkgqs,bskd->bqkgd', p[..., :band], vb)
                + jnp.einsum('bkgqs,bskd->bqkgd', p[..., band:], v_ctx))

    o = lax.map(blk, (qb, jnp.arange(n)))
    return jnp.moveaxis(o, 0, 1).reshape(B, T, AT_HEADS * AT_HEAD_DIM)


def conv_ffn(h, w_up, conv_w, conv_b, w_down):
    u = dwconv_centred(h @ w_up, conv_w, conv_b)
    a, b = jnp.split(u, 2, axis=-1)
    return (jax.nn.silu(a) * b) @ w_down


def setup_inputs(seed: int = 0) -> dict:
    key = jax.random.key(seed)
    ks = jax.random.split(key, 24)
    nrm = lambda k, s, sc: jax.random.normal(k, s, F32) * sc
    D = D_MODEL
    return {
        "x_prompt": nrm(ks[0], (BATCH, SEQ, D), 1.0),
        "x_sample": nrm(ks[1], (DEC_BATCH, DEC_SEQ, D), 1.0),
        "state_delta": nrm(ks[2], (DEC_BATCH, N_DELTA, 2, DN_HEADS, DN_DK, DN_DV), 0.2),
        "cache_k": nrm(ks[3], (DEC_BATCH, N_ATTN, PAST_LEN, AT_KV_HEADS, AT_HEAD_DIM), 1.0),
        "cache_v": nrm(ks[4], (DEC_BATCH, N_ATTN, PAST_LEN, AT_KV_HEADS, AT_HEAD_DIM), 1.0),
        "c": nrm(ks[5], (DEC_BATCH, D), 1.0),
        "c_ctx": nrm(ks[6], (D,), 1.0),
        "w_ada": nrm(ks[7], (DEPTH, D, 6 * D), 0.5 * D ** -0.5),
        "b_ada": nrm(ks[8], (DEPTH, 6 * D), 0.02),
        "norm_g": 1.0 + nrm(ks[9], (DEPTH, 4, D), 0.1),
        "dn_w_in": nrm(ks[10], (N_DELTA, D, DN_PROJ), D ** -0.5),
        "dn_conv_w": nrm(ks[11], (N_DELTA, DN_CONV, DN_QKV), DN_CONV ** -0.5),
        "dn_a_log": jnp.log(jax.random.uniform(ks[12], (N_DELTA, 2, DN_HEADS), F32, 1.0, 16.0)),
        "dn_dt_bias": jax.random.uniform(ks[13], (N_DELTA, 2, DN_HEADS), F32, -4.6, -2.3),
        "dn_onorm_g": 1.0 + nrm(ks[14], (N_DELTA, DN_DV), 0.1),
        "dn_w_out": nrm(ks[15], (N_DELTA, DN_VD, D), DN_VD ** -0.5),
        "at_w_qkv": nrm(ks[16], (N_ATTN, D, AT_PROJ), D ** -0.5),
        "at_sink": nrm(ks[17], (N_ATTN, AT_HEADS), 0.5),
        "at_w_o": nrm(ks[18], (N_ATTN, AT_HEADS * AT_HEAD_DIM, D), (AT_HEADS * AT_HEAD_DIM) ** -0.5),
        "ffn_w_up": nrm(ks[19], (DEPTH, D, 2 * FFN_DIM), D ** -0.5),
        "ffn_conv_w": nrm(ks[20], (DEPTH, FFN_CONV, 2 * FFN_DIM), FFN_CONV ** -0.5),
        "ffn_conv_b": nrm(ks[21], (DEPTH, 2 * FFN_DIM), 0.02),
        "ffn_w_down": nrm(ks[22], (DEPTH, FFN_DIM, D), FFN_DIM ** -0.5),
    }


def reference(x_prompt, x_sample, state_delta, cache_k, cache_v, c, c_ctx,
              w_ada, b_ada, norm_g,
              dn_w_in, dn_conv_w, dn_a_log, dn_dt_bias, dn_onorm_g, dn_w_out,
              at_w_qkv, at_sink, at_w_o,
              ffn_w_up, ffn_conv_w, ffn_conv_b, ffn_w_down):
    xp, xs = x_prompt, x_sample
    st_new, ck_new, cv_new = [], [], []
    for i in range(DEPTH):
        j = i // N_MIXERS
        mp = modulation(c_ctx[None, :], w_ada[i], b_ada[i])
        ms = modulation(c, w_ada[i], b_ada[i])
        hp = pre(xp, norm_g[i, 0], mp[0], mp[1])
        hs = pre(xs, norm_g[i, 0], ms[0], ms[1])
        if i % N_MIXERS == 0:
            s_zero = jnp.zeros((xp.shape[0], 2, DN_HEADS, DN_DK, DN_DV), xp.dtype)
            yp, sp = gated_delta_mixer(hp, s_zero, dn_w_in[j], dn_conv_w[j], dn_a_log[j],
                                       dn_dt_bias[j], dn_onorm_g[j], dn_w_out[j])
            ys, _ = gated_delta_mixer(hs, state_delta[:, j], dn_w_in[j], dn_conv_w[j], dn_a_log[j],
                                      dn_dt_bias[j], dn_onorm_g[j], dn_w_out[j])
            st_new.append(sp)
        else:
            qp, kp, vp = attn_proj(hp, at_w_qkv[j])
            yp = context_attention(qp, kp, vp, at_sink[j]) @ at_w_o[j]
            qs, ks_, vs = attn_proj(hs, at_w_qkv[j])
            ys = latent_attention(rope2d(qs), rope2d(ks_), vs, cache_k[:, j], cache_v[:, j],
                                  at_sink[j]) @ at_w_o[j]
            ck_new.append(kp)
            cv_new.append(vp)
        xp = xp + mp[2] * rmsnorm(yp, norm_g[i, 1])
        xs = xs + ms[2] * rmsnorm(ys, norm_g[i, 1])
        hp = pre(xp, norm_g[i, 2], mp[3], mp[4])
        hs = pre(xs, norm_g[i, 2], ms[3], ms[4])
        fp = conv_ffn(hp, ffn_w_up[i], ffn_conv_w[i], ffn_conv_b[i], ffn_w_down[i])
        fs = conv_ffn(hs, ffn_w_up[i], ffn_conv_w[i], ffn_conv_b[i], ffn_w_down[i])
        xp = xp + mp[5] * rmsnorm(fp, norm_g[i, 3])
        xs = xs + ms[5] * rmsnorm(fs, norm_g[i, 3])
    y_prompt = xp
    y_sample = xs
    state_delta_new = jnp.stack(st_new, axis=1)
    cache_k_new = jnp.stack(ck_new, axis=1)
    cache_v_new = jnp.stack(cv_new, axis=1)
    return (y_prompt, y_sample, state_delta_new, cache_k_new, cache_v_new)
```

```python
import contextlib
import os
DN_STAGE = int(os.environ.get('DN_STAGE', '5'))
import numpy as np
import concourse.bass as bass
import concourse.mybir as mybir
from concourse.bass_utils import run_bass_kernel_spmd

F32 = mybir.dt.float32
BF16 = mybir.dt.bfloat16
ALU = mybir.AluOpType
AF = mybir.ActivationFunctionType
AX = mybir.AxisListType

D = 1024
NT = 1536
DEPTH = 4
FFN = 2816
EPS = 1e-6
BIG = 200.0
SEGS = [(0, 256), (256, 512), (512, 1536)]
TBS = [(0, 512, 0), (512, 1024, 1), (1024, 1536, 1)]
SLOT = 4096
NRING = 2


class Grp:
    def __init__(self, name, sem):
        self.name = name
        self.sem = sem
        self.count = 0


class Prog:
    def __init__(self, nc, es, needed=None):
        self.nc = nc
        self.es = es
        self.needed = needed
        self.used = set()
        self.semval = {}
        self.inc = {}
        self.eng = {'pe': nc.tensor, 'dve': nc.vector, 'act': nc.scalar, 'pool': nc.gpsimd, 'sp': nc.sync}
        self.sem = {k: es.enter_context(nc.semaphore('s_' + k)) for k in self.eng}
        self.cnt = {k: 0 for k in self.eng}
        self.waited = {k: {} for k in self.eng}
        self.lastw = {}
        self.readers = {}
        self.groups = []
        self.nwait = 0

    def group(self, name):
        g = Grp(name, self.es.enter_context(self.nc.semaphore('g_' + name)))
        self.groups.append(g)
        return g

    def _wait(self, eng, ev):
        if ev[0] == 'c':
            src, val = ev[1], ev[2]
            if src == eng and eng in ('pe', 'sp'):
                return
            sem, name = self.sem[src], src
        else:
            g = ev[1]
            sem, name, val = g.sem, g.name, g.count
        if self.waited[eng].get(name, 0) >= val:
            return
        self.waited[eng][name] = val
        if ev[0] == 'c':
            self.used.add((src, val))
            if self.needed is not None:
                val = self.semval[(src, val)]
        self.eng[eng].wait_ge(sem, val)
        self.nwait += 1

    def _deps(self, eng, r, w):
        for k in r:
            ev = self.lastw.get(k)
            if ev is not None:
                self._wait(eng, ev)
            if k[0] == 'ps':
                for en2, ev2 in self.readers.get(k, {}).items():
                    if en2 != eng:
                        self._wait(eng, ev2)
        for k in w:
            ev = self.lastw.get(k)
            if ev is not None:
                self._wait(eng, ev)
            for ev in self.readers.get(k, {}).values():
                self._wait(eng, ev)

    def _record(self, ev, evname, r, w):
        for k in r:
            self.readers.setdefault(k, {})[evname] = ev
        for k in w:
            self.lastw[k] = ev
            self.readers[k] = {}

    def op(self, eng, fn, r=(), w=()):
        self._deps(eng, r, w)
        ins = fn(self.eng[eng])
        self.cnt[eng] += 1
        n = self.cnt[eng]
        if self.needed is None:
            ins.then_inc(self.sem[eng], 1)
        elif (eng, n) in self.needed:
            self.inc[eng] = self.inc.get(eng, 0) + 1
            self.semval[(eng, n)] = self.inc[eng]
            ins.then_inc(self.sem[eng], 1)
        self._record(('c', eng, n), eng, r, w)

    def dma(self, q, grp, out, in_, r=(), w=(), slow=False):
        self._deps(q, r, w)
        if slow:
            ins = self.eng[q].dma_start(out=out, in_=in_, allow_slow_non_contiguous=True)
        else:
            ins = self.eng[q].dma_start(out=out, in_=in_)
        ins.then_inc(grp.sem, 16)
        grp.count += 16
        self._record(('d', grp), grp.name, r, w)

    def barrier(self):
        for e in self.eng:
            for s in ('pe', 'dve', 'act', 'pool'):
                if s != e and self.cnt[s] > 0:
                    self._wait(e, ('c', s, self.cnt[s]))
            for g in self.groups:
                if g.count > 0:
                    self._wait(e, ('d', g))
        self.lastw = {}
        self.readers = {}


def kk(name, *idx):
    out = [(name,)]
    for i in idx:
        if isinstance(i, int):
            out = [o + (i,) for o in out]
        else:
            out = [o + (j,) for o in out for j in i]
    return out


def build_program(n_layers=DEPTH, mixers=3, dbg=False):
    _, P1 = build_pass(n_layers, mixers, dbg, None)
    return build_pass(n_layers, mixers, dbg, P1.used)


def build_pass(n_layers, mixers, dbg, needed):
    nc = bass.Bass("TRN2", target_bir_lowering=False)
    es = contextlib.ExitStack()
    P = Prog(nc, es, needed)

    def din(name, shape):
        return nc.dram_tensor(name, list(shape), F32, kind="ExternalInput").ap()

    def dout(name, shape):
        return nc.dram_tensor(name, list(shape), F32, kind="ExternalOutput").ap()

    xin = din("x_in", (NT, D))
    state_in = din("state_in", (2, 2, 8, 128, 128))
    ck_in = din("ck_in", (2, 512, 256))
    cv_in = din("cv_in", (2, 512, 256))
    c_in = din("c_in", (2, D))
    w_ada = din("w_ada", (DEPTH, D, 6 * D))
    b_ada = din("b_ada", (DEPTH, 6 * D))
    norm_g = din("norm_g", (DEPTH, 4, D))
    dn_w_in = din("dn_w_in", (2, D, 4128))
    dn_conv_w = din("dn_conv_w", (2, 3, 3072))
    dn_a_log = din("dn_a_log", (2, 16))
    dn_dt_bias = din("dn_dt_bias", (2, 16))
    dn_onorm_g = din("dn_onorm_g", (2, 128))
    dn_w_out = din("dn_w_out", (2, D, D))
    at_w_qkv = din("at_w_qkv", (2, D, 1536))
    at_sink = din("at_sink", (2, 16))
    at_w_o = din("at_w_o", (2, D, D))
    ffn_w_up = din("ffn_w_up", (DEPTH, D, 2 * FFN))
    ffn_conv_w = din("ffn_conv_w", (DEPTH, 3, 2 * FFN))
    ffn_conv_b = din("ffn_conv_b", (DEPTH, 2 * FFN))
    ffn_w_down = din("ffn_w_down", (DEPTH, FFN, D))
    cst = din("cst", (128, 12, 128))
    rope_t = din("rope_t", (128, 2, 1024))

    y_out = dout("y_out", (NT, D))
    state_out = dout("state_out", (2, 2, 2, 8, 128, 128))
    ck_out = dout("ck_out", (2, 2, 256, 256))
    cv_out = dout("cv_out", (2, 2, 256, 256))
    dbg_out = dout("dbg_out", (8, 128, 8, NT)) if dbg else None

    uid = [0]

    def sb(name, shape, dt=F32, stack=None):
        uid[0] += 1
        return (stack or es).enter_context(nc.sbuf_tensor("%s_%d" % (name, uid[0]), list(shape), dt))

    xT = sb("xT", (128, 8, NT))
    cst_f = sb("cst_f", (128, 12, 128))
    cst_b = sb("cst_b", (128, 12, 128), BF16)
    ring = [sb("ring%d" % i, (128, SLOT), BF16) for i in range(NRING)]
    ring_g = [P.group("ring%d" % i) for i in range(NRING)]
    ring_i = [0]
    ada_ref = [None]
    ada_g = P.group("ada")
    scT = sb("scT", (128, 8, 2), BF16)
    modT = sb("modT", (128, 48, 2))
    mAB = sb("mAB", (128, 6, 8, 2))
    ngT = sb("ngT", (128, 4, 8))
    badaT = sb("badaT", (128, 48))
    sq = [sb("sq%d" % i, (128, 512), BF16) for i in range(2)]
    rs_t = sb("rs_t", (128, 512))
    rstd = sb("rstd", (128, 512))
    ntmp = [sb("ntmp0", (128, 512))] * 2
    cw_all = sb("cw_all", (128, 3, 44))
    cb_all = sb("cb_all", (128, 44))
    PS = es.enter_context(nc.psum_tensor("PS", [128, 4096], F32))

    g_in = P.group("in")
    g_misc = P.group("misc")
    g_out = P.group("out")
    g_poolm = P.group("poolm")
    g_stg = [P.group("stg0"), P.group("stg1")]
    g_craw = P.group("craw")
    g_bada = P.group("bada")
    g_ng = P.group("ng")
    g_cwcb = P.group("cwcb")
    g_att = P.group("att")
    g_esink = P.group("esink")
    g_dnc = P.group("dnc")
    g_S = {2: P.group("S2"), 5: P.group("S5")}
    g_so = P.group("so")
    g_kv = P.group("kv")
    g_os = [P.group("os0"), P.group("os1")]
    out_groups = [g_out, g_so, g_kv, g_os[0], g_os[1]]

    IDF = cst_f[:, 0, :]
    IDB = cst_b[:, 0, :]
    ONESB = cst_b[:, 1, :]

    def bank(b):
        return PS[:, b * 512:(b + 1) * 512]

    def bk(*bs):
        return [('ps', b) for b in bs]

    def next_slot():
        i = ring_i[0] % NRING
        ring_i[0] += 1
        return i

    P.dma('sp', g_in, cst_f[:], cst, w=[('cst',)])
    P.dma('pool', g_poolm, cst_b[:], cst, w=[('cstb',)])

    with contextlib.ExitStack() as st0:
        craw = sb("craw", (128, 2, 8), stack=st0)
        for g in range(2):
            P.dma('sp', g_craw, craw[:, g, :], c_in[g:g + 1, :].rearrange("o (kc p) -> p (o kc)", p=128),
                  w=[('craw',)], slow=True)
        P.op('act', lambda e: e.activation(out=scT[:].rearrange("p k g -> p g k"), in_=craw[:], func=AF.Silu),
             r=[('craw',)], w=[('scT',)])
        P.barrier()

    def phase0_gen(st0):
        stage = [sb("xstage%d" % i, (128, D), stack=st0) for i in range(2)]
        for tt in range(12):
            s = stage[tt % 2]
            P.dma('sp', g_stg[tt % 2], s[:], xin[tt * 128:(tt + 1) * 128, :], w=[('stg', tt % 2)])
            pb = (tt % 2) * 2
            for dc in range(8):
                b = pb + dc // 4
                P.op('pe', lambda e, s=s, dc=dc, b=b: e.matmul(
                    PS[:, b * 512 + (dc % 4) * 128: b * 512 + (dc % 4) * 128 + 128],
                    lhsT=s[:, dc * 128:(dc + 1) * 128], rhs=IDF, start=True, stop=True),
                    r=[('stg', tt % 2), ('cst',)], w=bk(b))
            eng = 'act' if tt % 2 == 0 else 'dve'
            src = PS[:, pb * 512: pb * 512 + 1024].rearrange("p (c t) -> p c t", c=8)
            if eng == 'act':
                P.op('act', lambda e, tt=tt, src=src: e.copy(out=xT[:, :, tt * 128:(tt + 1) * 128], in_=src),
                     r=bk(pb, pb + 1), w=kk('xT', range(8), tt // 4))
            else:
                P.op('dve', lambda e, tt=tt, src=src: e.tensor_copy(out=xT[:, :, tt * 128:(tt + 1) * 128], in_=src),
                     r=bk(pb, pb + 1), w=kk('xT', range(8), tt // 4))
            yield

    def load_w(eng_q, src2d, kc, ncol, key):
        i = next_slot()
        view = ring[i][:, 0:kc * ncol].rearrange("p (k n) -> p k n", k=kc)
        P.dma('pool', ring_g[i], view, src2d.rearrange("(k p) n -> p k n", p=128), w=[('ring', i)])
        return i, view

    def mod_stream(L):
        mb = 7
        for piece in range(12):
            view = ada_ref[0][:, 0:4096].rearrange("p (k n) -> p k n", k=8)
            P.dma('pool', ada_g, view, w_ada[L][:, piece * 512:(piece + 1) * 512].rearrange("(k p) n -> p k n", p=128), w=[('adaslot',)])
            i = 'ada'
            for o4 in range(4):
                oc = piece * 4 + o4
                for kc in range(8):
                    P.op('pe', lambda e, view=view, o4=o4, kc=kc, oc=oc: e.matmul(
                        PS[:, mb * 512 + 2 * oc: mb * 512 + 2 * oc + 2],
                        lhsT=view[:, kc, o4 * 128:(o4 + 1) * 128], rhs=scT[:, kc, :],
                        start=(kc == 0), stop=(kc == 7)),
                        r=[('adaslot',), ('scT',)], w=bk(mb))
            yield

    def modulation(L):
        mb = 7
        P.dma('sp', g_bada, badaT[:], b_ada[L].rearrange("(oc p) -> p oc", p=128), w=[('bada',)], slow=True)
        for v in range(4):
            P.dma('sp', g_ng, ngT[:, v, :], norm_g[L][v].rearrange("(dc p) -> p dc", p=128), w=[('ngT',)], slow=True)
        P.op('dve', lambda e: e.tensor_tensor(
            out=modT[:], in0=PS[:, mb * 512: mb * 512 + 96].rearrange("p (o g) -> p o g", g=2),
            in1=badaT[:].unsqueeze(2).to_broadcast([128, 48, 2]), op=ALU.add),
            r=bk(mb) + [('bada',)], w=[('modT',)])

        def mv(v):
            return modT[:, v * 8:(v + 1) * 8, :]

        def ng(v):
            return ngT[:, v, :].unsqueeze(2).to_broadcast([128, 8, 2])
        for (dst, vs, vg) in ((0, 1, 0), (3, 4, 2)):
            P.op('dve', lambda e, dst=dst, vs=vs: e.tensor_scalar(
                out=mAB[:, dst], in0=mv(vs), scalar1=1.0, scalar2=None, op0=ALU.add),
                r=[('modT',)], w=[('mAB', dst)])
            P.op('dve', lambda e, dst=dst, vg=vg: e.tensor_tensor(
                out=mAB[:, dst], in0=mAB[:, dst], in1=ng(vg), op=ALU.mult),
                r=[('mAB', dst), ('ngT',)], w=[('mAB', dst)])
        for (dst, vs) in ((1, 0), (4, 3)):
            P.op('dve', lambda e, dst=dst, vs=vs: e.tensor_copy(out=mAB[:, dst], in_=mv(vs)),
                 r=[('modT',)], w=[('mAB', dst)])
        for (dst, vs, vg) in ((2, 2, 1), (5, 5, 3)):
            P.op('dve', lambda e, dst=dst, vs=vs, vg=vg: e.tensor_tensor(
                out=mAB[:, dst], in0=mv(vs), in1=ng(vg), op=ALU.mult),
                r=[('modT',), ('ngT',)], w=[('mAB', dst)])

    def sumsq_rstd(src_fn, src_keys_fn, nchunk, scale, tbi, psb, ncols=512):
        for dc in range(nchunk):
            s = sq[dc % 2]
            P.op('act', lambda e, s=s, dc=dc: e.activation(out=s[:, 0:ncols], in_=src_fn(dc), func=AF.Square),
                 r=src_keys_fn(dc), w=[('sq', dc % 2)])
            P.op('pe', lambda e, s=s, dc=dc: e.matmul(
                PS[:, psb * 512: psb * 512 + ncols], lhsT=ONESB, rhs=s[:, 0:ncols],
                start=(dc == 0), stop=(dc == nchunk - 1)),
                r=[('sq', dc % 2), ('cstb',)], w=bk(psb))
        P.op('act', lambda e: e.activation(out=rs_t[:, 0:ncols], in_=PS[:, psb * 512: psb * 512 + ncols],
                                           func=AF.Ln, bias=EPS, scale=scale),
             r=bk(psb), w=[('rs_t',)])
        P.op('act', lambda e: e.activation(out=rstd[:, 0:ncols], in_=rs_t[:, 0:ncols], func=AF.Exp, scale=-0.5),
             r=[('rs_t',)], w=[('rstd',)])

    def pre_norm(hT, iA, iB):
        for tbi, (t0, t1, grp) in enumerate(TBS):
            sumsq_rstd(lambda dc: xT[:, dc, t0:t1], lambda dc: [('xT', dc, tbi)], 8, 1.0 / D, tbi, 6)
            for dc in range(8):
                t = ntmp[dc % 2]
                P.op('dve', lambda e, t=t, dc=dc: e.scalar_tensor_tensor(
                    out=t[:], in0=xT[:, dc, t0:t1], scalar=mAB[:, iA, dc, grp:grp + 1], in1=rstd[:],
                    op0=ALU.mult, op1=ALU.mult),
                    r=[('xT', dc, tbi), ('mAB', iA), ('rstd',)], w=[('ntmp', 0)])
                P.op('act', lambda e, t=t, dc=dc: e.activation(
                    out=hT[:, dc, t0:t1], in_=t[:], func=AF.Identity, bias=mAB[:, iB, dc, grp:grp + 1], scale=1.0),
                    r=[('ntmp', 0), ('mAB', iB)], w=[('hT', dc, tbi)])

    def post_norm_block(yblk, tbi, iG):
        t0, t1, grp = TBS[tbi]
        sumsq_rstd(lambda dc: yblk[:, dc, :], lambda dc: [('yblk', dc)], 8, 1.0 / D, tbi, 6)
        for dc in range(8):
            t = ntmp[dc % 2]
            P.op('dve', lambda e, t=t, dc=dc: e.scalar_tensor_tensor(
                out=t[:], in0=yblk[:, dc, :], scalar=mAB[:, iG, dc, grp:grp + 1], in1=rstd[:],
                op0=ALU.mult, op1=ALU.mult),
                r=[('yblk', dc), ('mAB', iG), ('rstd',)], w=[('ntmp', 0)])
            P.op('dve', lambda e, t=t, dc=dc: e.tensor_tensor(
                out=xT[:, dc, t0:t1], in0=xT[:, dc, t0:t1], in1=t[:], op=ALU.add),
                r=[('ntmp', 0), ('xT', dc, tbi)], w=[('xT', dc, tbi)])

    def conv_taps(dst, upsum, w0, w1, w2, bias, key_dst, key_src):
        if bias is None:
            P.op('act', lambda e: e.activation(out=dst[:], in_=upsum, func=AF.Identity, bias=0.0, scale=w1),
                 r=key_src, w=key_dst)
        else:
            P.op('act', lambda e: e.activation(out=dst[:], in_=upsum, func=AF.Identity, bias=bias, scale=w1),
                 r=key_src, w=key_dst)
        u3 = upsum[:, 0:512].rearrange("p (s t) -> p s t", s=2)
        d3 = dst[:, 0:512].rearrange("p (s t) -> p s t", s=2)
        for (wt, so, do) in ((w0, 0, 1), (w2, 1, 0)):
            P.op('dve', lambda e, wt=wt, so=so, do=do: e.scalar_tensor_tensor(
                out=d3[:, :, do:do + 255], in0=u3[:, :, so:so + 255], scalar=wt, in1=d3[:, :, do:do + 255],
                op0=ALU.mult, op1=ALU.add), r=key_src + key_dst, w=key_dst)
            P.op('dve', lambda e, wt=wt, so=so, do=do: e.scalar_tensor_tensor(
                out=dst[:, 512 + do:512 + do + 1023], in0=upsum[:, 512 + so:512 + so + 1023], scalar=wt,
                in1=dst[:, 512 + do:512 + do + 1023], op0=ALU.mult, op1=ALU.add),
                r=key_src + key_dst, w=key_dst)

    def ffn(L, pump=None):
        def pump1():
            if pump is not None:
                next(pump, None)
        with contextlib.ExitStack() as stf:
            gT = sb("gT", (128, 22, NT), BF16, stack=stf)
            if pump is not None:
                ada_ref[0] = sb("ada_slot", (128, SLOT), BF16, stack=stf)
            with contextlib.ExitStack() as stu:
                hT = sb("hT", (128, 8, NT), BF16, stack=stu)
                ya = [sb("ya%d" % i, (128, NT), stack=stu) for i in range(2)]
                yb = sb("yb", (128, NT), stack=stu)
                for k3 in range(3):
                    P.dma('sp', g_cwcb, cw_all[:, k3, :], ffn_conv_w[L][k3].rearrange("(c p) -> p c", p=128), w=[('cw',)], slow=True)
                P.dma('sp', g_cwcb, cb_all[:], ffn_conv_b[L].rearrange("(c p) -> p c", p=128), w=[('cb',)], slow=True)
                pre_norm(hT, 3, 4)
                hkeys = kk('hT', range(8), range(3))
                for pr in range(11):
                    pump1()
                    i = next_slot()
                    view = ring[i][:, 0:4096].rearrange("p (k n) -> p k n", k=8)
                    P.dma('pool', ring_g[i], view[:, :, 0:256],
                          ffn_w_up[L][:, pr * 256:(pr + 1) * 256].rearrange("(k p) n -> p k n", p=128),
                          w=[('ring', i)])
                    P.dma('pool', ring_g[i], view[:, :, 256:512],
                          ffn_w_up[L][:, FFN + pr * 256:FFN + (pr + 1) * 256].rearrange("(k p) n -> p k n", p=128),
                          w=[('ring', i)])
                    for c2 in range(2):
                        j = pr * 2 + c2
                        for half in range(2):
                            pb = half * 3
                            for tbi, (t0, t1, grp) in enumerate(TBS):
                                for kc in range(8):
                                    P.op('pe', lambda e, view=view, kc=kc, half=half, c2=c2, pb=pb, tbi=tbi, t0=t0, t1=t1: e.matmul(
                                        bank(pb + tbi), lhsT=view[:, kc, half * 256 + c2 * 128: half * 256 + c2 * 128 + 128],
                                        rhs=hT[:, kc, t0:t1], start=(kc == 0), stop=(kc == 7)),
                                        r=[('ring', i)] + kk('hT', kc, tbi), w=bk(pb + tbi))
                            f = j + 22 * half
                            dst = ya[j % 2] if half == 0 else yb
                            kd = [('ya', j % 2)] if half == 0 else [('yb',)]
                            conv_taps(dst, PS[:, pb * 512: pb * 512 + NT], cw_all[:, 0, f:f + 1], cw_all[:, 1, f:f + 1],
                                      cw_all[:, 2, f:f + 1], cb_all[:, f:f + 1], kd, bk(pb, pb + 1, pb + 2) + [('cw',), ('cb',)])
                        yaj = ya[j % 2]
                        P.op('act', lambda e, yaj=yaj: e.activation(out=yaj[:], in_=yaj[:], func=AF.Silu),
                             r=[('ya', j % 2)], w=[('ya', j % 2)])
                        P.op('dve', lambda e, yaj=yaj, j=j: e.tensor_tensor(out=gT[:, j, :], in0=yaj[:], in1=yb[:], op=ALU.mult),
                             r=[('ya', j % 2), ('yb',)], w=[('gT', j)])
                P.barrier()
            with contextlib.ExitStack() as std:
                yblk = sb("yblk", (128, 8, 512), stack=std)
                for tbi, (t0, t1, grp) in enumerate(TBS):
                    for oc2 in range(8):
                        if oc2 % 4 == 0:
                            pump1()
                        i = next_slot()
                        view = ring[i][:, 0:22 * 128].rearrange("p (k n) -> p k n", k=22)
                        P.dma('pool', ring_g[i], view,
                              ffn_w_down[L][:, oc2 * 128:(oc2 + 1) * 128].rearrange("(k p) n -> p k n", p=128),
                              w=[('ring', i)])
                        pb = oc2 % 2
                        for j in range(22):
                            P.op('pe', lambda e, view=view, j=j, pb=pb, t0=t0, t1=t1: e.matmul(
                                bank(pb), lhsT=view[:, j, :], rhs=gT[:, j, t0:t1], start=(j == 0), stop=(j == 21)),
                                r=[('ring', i), ('gT', j)], w=bk(pb))
                        P.op('act', lambda e, oc2=oc2, pb=pb: e.copy(out=yblk[:, oc2, :], in_=bank(pb)),
                             r=bk(pb), w=[('yblk', oc2)])
                    post_norm_block(yblk, tbi, 5)
                if pump is not None:
                    for _ in pump:
                        pass
                P.barrier()


    def out_proj(srcT, wsrc, iG):
        with contextlib.ExitStack() as sto:
            yblk = sb("yblk", (128, 8, 512), stack=sto)
            for tbi, (t0, t1, grp) in enumerate(TBS):
                for half in range(2):
                    i, view = load_w('pool', wsrc[:, half * 512:(half + 1) * 512], 8, 512, 'wo')
                    for o4 in range(4):
                        oc = half * 4 + o4
                        pb = oc % 2
                        for kc in range(8):
                            P.op('pe', lambda e, view=view, o4=o4, kc=kc, pb=pb: e.matmul(
                                bank(pb), lhsT=view[:, kc, o4 * 128:(o4 + 1) * 128], rhs=srcT[:, kc, t0:t1],
                                start=(kc == 0), stop=(kc == 7)),
                                r=[('ring', i), ('srcT', kc, tbi)], w=bk(pb))
                        P.op('act', lambda e, oc=oc, pb=pb: e.copy(out=yblk[:, oc, :], in_=bank(pb)),
                             r=bk(pb), w=[('yblk', oc)])
                post_norm_block(yblk, tbi, iG)
            P.barrier()

    def attention(L, j):
        with contextlib.ExitStack() as sta:
            qT = sb("qT", (128, 8, NT), BF16, stack=sta)
            kT = sb("kT", (128, 2, 2, NT + 512), BF16, stack=sta)
            Vd = sb("Vd", (128, 16, 4, 128), BF16, stack=sta)
            esink = sb("esink", (128, 16), stack=sta)
            P.op('dve', lambda e: e.memset(kT[:].rearrange("p a b t -> p (a b t)"), 0.0), w=[('kTz',)])
            P.dma('sp', g_esink, esink[:], at_sink[j:j + 1, :].partition_broadcast(128), w=[('esink',)])
            P.op('act', lambda e: e.activation(out=esink[:], in_=esink[:], func=AF.Exp), r=[('esink',)], w=[('esink',)])
            with contextlib.ExitStack() as stp:
                hT = sb("hT", (128, 8, NT), BF16, stack=stp)
                rope = sb("rope", (128, 2, 1024), stack=stp)
                qraw = sb("qraw", (128, 1024), stack=stp)
                t1 = sb("t1", (128, 1024), stack=stp)
                kvst = sb("kvst", (128, 2, 4, 256), stack=stp)
                ckst = sb("ckst", (128, 4, 256), stack=stp)
                P.dma('sp', g_att, rope[:], rope_t, w=[('rope',)])
                P.dma('sp', g_att, ckst[:], ck_in[j].rearrange("(b p) f -> p b f", p=128), w=[('ckst',)])
                for b in range(4):
                    for hf in range(2):
                        P.dma('pool', g_poolm, Vd[:, 12 + b, :, hf * 64:(hf + 1) * 64],
                              cv_in[j][b * 128:(b + 1) * 128, :].rearrange("p (g d) -> p g d", g=4), w=[('Vd', 12 + b)])
                pre_norm(hT, 0, 1)
                RF = cst_f[:, 10, :]

                def proj_chunk(view, col0, dstT, q8, pb):
                    U = PS[:, pb * 512: pb * 512 + NT]
                    for tbi, (t0, t1_, grp) in enumerate(TBS):
                        for kc in range(8):
                            P.op('pe', lambda e, kc=kc, tbi=tbi, t0=t0, t1_=t1_: e.matmul(
                                bank(pb + tbi), lhsT=view[:, kc, col0:col0 + 128], rhs=hT[:, kc, t0:t1_],
                                start=(kc == 0), stop=(kc == 7)),
                                r=[('ring', view_i[0])] + kk('hT', kc, tbi), w=bk(pb + tbi))
                    if dstT is kT:
                        for g2_ in range(2):
                            hp_ = slice(g2_ * 64, (g2_ + 1) * 64)
                            P.op('act', lambda e, g2_=g2_, hp_=hp_: e.copy(out=kT[hp_, q8, g2_, 0:512], in_=U[hp_, 0:512]),
                                 r=bk(pb) + [('kTz',)], w=[('qk', id(dstT), q8, 0, g2_)])
                    else:
                        P.op('act', lambda e: e.copy(out=dstT[:, q8, 0:512], in_=U[:, 0:512]),
                             r=bk(pb), w=[('qk', id(dstT), q8, 0)])
                    P.op('act', lambda e: e.copy(out=qraw[:], in_=U[:, 512:NT]), r=bk(pb + 1, pb + 2), w=[('qraw',)])
                    for hh in range(2):
                        P.op('pe', lambda e, hh=hh: e.matmul(bank(6 + hh), lhsT=RF, rhs=qraw[:, hh * 512:(hh + 1) * 512],
                                                            start=True, stop=True),
                             r=[('qraw',), ('cst',)], w=bk(6 + hh))
                    P.op('dve', lambda e: e.tensor_tensor(out=t1[:], in0=PS[:, 6 * 512: 6 * 512 + 1024], in1=rope[:, 1, :], op=ALU.mult),
                         r=bk(6, 7) + [('rope',)], w=[('t1',)])
                    P.op('dve', lambda e: e.tensor_tensor(out=qraw[:], in0=qraw[:], in1=rope[:, 0, :], op=ALU.mult),
                         r=[('qraw',), ('rope',)], w=[('qraw',)])
                    if dstT is kT:
                        for g2_ in range(2):
                            hp_ = slice(g2_ * 64, (g2_ + 1) * 64)
                            P.op('dve', lambda e, g2_=g2_, hp_=hp_: e.tensor_tensor(out=kT[hp_, q8, g2_, 512:NT], in0=qraw[hp_, :], in1=t1[hp_, :], op=ALU.add),
                                 r=[('qraw',), ('t1',), ('kTz',)], w=[('qk', id(dstT), q8, 1, g2_)])
                    else:
                        P.op('dve', lambda e: e.tensor_tensor(out=dstT[:, q8, 512:NT], in0=qraw[:], in1=t1[:], op=ALU.add),
                             r=[('qraw',), ('t1',)], w=[('qk', id(dstT), q8, 1)])

                view_i = [0]
                for c in range(2):
                    i = next_slot()
                    view_i[0] = i
                    view = ring[i][:, 0:4096].rearrange("p (k n) -> p k n", k=8)
                    for g2 in range(2):
                        for r4 in range(4):
                            col = ((2 * c + g2) * 4 + r4) * 64
                            P.dma('pool', ring_g[i], view[:, :, r4 * 128 + g2 * 64: r4 * 128 + g2 * 64 + 64],
                                  at_w_qkv[j][:, col:col + 64].rearrange("(k p) n -> p k n", p=128), w=[('ring', i)])
                    for r4 in range(4):
                        proj_chunk(view, r4 * 128, qT, c * 4 + r4, (r4 % 2) * 3)
                i, view = load_w('pool', at_w_qkv[j][:, 1024:1536], 8, 512, 'kv')
                view_i[0] = i
                for c in range(2):
                    proj_chunk(view, c * 128, kT, c, c * 3)
                for c in range(2):
                    for b in range(4):
                        P.op('pe', lambda e, c=c, b=b: e.matmul(PS[:, c * 512 + b * 128: c * 512 + b * 128 + 128],
                                                                lhsT=ckst[:, b, c * 128:(c + 1) * 128], rhs=IDF, start=True, stop=True),
                             r=[('ckst',), ('cst',)], w=bk(c))
                    for g2_ in range(2):
                        hp_ = slice(g2_ * 64, (g2_ + 1) * 64)
                        P.op('act', lambda e, c=c, g2_=g2_, hp_=hp_: e.copy(out=kT[hp_, c, g2_, NT:NT + 512], in_=bank(c)[hp_, :]),
                             r=bk(c) + [('kTz',)], w=[('qk', id(kT), c, 2, g2_)])
                for tt in range(12):
                    kinds = (0, 1) if tt < 4 else (1,)
                    for kind in kinds:
                        pb = 2 + kind
                        for kc in range(8):
                            P.op('pe', lambda e, kc=kc, kind=kind, pb=pb, tt=tt: e.matmul(
                                PS[:, pb * 512: pb * 512 + 256], lhsT=hT[:, kc, tt * 128:(tt + 1) * 128],
                                rhs=view[:, kc, kind * 256:(kind + 1) * 256], start=(kc == 0), stop=(kc == 7)),
                                r=[('ring', i)] + kk('hT', kc, tt // 4), w=bk(pb))
                        if tt < 4:
                            P.op('act', lambda e, kind=kind, pb=pb, tt=tt: e.copy(out=kvst[:, kind, tt, :], in_=PS[:, pb * 512: pb * 512 + 256]),
                                 r=bk(pb), w=[('kvst', kind, tt)])
                            dst = ck_out if kind == 0 else cv_out
                            P.dma('sp', g_kv, dst[tt // 2, j, (tt % 2) * 128:(tt % 2) * 128 + 128, :], kvst[:, kind, tt, :],
                                  r=[('kvst', kind, tt)])
                        if kind == 1:
                            src = PS[:, pb * 512: pb * 512 + 256].rearrange("p (g d) -> p g d", g=4)
                            P.op('act', lambda e, tt=tt, src=src: e.copy(out=Vd[:, tt, :, 0:64], in_=src), r=bk(pb), w=[('Vd', tt)])
                            P.op('dve', lambda e, tt=tt, src=src: e.tensor_copy(out=Vd[:, tt, :, 64:128], in_=src), r=bk(pb), w=[('Vd', tt)])
                P.barrier()
            with contextlib.ExitStack() as stc:
                oT = sb("oT", (128, 8, NT), BF16, stack=stc)
                PT = [sb("PT%d" % i, (128, 512), BF16, stack=stc) for i in range(4)]
                rden = sb("rden", (128, 512), stack=stc)
                pti = [0]
                sbi = [0]
                obi = [0]
                MPREV = cst_b[:, 3, :]
                MNEXT = cst_b[:, 2, :]
                jobs = []
                for sq_ in range(2):
                    for qb in range(2):
                        t0 = sq_ * 2
                        jobs.append((t0 + qb, [(t0, t0 * 128, None), (t0 + 1, (t0 + 1) * 128, None)]))
                for qb in range(8):
                    kbs = []
                    if qb > 0:
                        kbs.append((4 + qb - 1, (4 + qb - 1) * 128, MPREV))
                    kbs.append((4 + qb, (4 + qb) * 128, None))
                    if qb < 7:
                        kbs.append((4 + qb + 1, (4 + qb + 1) * 128, MNEXT))
                    for b in range(4):
                        kbs.append((12 + b, NT + b * 128, None))
                    jobs.append((4 + qb, kbs))
                steps = []
                for (qt, kbs) in jobs:
                    for g in range(4):
                        ob = 3 + obi[0] % 2
                        db = 5 + obi[0] % 2
                        obi[0] += 1
                        for ki, (vb, kcol, msk) in enumerate(kbs):
                            steps.append(dict(qt=qt, g=g, ob=ob, db=db, ki=ki, nk=len(kbs), vb=vb, kcol=kcol, msk=msk))
                LA = 2

                def emit_scores(st, idx):
                    g, qc0 = st['g'], st['qt'] * 128
                    c, g2 = g // 2, g % 2
                    sbk = idx % 3
                    pt = PT[idx % 4]
                    ptk = ('PT', idx % 4)
                    kcol, msk = st['kcol'], st['msk']
                    P.op('pe', lambda e: e.matmul(
                        bank(sbk), lhsT=kT[:, c, g2, kcol:kcol + 128], rhs=qT[:, c * 4:(c + 1) * 4, qc0:qc0 + 128],
                        start=True, stop=True), r=[], w=bk(sbk))
                    P.op('act', lambda e: e.activation(out=pt[:], in_=bank(sbk), func=AF.Exp, scale=0.125),
                         r=bk(sbk), w=[ptk])
                    if msk is not None:
                        P.op('dve', lambda e: e.tensor_tensor(
                            out=pt[:].rearrange("p (r q) -> p r q", r=4), in0=pt[:].rearrange("p (r q) -> p r q", r=4),
                            in1=msk.unsqueeze(1).to_broadcast([128, 4, 128]), op=ALU.mult), r=[ptk, ('cstb',)], w=[ptk])

                def emit_pv(st, idx):
                    g, qc0, ob, db, ki, nk, vb = st['g'], st['qt'] * 128, st['ob'], st['db'], st['ki'], st['nk'], st['vb']
                    pt = PT[idx % 4]
                    ptk = ('PT', idx % 4)
                    P.op('pe', lambda e: e.matmul(bank(ob), lhsT=Vd[:, vb, g, :], rhs=pt[:], start=(ki == 0), stop=(ki == nk - 1)),
                         r=[ptk, ('Vd', vb)], w=bk(ob))
                    P.op('pe', lambda e: e.matmul(bank(db), lhsT=ONESB, rhs=pt[:], start=(ki == 0), stop=(ki == nk - 1)),
                         r=[ptk, ('cstb',)], w=bk(db))
                    if ki != nk - 1:
                        return
                    P.op('dve', lambda e: e.tensor_tensor(
                        out=rden[:].rearrange("p (r q) -> p r q", r=4), in0=bank(db).rearrange("p (r q) -> p r q", r=4),
                        in1=esink[:, 4 * g:4 * g + 4].unsqueeze(2).to_broadcast([128, 4, 128]), op=ALU.add),
                        r=bk(db) + [('esink',)], w=[('rden',)])
                    P.op('act', lambda e: e.activation(out=rden[:], in_=rden[:], func=AF.Ln), r=[('rden',)], w=[('rden',)])
                    P.op('act', lambda e: e.activation(out=rden[:], in_=rden[:], func=AF.Exp, scale=-1.0), r=[('rden',)], w=[('rden',)])
                    for hf in range(2):
                        hq = slice(hf * 64, (hf + 1) * 64)
                        o4 = bank(ob).rearrange("p (a b q) -> p a b q", a=2, b=2)
                        r4 = rden[:].rearrange("p (a b q) -> p a b q", a=2, b=2)
                        P.op('dve', lambda e, hq=hq, hf=hf, o4=o4, r4=r4: e.tensor_tensor(
                            out=oT[hq, 2 * g:2 * g + 2, qc0:qc0 + 128], in0=o4[hq, :, hf, :], in1=r4[hq, :, hf, :], op=ALU.mult),
                            r=bk(ob) + [('rden',)], w=kk('srcT', (2 * g, 2 * g + 1), st['qt'] // 4))

                for idx in range(len(steps) + LA):
                    if idx < len(steps):
                        emit_scores(steps[idx], idx)
                    if idx - LA >= 0:
                        emit_pv(steps[idx - LA], idx - LA)
                P.barrier()
                out_proj(oT, at_w_o[j], 2)

    def deltanet(L, j2):
        with contextlib.ExitStack() as std0:
            ogT = sb("ogT", (128, 8, NT), BF16, stack=std0)
            deltanet_inner(L, j2, ogT)
            out_proj(ogT, dn_w_out[j2], 2)

    def deltanet_inner(L, j2, ogT):
        with contextlib.ExitStack() as std:
            hT = sb("hT", (128, 8, NT), BF16, stack=std)
            G = {n: sb("g_" + n, (128, 2, 12, 8), stack=std) for n in ("g", "b", "negb", "gc", "eg", "negeg", "egl", "bd")}
            dtb = sb("dtb", (128, 16), stack=std)
            alog = sb("alog", (128, 16), stack=std)
            ong = sb("ong", (128, 128), stack=std)
            cwd = sb("cwd", (128, 3, 24), stack=std)
            yq = sb("yq", (128, NT), stack=std)
            yq2 = sb("yq2", (128, NT), stack=std)
            yqs = [yq, yq2]
            qT = sb("dqT", (128, NT), BF16, stack=std)
            kT = sb("dkT", (128, NT), BF16, stack=std)
            k_tm = sb("k_tm", (128, 12, 128), BF16, stack=std)
            v_tm = sb("v_tm", (128, 12, 128), BF16, stack=std)
            sz = sb("sz", (128, 12, 128), BF16, stack=std)
            vT = sz[:].rearrange("p t d -> p (t d)")
            o_acc = sb("o_acc", (128, 12, 128), stack=std)
            abt = o_acc[:, 0:3, :].rearrange("p a d -> p (a d)").rearrange("p (t c) -> p t c", t=12)
            TT = sb("TT", (128, 2, 12, 128), BF16, stack=std)
            QKd = sb("QKd", (128, 2, 12, 128), BF16, stack=std)
            Drhs = [sb("Drhs%d" % i, (128, 2, 128), stack=std) for i in range(2)]
            DecT = [sb("DecT%d" % i, (128, 2, 128), BF16, stack=std) for i in range(2)]
            ZTs = [sb("ZT%d" % i, (128, 2, 128), BF16, stack=std) for i in range(2)]
            Zts = [sb("Zt%d" % i, (128, 2, 128), BF16, stack=std) for i in range(2)]
            ZdTs = [sb("ZdT%d" % i, (128, 2, 128), BF16, stack=std) for i in range(2)]
            Zds = [sb("Zd%d" % i, (128, 2, 128), BF16, stack=std) for i in range(2)]
            Ybs = [[sb("Yb%d_%d" % (i, k_), (128, 2, 128), BF16, stack=std) for k_ in range(2)] for i in range(2)]
            YTbs = [[sb("YTb%d_%d" % (i, k_), (128, 2, 128), BF16, stack=std) for k_ in range(2)] for i in range(2)]
            PTbs = [[sb("PTb%d_%d" % (i, k_), (128, 2, 128), BF16, stack=std) for k_ in range(2)] for i in range(2)]
            S32 = sb("S32", (128, 6, 128), stack=std)
            S32s = sb("S32s", (128, 6, 128), stack=std)
            Sbf = sb("Sbf", (128, 6, 128), BF16, stack=std)
            rr = sb("rr", (128, 6, 128), BF16, stack=std)
            vn = sb("vn", (128, 6, 128), BF16, stack=std)
            vn2 = sb("vn2", (128, 6, 128), BF16, stack=std)
            ssq = sb("ssq", (128, 12), stack=std)
            rst = sb("rst", (128, 12), stack=std)

            P.dma('sp', g_dnc, dtb[:], dn_dt_bias[j2:j2 + 1, :].partition_broadcast(128), w=[('dtb',)])
            P.dma('sp', g_dnc, alog[:], dn_a_log[j2:j2 + 1, :].partition_broadcast(128), w=[('alog',)])
            P.dma('sp', g_dnc, ong[:], dn_onorm_g[j2:j2 + 1, :].partition_broadcast(128), w=[('ong',)])
            for k3 in range(3):
                P.dma('sp', g_dnc, cwd[:, k3, :], dn_conv_w[j2][k3].rearrange("(c p) -> p c", p=128), w=[('cwd',)], slow=True)
            pre_norm(hT, 0, 1)
            hkeys = kk('hT', range(8), range(3))

            i, view = load_w('pool', dn_w_in[j2][:, 4096:4128], 8, 32, 'ab')
            for tt in range(12):
                for kc in range(8):
                    P.op('pe', lambda e, tt=tt, kc=kc: e.matmul(PS[:, 7 * 512 + tt * 32: 7 * 512 + tt * 32 + 32],
                                                                lhsT=hT[:, kc, tt * 128:(tt + 1) * 128], rhs=view[:, kc, :],
                                                                start=(kc == 0), stop=(kc == 7)),
                         r=[('ring', i)] + kk('hT', kc, tt // 4), w=bk(7))
            P.op('act', lambda e: e.copy(out=abt, in_=PS[:, 7 * 512: 7 * 512 + 384].rearrange("p (t c) -> p t c", t=12)),
                 r=bk(7), w=[('abt',)])

            def perm(t):
                return t[:].rearrange("p d t h -> p t d h")

            def bc16(t):
                return t[:].rearrange("p (d h) -> p d h", d=2).unsqueeze(1).to_broadcast([128, 12, 2, 8])
            P.op('act', lambda e: e.activation(out=alog[:], in_=alog[:], func=AF.Exp), r=[('alog',)], w=[('alog',)])
            P.op('dve', lambda e: e.tensor_scalar(out=alog[:], in0=alog[:], scalar1=-1.0, scalar2=None, op0=ALU.mult),
                 r=[('alog',)], w=[('alog',)])
            a4 = abt[:, :, 0:16].rearrange("p t (d h) -> p t d h", d=2)
            b4 = abt[:, :, 16:32].rearrange("p t (d h) -> p t d h", d=2)
            P.op('dve', lambda e: e.tensor_tensor(out=perm(G["g"]), in0=a4, in1=bc16(dtb), op=ALU.add),
                 r=[('abt',), ('dtb',)], w=[('G', 'g')])
            P.op('act', lambda e: e.activation(out=G["g"][:], in_=G["g"][:], func=AF.Exp), r=[('G', 'g')], w=[('G', 'g')])
            P.op('act', lambda e: e.activation(out=G["g"][:], in_=G["g"][:], func=AF.Ln, bias=1.0, scale=1.0), r=[('G', 'g')], w=[('G', 'g')])
            P.op('dve', lambda e: e.tensor_tensor(out=perm(G["g"]), in0=perm(G["g"]), in1=bc16(alog), op=ALU.mult),
                 r=[('G', 'g'), ('alog',)], w=[('G', 'g')])
            P.op('act', lambda e: e.activation(out=perm(G["b"]), in_=b4, func=AF.Sigmoid), r=[('abt',)], w=[('G', 'b')])
            P.op('dve', lambda e: e.tensor_scalar(out=G["negb"][:], in0=G["b"][:], scalar1=-1.0, scalar2=None, op0=ALU.mult),
                 r=[('G', 'b')], w=[('G', 'negb')])
            gflat = G["g"][:].rearrange("p d t h -> p (d t h)")
            P.op('pe', lambda e: e.matmul(PS[:, 7 * 512: 7 * 512 + 96], lhsT=cst_f[:, 2, :], rhs=gflat[:, 0:96], start=True, stop=True),
                 r=[('G', 'g'), ('cst',)], w=bk(7))
            P.op('pe', lambda e: e.matmul(PS[:, 7 * 512 + 96: 7 * 512 + 192], lhsT=cst_f[:, 3, :], rhs=gflat[:, 96:192], start=True, stop=True),
                 r=[('G', 'g'), ('cst',)], w=bk(7))
            P.op('pe', lambda e: e.matmul(PS[:, 7 * 512 + 192: 7 * 512 + 384], lhsT=cst_f[:, 1, :], rhs=gflat, start=True, stop=True),
                 r=[('G', 'g'), ('cst',)], w=bk(7))

            def fl(n):
                return G[n][:].rearrange("p d t h -> p (d t h)")
            P.op('act', lambda e: e.copy(out=fl("gc"), in_=PS[:, 7 * 512: 7 * 512 + 192]), r=bk(7), w=[('G', 'gc')])
            P.op('act', lambda e: e.activation(out=fl("eg"), in_=PS[:, 7 * 512: 7 * 512 + 192], func=AF.Exp), r=bk(7), w=[('G', 'eg')])
            P.op('act', lambda e: e.activation(out=fl("egl"), in_=PS[:, 7 * 512 + 192: 7 * 512 + 384], func=AF.Exp), r=bk(7), w=[('G', 'egl')])
            P.op('dve', lambda e: e.tensor_scalar(out=fl("negeg"), in0=fl("eg"), scalar1=-1.0, scalar2=None, op0=ALU.mult),
                 r=[('G', 'eg')], w=[('G', 'negeg')])
            P.op('dve', lambda e: e.tensor_tensor(out=fl("bd"), in0=PS[:, 7 * 512 + 192: 7 * 512 + 384], in1=fl("gc"), op=ALU.subtract),
                 r=bk(7) + [('G', 'gc')], w=[('G', 'bd')])
            P.op('act', lambda e: e.activation(out=fl("bd"), in_=fl("bd"), func=AF.Exp), r=[('G', 'bd')], w=[('G', 'bd')])
            P.op('dve', lambda e: e.tensor_tensor(out=fl("bd"), in0=fl("bd"), in1=fl("b"), op=ALU.mult),
                 r=[('G', 'bd'), ('G', 'b')], w=[('G', 'bd')])
            GK = [('G', n) for n in G]
            P.barrier()

            chains = []
            for d_ in range(2):
                for sidx, tiles in enumerate(([0, 1], [2, 3], list(range(4, 12)))):
                    chains.append((d_, sidx, tiles if d_ == 0 else tiles[::-1]))

            def head_gen(h):
                i = next_slot()
                view = ring[i][:, 0:4096].rearrange("p (k x n) -> p k x n", k=8, x=4)
                for X in range(4):
                    P.dma('pool', ring_g[i], view[:, :, X, :],
                          dn_w_in[j2][:, X * 1024 + h * 128: X * 1024 + (h + 1) * 128].rearrange("(k p) n -> p k n", p=128),
                          w=[('ring', i)])
                def projMM(X):
                    pb = (X % 2) * 3
                    for tbi, (t0, t1_, grp) in enumerate(TBS):
                        for kc in range(8):
                            P.op('pe', lambda e, X=X, kc=kc, tbi=tbi, t0=t0, t1_=t1_, pb=pb: e.matmul(
                                bank(pb + tbi), lhsT=view[:, kc, X, :], rhs=hT[:, kc, t0:t1_], start=(kc == 0), stop=(kc == 7)),
                                r=[('ring', i)] + kk('hT', kc, tbi), w=bk(pb + tbi))

                def projPost(X):
                    yb = yqs[X % 2]
                    ky = ('yq', X % 2)
                    pb = (X % 2) * 3
                    U = PS[:, pb * 512: pb * 512 + NT]
                    cf = X * 8 + h
                    conv_taps(yb, U, cwd[:, 0, cf:cf + 1], cwd[:, 1, cf:cf + 1], cwd[:, 2, cf:cf + 1], None,
                              [ky], bk(pb, pb + 1, pb + 2) + [('cwd',)])
                    yield
                    P.op('act', lambda e: e.activation(out=yb[:], in_=yb[:], func=AF.Silu), r=[ky], w=[ky])
                    yield
                    if X < 2:
                        dst = qT if X == 0 else kT
                        for tbi, (t0, t1_, grp) in enumerate(TBS):
                            sumsq_rstd(lambda dc, t0=t0, t1_=t1_: yb[:, t0:t1_], lambda dc: [ky], 1, 1.0, tbi, 6)
                            P.op('dve', lambda e, dst=dst, t0=t0, t1_=t1_, X=X: e.scalar_tensor_tensor(
                                out=dst[:, t0:t1_], in0=yb[:, t0:t1_], scalar=(128.0 ** -0.5 if X == 0 else 1.0), in1=rstd[:],
                                op0=ALU.mult, op1=ALU.mult), r=[ky, ('rstd',)], w=[('dq', X)])
                            yield
                    else:
                        P.op('act', lambda e: e.copy(out=vT, in_=yb[:]), r=[ky], w=[('sz',)])
                        yield

                def run_gens(gens):
                    gens = list(gens)
                    while gens:
                        for g_ in list(gens):
                            try:
                                next(g_)
                            except StopIteration:
                                gens.remove(g_)
                def run_gens_iter(gens):
                    gens = list(gens)
                    while gens:
                        for g_ in list(gens):
                            try:
                                next(g_)
                            except StopIteration:
                                gens.remove(g_)
                        yield 'A'
                projMM(0)
                yield 'A'
                projMM(1)
                yield 'A'
                yield from run_gens_iter([projPost(0), projPost(1)])
                yield 'A_pre2'
                projMM(2)
                yield 'A'
                yield from run_gens_iter([projPost(2)])
                for (src, dstm, X) in ((kT, k_tm, 1), (vT, v_tm, 2)):
                    for b4_ in range(3):
                        pb = 6 + (b4_ % 2)
                        for t4 in range(4):
                            tt = b4_ * 4 + t4
                            P.op('pe', lambda e, src=src, tt=tt, t4=t4, pb=pb: e.matmul(
                                PS[:, pb * 512 + t4 * 128: pb * 512 + t4 * 128 + 128], lhsT=src[:, tt * 128:(tt + 1) * 128], rhs=IDB,
                                start=True, stop=True), r=[(('dq', X) if X == 1 else ('sz',)), ('cstb',)], w=bk(pb))
                        P.op('act', lambda e, dstm=dstm, b4_=b4_, pb=pb: e.copy(
                            out=dstm[:, b4_ * 4:(b4_ + 1) * 4, :].rearrange("p t d -> p (t d)"), in_=bank(pb)),
                            r=bk(pb), w=[('tm', X)])
                        yield 'A'
                for b4_ in range(3):
                    pb = 6 + (b4_ % 2)
                    for t4 in range(4):
                        tt = b4_ * 4 + t4
                        for kc in range(8):
                            P.op('pe', lambda e, tt=tt, t4=t4, kc=kc, pb=pb: e.matmul(
                                PS[:, pb * 512 + t4 * 128: pb * 512 + t4 * 128 + 128], lhsT=hT[:, kc, tt * 128:(tt + 1) * 128],
                                rhs=view[:, kc, 3, :], start=(kc == 0), stop=(kc == 7)),
                                r=[('ring', i)] + kk('hT', kc, tt // 4), w=bk(pb))
                    P.op('act', lambda e, b4_=b4_, pb=pb: e.activation(
                        out=sz[:, b4_ * 4:(b4_ + 1) * 4, :].rearrange("p t d -> p (t d)"), in_=bank(pb), func=AF.Silu),
                        r=bk(pb), w=[('sz',)])
                    yield 'A'
                yield 'A_end'

                NB = 2

                def phaseC(d_, c0, bs):
                    MSK = cst_f[:, 4 + d_, :]
                    NBX = cst_f[:, 6 + d_, :]
                    LU = cst_f[:, 2 + d_, :]
                    SMT = cst_b[:, 8 + d_, :]
                    cs = [c0 + t_ for t_ in range(NB)]
                    B0 = bs * 3
                    W_ = NB * 128
                    ROLE = {0: (0, 0), 1: (1, 0), 2: (2, 0), 3: (1, 256)}

                    def bnk(i):
                        b_, o_ = ROLE[i]
                        return PS[:, (B0 + b_) * 512 + o_: (B0 + b_) * 512 + o_ + W_]

                    def bq(i, t_):
                        b_, o_ = ROLE[i]
                        return PS[:, (B0 + b_) * 512 + o_ + t_ * 128: (B0 + b_) * 512 + o_ + t_ * 128 + 128]

                    def bkr(i):
                        return bk(B0 + ROLE[i][0])

                    def b3(i):
                        return bnk(i).rearrange("p (t d) -> p t d", t=NB)

                    def fl3(t):
                        return t[:].rearrange("p t d -> p (t d)")

                    def K(n, *x):
                        return (n, bs) + x
                    Dr, De, ZT_, Zt_, ZdT_, Zd_ = Drhs[bs], DecT[bs], ZTs[bs], Zts[bs], ZdTs[bs], Zds[bs]
                    Yb_, YTb_, PTb_ = Ybs[bs], YTbs[bs], PTbs[bs]
                    BDb = cst_b[:, 11, :].unsqueeze(1).to_broadcast([128, NB, 128])
                    IDb4 = IDB.unsqueeze(1).to_broadcast([128, NB, 128])
                    for t_, cch in enumerate(cs):
                        P.op('dve', lambda e, t_=t_, cch=cch: e.scalar_tensor_tensor(
                            out=Dr[:, t_, :], in0=MSK, scalar=G["g"][:, d_, cch, h:h + 1], in1=NBX, op0=ALU.mult, op1=ALU.add),
                            r=[('cst',), ('G', 'g')], w=[K('Drhs')])
                    yield
                    for t_, cch in enumerate(cs):
                        cs_ = slice(cch * 128, (cch + 1) * 128)
                        P.op('pe', lambda e, t_=t_: e.matmul(bq(0, t_), lhsT=Dr[:, t_, :], rhs=LU, start=True, stop=True),
                             r=[K('Drhs'), ('cst',)], w=bkr(0))
                        P.op('pe', lambda e, t_=t_, cs_=cs_: e.matmul(bq(1, t_), lhsT=kT[:, cs_], rhs=kT[:, cs_], start=True, stop=True),
                             r=[('dq', 1)], w=bkr(1))
                        P.op('pe', lambda e, t_=t_, cs_=cs_: e.matmul(bq(2, t_), lhsT=kT[:, cs_], rhs=qT[:, cs_], start=True, stop=True),
                             r=[('dq', 1), ('dq', 0)], w=bkr(2))
                    yield
                    P.op('act', lambda e: e.activation(out=fl3(De), in_=bnk(0), func=AF.Exp), r=bkr(0), w=[K('DecT')])
                    yield
                    for t_, cch in enumerate(cs):
                        P.op('dve', lambda e, t_=t_, cch=cch: e.scalar_tensor_tensor(
                            out=ZT_[:, t_, :], in0=bq(1, t_), scalar=G["negb"][:, d_, cch, h:h + 1],
                            in1=De[:, t_, :], op0=ALU.mult, op1=ALU.mult), r=bkr(1) + [K('DecT'), ('G', 'negb')], w=[K('ZT')])
                    P.op('dve', lambda e: e.tensor_tensor(out=ZT_[:], in0=ZT_[:], in1=SMT.unsqueeze(1).to_broadcast([128, NB, 128]), op=ALU.mult),
                         r=[K('ZT'), ('cstb',)], w=[K('ZT')])
                    P.op('dve', lambda e: e.tensor_tensor(out=QKd[:, d_, c0:c0 + NB, :], in0=b3(2), in1=De[:], op=ALU.mult),
                         r=bkr(2) + [K('DecT')], w=[('QKd', d_, c0)])
                    yield
                    for t_ in range(NB):
                        P.op('pe', lambda e, t_=t_: e.matmul(bq(3, t_), lhsT=ZT_[:, t_, :], rhs=IDB, start=True, stop=True),
                             r=[K('ZT'), ('cstb',)], w=bkr(3))
                    yield
                    P.op('act', lambda e: e.copy(out=fl3(Zt_), in_=bnk(3)), r=bkr(3), w=[K('Zt')])
                    P.op('dve', lambda e: e.tensor_tensor(out=ZdT_[:], in0=ZT_[:], in1=BDb, op=ALU.mult), r=[K('ZT'), ('cstb',)], w=[K('ZdT')])
                    P.op('dve', lambda e: e.tensor_tensor(out=PTb_[0][:], in0=ZdT_[:], in1=IDb4, op=ALU.add), r=[K('ZdT'), ('cstb',)], w=[K('PTb', 0)])
                    yield
                    P.op('dve', lambda e: e.tensor_tensor(out=Zd_[:], in0=Zt_[:], in1=BDb, op=ALU.mult), r=[K('Zt'), ('cstb',)], w=[K('Zd')])
                    P.op('dve', lambda e: e.tensor_tensor(out=Zt_[:], in0=Zt_[:], in1=Zd_[:], op=ALU.subtract), r=[K('Zt'), K('Zd')], w=[K('Zt')])
                    yield

                    def mm4(pbk, lh, rh, kl, kr):
                        for t_ in range(NB):
                            P.op('pe', lambda e, t_=t_: e.matmul(bq(pbk, t_), lhsT=lh[:, t_, :], rhs=rh[:, t_, :], start=True, stop=True),
                                 r=[kl, kr], w=bkr(pbk))

                    def ev(eng, dst, kd, pbk):
                        if eng == 'act':
                            P.op('act', lambda e: e.copy(out=fl3(dst), in_=bnk(pbk)), r=bkr(pbk), w=[kd])
                        else:
                            P.op('dve', lambda e: e.tensor_copy(out=fl3(dst), in_=bnk(pbk)), r=bkr(pbk), w=[kd])

                    def padd(dst, kd, pbk, addend, ka, op=ALU.add):
                        P.op('dve', lambda e: e.tensor_tensor(out=dst, in0=b3(pbk), in1=addend, op=op), r=bkr(pbk) + [ka], w=[kd])
                    curY, curYT, kY, kYT = Zd_, ZdT_, K('Zd'), K('ZdT')
                    for lev in range(3):
                        ny, nyt = Yb_[lev % 2], YTb_[lev % 2]
                        mm4(0, curYT, curY, kYT, kY)
                        if lev < 2:
                            mm4(1, curY, curYT, kY, kYT)
                        yield
                        ev('act', ny, K('Yb', lev % 2), 0)
                        if lev < 2:
                            ev('act', nyt, K('YTb', lev % 2), 1)
                        yield
                        pbk = 2 + (lev % 2)
                        mm4(pbk, ny, PTb_[lev % 2], K('Yb', lev % 2), K('PTb', lev % 2))
                        yield
                        padd(PTb_[(lev + 1) % 2][:], K('PTb', (lev + 1) % 2), pbk, PTb_[lev % 2][:], K('PTb', lev % 2))
                        yield
                        curY, curYT, kY, kYT = ny, nyt, K('Yb', lev % 2), K('YTb', lev % 2)
                    TdT, kTdT = PTb_[1], K('PTb', 1)
                    for t_ in range(NB):
                        P.op('pe', lambda e, t_=t_: e.matmul(bq(0, t_), lhsT=TdT[:, t_, :], rhs=IDB, start=True, stop=True),
                             r=[kTdT, ('cstb',)], w=bkr(0))
                    mm4(1, TdT, Zt_, kTdT, K('Zt'))
                    mm4(2, Zt_, TdT, K('Zt'), kTdT)
                    yield
                    padd(ZdT_[:], K('ZdT'), 0, IDb4, ('cstb',), op=ALU.subtract)
                    ev('act', Yb_[0], K('Yb', 0), 1)
                    yield
                    ev('act', YTb_[0], K('YTb', 0), 2)
                    padd(PTb_[0][:], K('PTb', 0), 2, IDb4, ('cstb',))
                    yield
                    mm4(3, YTb_[0], Yb_[0], K('YTb', 0), K('Yb', 0))
                    mm4(0, Yb_[0], YTb_[0], K('Yb', 0), K('YTb', 0))
                    yield
                    ev('act', Yb_[1], K('Yb', 1), 3)
                    ev('act', YTb_[1], K('YTb', 1), 0)
                    yield
                    mm4(1, Yb_[1], PTb_[0], K('Yb', 1), K('PTb', 0))
                    mm4(2, YTb_[1], Yb_[1], K('YTb', 1), K('Yb', 1))
                    yield
                    padd(PTb_[1][:], K('PTb', 1), 1, PTb_[0][:], K('PTb', 0))
                    ev('act', Yb_[0], K('Yb', 0), 2)
                    yield
                    mm4(3, Yb_[0], PTb_[1], K('Yb', 0), K('PTb', 1))
                    yield
                    padd(PTb_[0][:], K('PTb', 0), 3, PTb_[1][:], K('PTb', 1))
                    yield
                    mm4(0, ZdT_, PTb_[0], K('ZdT'), K('PTb', 0))
                    yield
                    padd(TT[:, d_, c0:c0 + NB, :], ('TT', d_, c0), 0, PTb_[0][:], K('PTb', 0))
                    yield

                P.op('dve', lambda e: e.memset(o_acc[:], 0.0), w=kk('o_acc', range(12)))
                for ch, (d_, sidx, order) in enumerate(chains):
                    if sidx < 2:
                        P.op('dve', lambda e, ch=ch: e.memset(S32[:, ch, :], 0.0), w=[('S32', ch)])
                        P.op('dve', lambda e, ch=ch: e.memset(Sbf[:, ch, :], 0.0), w=[('Sbf', ch)])
                    else:
                        P.dma('sp', g_S[ch], S32[:, ch, :], state_in[j2, d_, h], w=[('S32', ch)])
                        P.op('act', lambda e, ch=ch: e.copy(out=Sbf[:, ch, :], in_=S32[:, ch, :]), r=[('S32', ch)], w=[('Sbf', ch)])

                lane_steps = [[(d_ * 3 + sidx, cch) for sidx in range(3) for cch in chains[d_ * 3 + sidx][2]] for d_ in range(2)]

                def scan_gen():
                    def q_(ln, qi):
                        return PS[:, (6 + ln) * 512 + qi * 128: (6 + ln) * 512 + qi * 128 + 128]

                    def qk_(ln):
                        return [('ps', 6 + ln)]
                    for k_ in range(12):
                        act = [(ln, lane_steps[ln][k_][0], lane_steps[ln][k_][1]) for ln in range(2)]
                        need = set((ln, (cch // 2) * 2) for ln, ch, cch in act)
                        yield need
                        for ln, ch, cch in act:
                            P.op('act', lambda e, ch=ch, cch=cch, ln=ln: e.activation(out=S32s[:, ch, :], in_=S32[:, ch, :], func=AF.Identity, bias=0.0,
                                                                                     scale=G["egl"][:, ln, cch, h:h + 1]),
                                 r=[('S32', ch), ('G', 'egl')], w=[('S32s', ch)])
                        for ln, ch, cch in act:
                            cs_ = slice(cch * 128, (cch + 1) * 128)
                            P.op('pe', lambda e, ch=ch, cs_=cs_, ln=ln: e.matmul(q_(ln, 0), lhsT=kT[:, cs_], rhs=Sbf[:, ch, :], start=True, stop=True),
                                 r=[('dq', 1), ('Sbf', ch)], w=qk_(ln))
                            P.op('pe', lambda e, ch=ch, cs_=cs_, ln=ln: e.matmul(q_(ln, 2), lhsT=qT[:, cs_], rhs=Sbf[:, ch, :], start=True, stop=True),
                                 r=[('dq', 0), ('Sbf', ch)], w=qk_(ln))
                        yield need
                        for ln, ch, cch in act:
                            P.op('dve', lambda e, ch=ch, cch=cch, ln=ln: e.scalar_tensor_tensor(
                                out=rr[:, ch, :], in0=q_(ln, 0), scalar=G["negeg"][:, ln, cch, h:h + 1], in1=v_tm[:, cch, :],
                                op0=ALU.mult, op1=ALU.add), r=qk_(ln) + [('G', 'negeg'), ('tm', 2)], w=[('rr', ch)])
                        yield need
                        for ln, ch, cch in act:
                            P.op('pe', lambda e, ch=ch, cch=cch, ln=ln: e.matmul(q_(ln, 1), lhsT=TT[:, ln, cch, :], rhs=rr[:, ch, :], start=True, stop=True),
                                 r=[('TT', ln, (cch // 2) * 2), ('rr', ch)], w=qk_(ln))
                        yield need
                        for ln, ch, cch in act:
                            P.op('act', lambda e, ch=ch, cch=cch, ln=ln: e.activation(out=vn2[:, ch, :], in_=q_(ln, 1), func=AF.Identity, bias=0.0,
                                                                                     scale=G["bd"][:, ln, cch, h:h + 1]),
                                 r=qk_(ln) + [('G', 'bd')], w=[('vn2', ch)])
                        for ln, ch, cch in act:
                            P.op('act', lambda e, ch=ch, cch=cch, ln=ln: e.activation(out=vn[:, ch, :], in_=q_(ln, 1), func=AF.Identity, bias=0.0,
                                                                                     scale=G["b"][:, ln, cch, h:h + 1]),
                                 r=qk_(ln) + [('G', 'b')], w=[('vn', ch)])
                        yield need
                        for ln, ch, cch in act:
                            P.op('pe', lambda e, ch=ch, cch=cch, ln=ln: e.matmul(q_(ln, 0), lhsT=k_tm[:, cch, :], rhs=vn2[:, ch, :], start=True, stop=True),
                                 r=[('tm', 1), ('vn2', ch)], w=qk_(ln))
                        for ln, ch, cch in act:
                            P.op('pe', lambda e, ch=ch, cch=cch, ln=ln: e.matmul(q_(ln, 3), lhsT=QKd[:, ln, cch, :], rhs=vn[:, ch, :], start=True, stop=True),
                                 r=[('QKd', ln, (cch // 2) * 2), ('vn', ch)], w=qk_(ln))
                        yield need
                        for ln, ch, cch in act:
                            P.op('dve', lambda e, ch=ch, ln=ln: e.tensor_tensor(out=Sbf[:, ch, :], in0=q_(ln, 0), in1=S32s[:, ch, :], op=ALU.add),
                                 r=qk_(ln) + [('S32s', ch)], w=[('Sbf', ch)])
                        yield need
                        for ln, ch, cch in act:
                            P.op('dve', lambda e, ch=ch, ln=ln: e.tensor_tensor(out=S32[:, ch, :], in0=q_(ln, 0), in1=S32s[:, ch, :], op=ALU.add),
                                 r=qk_(ln) + [('S32s', ch)], w=[('S32', ch)])
                            P.op('dve', lambda e, ch=ch, cch=cch, ln=ln: e.scalar_tensor_tensor(
                                out=o_acc[:, cch, :], in0=q_(ln, 2), scalar=G["eg"][:, ln, cch, h:h + 1], in1=o_acc[:, cch, :],
                                op0=ALU.mult, op1=ALU.add), r=qk_(ln) + [('o_acc', cch), ('G', 'eg')], w=[('o_acc', cch)])
                            P.op('dve', lambda e, ch=ch, cch=cch, ln=ln: e.tensor_tensor(out=o_acc[:, cch, :], in0=q_(ln, 3), in1=o_acc[:, cch, :], op=ALU.add),
                                 r=qk_(ln) + [('o_acc', cch)], w=[('o_acc', cch)])

                pending = [(0, 0), (1, 0), (0, 2), (1, 2), (0, 4), (1, 10), (0, 6), (1, 8), (0, 8), (1, 6), (0, 10), (1, 4)]
                active = [None, None]
                done = set()
                scan = scan_gen()
                scan_need = next(scan)
                while pending or any(g_ is not None for g_ in active) or scan is not None:
                    for bs in range(2):
                        if active[bs] is None and pending:
                            d_, c0 = pending.pop(0)
                            active[bs] = (phaseC(d_, c0, bs), (d_, c0))
                        if active[bs] is not None:
                            try:
                                next(active[bs][0])
                            except StopIteration:
                                done.add(active[bs][1])
                                active[bs] = None
                    if scan is not None and scan_need <= done:
                        try:
                            scan_need = next(scan)
                        except StopIteration:
                            scan = None
                for ch, (d_, sidx, order) in enumerate(chains):
                    if sidx < 2:
                        P.dma('sp', g_so, state_out[sidx, j2, d_, h], S32[:, ch, :], r=[('S32', ch)])
                yield 'C_end'

                okeys = kk('o_acc', range(12))
                yq3 = yq2[:].rearrange("p (t d) -> p t d", t=12)
                P.op('dve', lambda e: e.tensor_tensor(out=yq3, in0=o_acc[:], in1=o_acc[:], op=ALU.mult), r=okeys, w=[('yq', 1)])
                P.op('dve', lambda e: e.reduce_sum(out=ssq[:], in_=yq3, axis=AX.X), r=[('yq', 1)], w=[('ssq',)])
                yield 'E'
                P.op('act', lambda e: e.activation(out=ssq[:], in_=ssq[:], func=AF.Sqrt, bias=EPS, scale=1.0 / 128), r=[('ssq',)], w=[('ssq',)])
                yield 'E'
                P.op('dve', lambda e: e.reciprocal(out=rst[:], in_=ssq[:]), r=[('ssq',)], w=[('rst',)])
                yield 'E'
                P.op('dve', lambda e: e.tensor_tensor(out=o_acc[:], in0=o_acc[:], in1=rst[:].unsqueeze(2).to_broadcast([128, 12, 128]), op=ALU.mult),
                     r=okeys + [('rst',)], w=okeys)
                yield 'E'
                P.op('dve', lambda e: e.tensor_tensor(out=o_acc[:], in0=o_acc[:], in1=ong[:].unsqueeze(1).to_broadcast([128, 12, 128]), op=ALU.mult),
                     r=okeys + [('ong',)], w=okeys)
                yield 'E'
                P.op('dve', lambda e: e.tensor_tensor(out=v_tm[:], in0=o_acc[:], in1=sz[:], op=ALU.mult), r=okeys + [('sz',)], w=[('tm', 2)])
                yield 'E'
                for b4_ in range(3):
                    pb = 6 + (b4_ % 2)
                    for t4 in range(4):
                        tt = b4_ * 4 + t4
                        P.op('pe', lambda e, tt=tt, t4=t4, pb=pb: e.matmul(
                            PS[:, pb * 512 + t4 * 128: pb * 512 + t4 * 128 + 128], lhsT=v_tm[:, tt, :], rhs=IDB, start=True, stop=True),
                            r=[('tm', 2), ('cstb',)], w=bk(pb))
                    P.op('act', lambda e, b4_=b4_, pb=pb: e.copy(out=ogT[:, h, b4_ * 512:(b4_ + 1) * 512], in_=bank(pb)),
                         r=bk(pb), w=[('srcT', h, b4_)])
                    yield 'E'
            gens_h = [head_gen(h_) for h_ in range(8)]
            state_h = [None] * 8

            def step_h(h_):
                try:
                    state_h[h_] = next(gens_h[h_])
                except StopIteration:
                    state_h[h_] = 'done'
            while state_h[0] != 'A_end':
                step_h(0)
            for h_ in range(8):
                while state_h[h_] != 'C_end':
                    step_h(h_)
                while state_h[h_] != 'done' or (h_ + 1 < 8 and state_h[h_ + 1] != 'A_end'):
                    if state_h[h_] != 'done':
                        step_h(h_)
                    if h_ + 1 < 8 and state_h[h_ + 1] != 'A_end':
                        if not (state_h[h_ + 1] == 'A_pre2' and state_h[h_] != 'done'):
                            step_h(h_ + 1)
            P.barrier()

    with contextlib.ExitStack() as st0:
        ada_ref[0] = sb("ada_slot0", (128, SLOT), BF16, stack=st0)
        g0 = phase0_gen(st0)
        m0 = mod_stream(0)
        alive = [g0, m0]
        while alive:
            for g_ in list(alive):
                try:
                    next(g_)
                except StopIteration:
                    alive.remove(g_)
        P.barrier()
    for L in range(n_layers):
        modulation(L)
        if L % 2 == 0 and mixers in (2, 3):
            deltanet(L, L // 2)
        if L % 2 == 1 and mixers in (1, 3):
            attention(L, L // 2)
        if dbg:
            P.dma('sp', g_out, dbg_out[2 * L], xT[:], r=kk('xT', range(8), range(3)))
        pump = mod_stream(L + 1) if L + 1 < n_layers else None
        ffn(L, pump)
        if dbg:
            P.dma('sp', g_out, dbg_out[2 * L + 1], xT[:], r=kk('xT', range(8), range(3)))

    with contextlib.ExitStack() as st0:
        ostage = [sb("ostage%d" % i, (128, D), stack=st0) for i in range(2)]
        for tt in range(12):
            s = ostage[tt % 2]
            pb = (tt % 2) * 2
            for dc in range(8):
                b = pb + dc // 4
                P.op('pe', lambda e, tt=tt, dc=dc, b=b: e.matmul(
                    PS[:, b * 512 + (dc % 4) * 128: b * 512 + (dc % 4) * 128 + 128],
                    lhsT=xT[:, dc, tt * 128:(tt + 1) * 128], rhs=IDF, start=True, stop=True),
                    r=[('xT', dc, tt // 4), ('cst',)], w=bk(b))
            if tt % 2 == 0:
                P.op('act', lambda e, s=s, pb=pb: e.copy(out=s[:], in_=PS[:, pb * 512: pb * 512 + 1024]),
                     r=bk(pb, pb + 1), w=[('ostg', tt % 2)])
            else:
                P.op('dve', lambda e, s=s, pb=pb: e.tensor_copy(out=s[:], in_=PS[:, pb * 512: pb * 512 + 1024]),
                     r=bk(pb, pb + 1), w=[('ostg', tt % 2)])
            P.dma('sp', g_os[tt % 2], y_out[tt * 128:(tt + 1) * 128, :], s[:], r=[('ostg', tt % 2)])
        P.barrier()
    for g_ in out_groups:
        if g_.count > 0:
            P.eng['sp'].wait_ge(g_.sem, g_.count)
    es.close()
    return nc, P


def make_consts():
    c = np.zeros((128, 12, 128), np.float32)
    i = np.arange(128)
    m, j = np.meshgrid(i, i, indexing="ij")
    c[:, 0, :] = np.eye(128)
    c[:, 1, :] = 1.0
    c[:, 2, :] = (m <= j)
    c[:, 3, :] = (m >= j)
    c[:, 4, :] = (m > j)
    c[:, 5, :] = (m < j)
    xf = np.zeros((128, 128), np.float32)
    xf[0, 1:] = 1.0
    xf[i[1:], i[1:]] = -1.0
    c[:, 6, :] = -BIG * xf
    xb = np.zeros((128, 128), np.float32)
    xb[127, :127] = 1.0
    xb[i[:127], i[:127]] = -1.0
    c[:, 7, :] = -BIG * xb
    c[:, 8, :] = (j > m)
    c[:, 9, :] = (j < m)
    R = np.zeros((128, 128), np.float32)
    for base in range(0, 128, 32):
        for d in range(16):
            R[base + d + 16, base + d] = -1.0
            R[base + d, base + d + 16] = 1.0
    c[:, 10, :] = R
    c[:, 11, :] = (m // 16 == j // 16)
    return c


def make_rope():
    half = 32
    inv = np.power(10000.0, -np.arange(0, half, 2, dtype=np.float32) / half).astype(np.float32)
    pos = np.arange(1024)
    row = (pos // 64).astype(np.float32)
    col = (pos % 64).astype(np.float32)
    t = np.zeros((128, 2, 1024), np.float32)
    for h in range(2):
        for d in range(64):
            p = row if d < 32 else col
            f = inv[d % 16]
            ang = (p * f).astype(np.float32)
            t[h * 64 + d, 0, :] = np.cos(ang)
            t[h * 64 + d, 1, :] = np.sin(ang)
    return t


_CACHE = {}


def kernel(x_prompt, x_sample, state_delta, cache_k, cache_v, c, c_ctx,
           w_ada, b_ada, norm_g, dn_w_in, dn_conv_w, dn_a_log, dn_dt_bias, dn_onorm_g, dn_w_out,
           at_w_qkv, at_sink, at_w_o, ffn_w_up, ffn_conv_w, ffn_conv_b, ffn_w_down, _n_layers=DEPTH, _mixers=3, _dbg=False):
    f = lambda a: np.ascontiguousarray(np.asarray(a, dtype=np.float32))
    key = (_n_layers, _mixers, _dbg)
    if key not in _CACHE:
        _CACHE[key] = build_program(_n_layers, _mixers, _dbg)
    nc, _ = _CACHE[key]
    x_prompt, x_sample = f(x_prompt), f(x_sample)
    shared = {
        "w_ada": f(w_ada), "b_ada": f(b_ada), "norm_g": f(norm_g), "dn_w_in": f(dn_w_in),
        "dn_conv_w": f(dn_conv_w), "dn_a_log": f(dn_a_log).reshape(2, 16), "dn_dt_bias": f(dn_dt_bias).reshape(2, 16),
        "dn_onorm_g": f(dn_onorm_g), "dn_w_out": f(dn_w_out), "at_w_qkv": f(at_w_qkv), "at_sink": f(at_sink),
        "at_w_o": f(at_w_o), "ffn_w_up": f(ffn_w_up), "ffn_conv_w": f(ffn_conv_w), "ffn_conv_b": f(ffn_conv_b),
        "ffn_w_down": f(ffn_w_down), "cst": make_consts(), "rope_t": make_rope(),
    }
    state_delta, cache_k, cache_v, c, c_ctx = f(state_delta), f(cache_k), f(cache_v), f(c), f(c_ctx)
    in_maps = []
    for i in range(8):
        m = dict(shared)
        m["x_in"] = np.concatenate([x_prompt[2 * i], x_prompt[2 * i + 1], x_sample[i]], axis=0)
        m["state_in"] = np.ascontiguousarray(state_delta[i])
        m["ck_in"] = np.ascontiguousarray(cache_k[i].reshape(2, 512, 256))
        m["cv_in"] = np.ascontiguousarray(cache_v[i].reshape(2, 512, 256))
        m["c_in"] = np.stack([c_ctx, c[i]], axis=0)
        in_maps.append(m)
    res = run_bass_kernel_spmd(nc, in_maps, core_ids=list(range(8)))
    R = res.results
    y_prompt = np.stack([R[i]["y_out"][s * 256:(s + 1) * 256] for i in range(8) for s in range(2)], 0)
    y_sample = np.stack([R[i]["y_out"][512:] for i in range(8)], 0)
    st = np.concatenate([R[i]["state_out"] for i in range(8)], 0)
    ck = np.concatenate([R[i]["ck_out"] for i in range(8)], 0).reshape(16, 2, 256, 4, 64)
    cv = np.concatenate([R[i]["cv_out"] for i in range(8)], 0).reshape(16, 2, 256, 4, 64)
    outs = (y_prompt.astype(np.float32), y_sample.astype(np.float32), st.astype(np.float32),
            ck.astype(np.float32), cv.astype(np.float32))
    if _dbg:
        return outs, [R[i]["dbg_out"] for i in range(8)]
    return outs
```

```python
import contextlib
import os
DN_STAGE = int(os.environ.get('DN_STAGE', '5'))
import numpy as np
import concourse.bass as bass
import concourse.mybir as mybir
from concourse.bass_utils import run_bass_kernel_spmd

F32 = mybir.dt.float32
BF16 = mybir.dt.bfloat16
ALU = mybir.AluOpType
AF = mybir.ActivationFunctionType
AX = mybir.AxisListType

D = 1024
NT = 1536
DEPTH = 4
FFN = 2816
EPS = 1e-6
BIG = 200.0
SEGS = [(0, 256), (256, 512), (512, 1536)]
TBS = [(0, 512, 0), (512, 1024, 1), (1024, 1536, 1)]
SLOT = 4096
NRING = 2
FUSE_WAITS = True


class Grp:
    def __init__(self, name, sem):
        self.name = name
        self.sem = sem
        self.count = 0


class Prog:
    def __init__(self, nc, es, needed=None):
        self.nc = nc
        self.es = es
        self.needed = needed
        self.used = set()
        self.semval = {}
        self.inc = {}
        self.pending = None
        self.nfused = 0
        self.eng = {'pe': nc.tensor, 'dve': nc.vector, 'act': nc.scalar, 'pool': nc.gpsimd, 'sp': nc.sync}
        self.sem = {k: es.enter_context(nc.semaphore('s_' + k)) for k in self.eng}
        self.cnt = {k: 0 for k in self.eng}
        self.waited = {k: {} for k in self.eng}
        self.lastw = {}
        self.readers = {}
        self.groups = []
        self.nwait = 0

    def group(self, name):
        g = Grp(name, self.es.enter_context(self.nc.semaphore('g_' + name)))
        self.groups.append(g)
        return g

    def _wait(self, eng, ev):
        if ev[0] == 'c':
            src, val = ev[1], ev[2]
            if src == eng and eng in ('pe', 'sp'):
                return
            sem, name = self.sem[src], src
        else:
            g = ev[1]
            sem, name, val = g.sem, g.name, g.count
        if self.waited[eng].get(name, 0) >= val:
            return
        self.waited[eng][name] = val
        if ev[0] == 'c':
            self.used.add((src, val))
            if self.needed is not None:
                val = self.semval[(src, val)]
        if self.pending is not None:
            self.pending.append((sem, val))
        else:
            self.eng[eng].wait_ge(sem, val)
        self.nwait += 1

    def _deps(self, eng, r, w):
        for k in r:
            ev = self.lastw.get(k)
            if ev is not None:
                self._wait(eng, ev)
            if k[0] == 'ps':
                for en2, ev2 in self.readers.get(k, {}).items():
                    if en2 != eng:
                        self._wait(eng, ev2)
        for k in w:
            ev = self.lastw.get(k)
            if ev is not None:
                self._wait(eng, ev)
            for ev in self.readers.get(k, {}).values():
                self._wait(eng, ev)

    def _record(self, ev, evname, r, w):
        for k in r:
            self.readers.setdefault(k, {})[evname] = ev
        for k in w:
            self.lastw[k] = ev
            self.readers[k] = {}

    def op(self, eng, fn, r=(), w=()):
        fuse = FUSE_WAITS and eng in ('dve', 'act')
        if fuse:
            self.pending = []
        self._deps(eng, r, w)
        pend, self.pending = self.pending, None
        if fuse and pend:
            for (sem_, val_) in pend[:-1]:
                self.eng[eng].wait_ge(sem_, val_)
        ins = fn(self.eng[eng])
        if fuse and pend:
            ins._wait_ge(pend[-1][0], pend[-1][1])
            self.nfused += 1
        self.cnt[eng] += 1
        n = self.cnt[eng]
        if self.needed is None:
            ins.then_inc(self.sem[eng], 1)
        elif (eng, n) in self.needed:
            self.inc[eng] = self.inc.get(eng, 0) + 1
            self.semval[(eng, n)] = self.inc[eng]
            ins.then_inc(self.sem[eng], 1)
        self._record(('c', eng, n), eng, r, w)

    def dma(self, q, grp, out, in_, r=(), w=(), slow=False):
        self._deps(q, r, w)
        if slow:
            ins = self.eng[q].dma_start(out=out, in_=in_, allow_slow_non_contiguous=True)
        else:
            ins = self.eng[q].dma_start(out=out, in_=in_)
        ins.then_inc(grp.sem, 16)
        grp.count += 16
        self._record(('d', grp), grp.name, r, w)

    def barrier(self):
        for e in self.eng:
            for s in ('pe', 'dve', 'act', 'pool'):
                if s != e and self.cnt[s] > 0:
                    self._wait(e, ('c', s, self.cnt[s]))
            for g in self.groups:
                if g.count > 0:
                    self._wait(e, ('d', g))
        self.lastw = {}
        self.readers = {}


def kk(name, *idx):
    out = [(name,)]
    for i in idx:
        if isinstance(i, int):
            out = [o + (i,) for o in out]
        else:
            out = [o + (j,) for o in out for j in i]
    return out


def build_program(n_layers=DEPTH, mixers=3, dbg=False):
    _, P1 = build_pass(n_layers, mixers, dbg, None)
    return build_pass(n_layers, mixers, dbg, P1.used)


def build_pass(n_layers, mixers, dbg, needed):
    nc = bass.Bass("TRN2", target_bir_lowering=False)
    es = contextlib.ExitStack()
    P = Prog(nc, es, needed)

    def din(name, shape):
        return nc.dram_tensor(name, list(shape), F32, kind="ExternalInput").ap()

    def dout(name, shape):
        return nc.dram_tensor(name, list(shape), F32, kind="ExternalOutput").ap()

    xin = din("x_in", (NT, D))
    state_in = din("state_in", (2, 2, 8, 128, 128))
    ck_in = din("ck_in", (2, 512, 256))
    cv_in = din("cv_in", (2, 512, 256))
    c_in = din("c_in", (2, D))
    w_ada = din("w_ada", (DEPTH, D, 6 * D))
    b_ada = din("b_ada", (DEPTH, 6 * D))
    norm_g = din("norm_g", (DEPTH, 4, D))
    dn_w_in = din("dn_w_in", (2, D, 4128))
    dn_conv_w = din("dn_conv_w", (2, 3, 3072))
    dn_a_log = din("dn_a_log", (2, 16))
    dn_dt_bias = din("dn_dt_bias", (2, 16))
    dn_onorm_g = din("dn_onorm_g", (2, 128))
    dn_w_out = din("dn_w_out", (2, D, D))
    at_w_qkv = din("at_w_qkv", (2, D, 1536))
    at_sink = din("at_sink", (2, 16))
    at_w_o = din("at_w_o", (2, D, D))
    ffn_w_up = din("ffn_w_up", (DEPTH, D, 2 * FFN))
    ffn_conv_w = din("ffn_conv_w", (DEPTH, 3, 2 * FFN))
    ffn_conv_b = din("ffn_conv_b", (DEPTH, 2 * FFN))
    ffn_w_down = din("ffn_w_down", (DEPTH, FFN, D))
    cst = din("cst", (128, 12, 128))
    rope_t = din("rope_t", (128, 2, 1024))

    y_out = dout("y_out", (NT, D))
    state_out = dout("state_out", (2, 2, 2, 8, 128, 128))
    ck_out = dout("ck_out", (2, 2, 256, 256))
    cv_out = dout("cv_out", (2, 2, 256, 256))
    dbg_out = dout("dbg_out", (8, 128, 8, NT)) if dbg else None

    uid = [0]

    def sb(name, shape, dt=F32, stack=None):
        uid[0] += 1
        return (stack or es).enter_context(nc.sbuf_tensor("%s_%d" % (name, uid[0]), list(shape), dt))

    xT = sb("xT", (128, 8, NT))
    cst_f = sb("cst_f", (128, 12, 128))
    cst_b = sb("cst_b", (128, 12, 128), BF16)
    ring = [sb("ring%d" % i, (128, SLOT), BF16) for i in range(NRING)]
    ring_g = [P.group("ring%d" % i) for i in range(NRING)]
    ring_i = [0]
    ada_ref = [None]
    ada_g = P.group("ada")
    scT = sb("scT", (128, 8, 2), BF16)
    modT = sb("modT", (128, 48, 2))
    mAB = sb("mAB", (128, 6, 8, 2))
    ngT = sb("ngT", (128, 4, 8))
    badaT = sb("badaT", (128, 48))
    sq = [sb("sq%d" % i, (128, 512), BF16) for i in range(2)]
    rs_t = sb("rs_t", (128, 512))
    rstd = sb("rstd", (128, 512))
    ntmp = [sb("ntmp0", (128, 512))] * 2
    cw_all = sb("cw_all", (128, 3, 44))
    cb_all = sb("cb_all", (128, 44))
    PS = es.enter_context(nc.psum_tensor("PS", [128, 4096], F32))

    g_in = P.group("in")
    g_misc = P.group("misc")
    g_out = P.group("out")
    g_poolm = P.group("poolm")
    g_stg = [P.group("stg0"), P.group("stg1")]
    g_craw = P.group("craw")
    g_bada = P.group("bada")
    g_ng = P.group("ng")
    g_cwcb = P.group("cwcb")
    g_att = P.group("att")
    g_esink = P.group("esink")
    g_dnc = P.group("dnc")
    g_S = {2: P.group("S2"), 5: P.group("S5")}
    g_so = P.group("so")
    g_kv = P.group("kv")
    g_os = [P.group("os0"), P.group("os1")]
    out_groups = [g_out, g_so, g_kv, g_os[0], g_os[1]]

    IDF = cst_f[:, 0, :]
    IDB = cst_b[:, 0, :]
    ONESB = cst_b[:, 1, :]

    def bank(b):
        return PS[:, b * 512:(b + 1) * 512]

    def bk(*bs):
        return [('ps', b) for b in bs]

    def next_slot():
        i = ring_i[0] % NRING
        ring_i[0] += 1
        return i

    P.dma('sp', g_in, cst_f[:], cst, w=[('cst',)])
    P.dma('pool', g_poolm, cst_b[:], cst, w=[('cstb',)])

    with contextlib.ExitStack() as st0:
        craw = sb("craw", (128, 2, 8), stack=st0)
        for g in range(2):
            P.dma('sp', g_craw, craw[:, g, :], c_in[g:g + 1, :].rearrange("o (kc p) -> p (o kc)", p=128),
                  w=[('craw',)], slow=True)
        P.op('act', lambda e: e.activation(out=scT[:].rearrange("p k g -> p g k"), in_=craw[:], func=AF.Silu),
             r=[('craw',)], w=[('scT',)])
        P.barrier()

    def phase0_gen(st0):
        stage = [sb("xstage%d" % i, (128, D), stack=st0) for i in range(2)]
        for tt in range(12):
            s = stage[tt % 2]
            P.dma('sp', g_stg[tt % 2], s[:], xin[tt * 128:(tt + 1) * 128, :], w=[('stg', tt % 2)])
            pb = (tt % 2) * 2
            for dc in range(8):
                b = pb + dc // 4
                P.op('pe', lambda e, s=s, dc=dc, b=b: e.matmul(
                    PS[:, b * 512 + (dc % 4) * 128: b * 512 + (dc % 4) * 128 + 128],
                    lhsT=s[:, dc * 128:(dc + 1) * 128], rhs=IDF, start=True, stop=True),
                    r=[('stg', tt % 2), ('cst',)], w=bk(b))
            eng = 'act' if tt % 2 == 0 else 'dve'
            src = PS[:, pb * 512: pb * 512 + 1024].rearrange("p (c t) -> p c t", c=8)
            if eng == 'act':
                P.op('act', lambda e, tt=tt, src=src: e.copy(out=xT[:, :, tt * 128:(tt + 1) * 128], in_=src),
                     r=bk(pb, pb + 1), w=kk('xT', range(8), tt // 4))
            else:
                P.op('dve', lambda e, tt=tt, src=src: e.tensor_copy(out=xT[:, :, tt * 128:(tt + 1) * 128], in_=src),
                     r=bk(pb, pb + 1), w=kk('xT', range(8), tt // 4))
            yield

    def load_w(eng_q, src2d, kc, ncol, key):
        i = next_slot()
        view = ring[i][:, 0:kc * ncol].rearrange("p (k n) -> p k n", k=kc)
        P.dma('pool', ring_g[i], view, src2d.rearrange("(k p) n -> p k n", p=128), w=[('ring', i)])
        return i, view

    def mod_stream(L):
        mb = 7
        for piece in range(12):
            view = ada_ref[0][:, 0:4096].rearrange("p (k n) -> p k n", k=8)
            P.dma('pool', ada_g, view, w_ada[L][:, piece * 512:(piece + 1) * 512].rearrange("(k p) n -> p k n", p=128), w=[('adaslot',)])
            i = 'ada'
            for o4 in range(4):
                oc = piece * 4 + o4
                for kc in range(8):
                    P.op('pe', lambda e, view=view, o4=o4, kc=kc, oc=oc: e.matmul(
                        PS[:, mb * 512 + 2 * oc: mb * 512 + 2 * oc + 2],
                        lhsT=view[:, kc, o4 * 128:(o4 + 1) * 128], rhs=scT[:, kc, :],
                        start=(kc == 0), stop=(kc == 7)),
                        r=[('adaslot',), ('scT',)], w=bk(mb))
            yield

    def modulation(L):
        mb = 7
        P.dma('sp', g_bada, badaT[:], b_ada[L].rearrange("(oc p) -> p oc", p=128), w=[('bada',)], slow=True)
        for v in range(4):
            P.dma('sp', g_ng, ngT[:, v, :], norm_g[L][v].rearrange("(dc p) -> p dc", p=128), w=[('ngT',)], slow=True)
        P.op('dve', lambda e: e.tensor_tensor(
            out=modT[:], in0=PS[:, mb * 512: mb * 512 + 96].rearrange("p (o g) -> p o g", g=2),
            in1=badaT[:].unsqueeze(2).to_broadcast([128, 48, 2]), op=ALU.add),
            r=bk(mb) + [('bada',)], w=[('modT',)])

        def mv(v):
            return modT[:, v * 8:(v + 1) * 8, :]

        def ng(v):
            return ngT[:, v, :].unsqueeze(2).to_broadcast([128, 8, 2])
        for (dst, vs, vg) in ((0, 1, 0), (3, 4, 2)):
            P.op('dve', lambda e, dst=dst, vs=vs: e.tensor_scalar(
                out=mAB[:, dst], in0=mv(vs), scalar1=1.0, scalar2=None, op0=ALU.add),
                r=[('modT',)], w=[('mAB', dst)])
            P.op('dve', lambda e, dst=dst, vg=vg: e.tensor_tensor(
                out=mAB[:, dst], in0=mAB[:, dst], in1=ng(vg), op=ALU.mult),
                r=[('mAB', dst), ('ngT',)], w=[('mAB', dst)])
        for (dst, vs) in ((1, 0), (4, 3)):
            P.op('dve', lambda e, dst=dst, vs=vs: e.tensor_copy(out=mAB[:, dst], in_=mv(vs)),
                 r=[('modT',)], w=[('mAB', dst)])
        for (dst, vs, vg) in ((2, 2, 1), (5, 5, 3)):
            P.op('dve', lambda e, dst=dst, vs=vs, vg=vg: e.tensor_tensor(
                out=mAB[:, dst], in0=mv(vs), in1=ng(vg), op=ALU.mult),
                r=[('modT',), ('ngT',)], w=[('mAB', dst)])

    def sumsq_rstd(src_fn, src_keys_fn, nchunk, scale, tbi, psb, ncols=512):
        for dc in range(nchunk):
            s = sq[dc % 2]
            P.op('act', lambda e, s=s, dc=dc: e.activation(out=s[:, 0:ncols], in_=src_fn(dc), func=AF.Square),
                 r=src_keys_fn(dc), w=[('sq', dc % 2)])
            P.op('pe', lambda e, s=s, dc=dc: e.matmul(
                PS[:, psb * 512: psb * 512 + ncols], lhsT=ONESB, rhs=s[:, 0:ncols],
                start=(dc == 0), stop=(dc == nchunk - 1)),
                r=[('sq', dc % 2), ('cstb',)], w=bk(psb))
        P.op('act', lambda e: e.activation(out=rs_t[:, 0:ncols], in_=PS[:, psb * 512: psb * 512 + ncols],
                                           func=AF.Ln, bias=EPS, scale=scale),
             r=bk(psb), w=[('rs_t',)])
        P.op('act', lambda e: e.activation(out=rstd[:, 0:ncols], in_=rs_t[:, 0:ncols], func=AF.Exp, scale=-0.5),
             r=[('rs_t',)], w=[('rstd',)])

    def pre_norm(hT, iA, iB):
        for tbi, (t0, t1, grp) in enumerate(TBS):
            sumsq_rstd(lambda dc: xT[:, dc, t0:t1], lambda dc: [('xT', dc, tbi)], 8, 1.0 / D, tbi, 6)
            for dc in range(8):
                t = ntmp[dc % 2]
                P.op('dve', lambda e, t=t, dc=dc: e.scalar_tensor_tensor(
                    out=t[:], in0=xT[:, dc, t0:t1], scalar=mAB[:, iA, dc, grp:grp + 1], in1=rstd[:],
                    op0=ALU.mult, op1=ALU.mult),
                    r=[('xT', dc, tbi), ('mAB', iA), ('rstd',)], w=[('ntmp', 0)])
                P.op('act', lambda e, t=t, dc=dc: e.activation(
                    out=hT[:, dc, t0:t1], in_=t[:], func=AF.Identity, bias=mAB[:, iB, dc, grp:grp + 1], scale=1.0),
                    r=[('ntmp', 0), ('mAB', iB)], w=[('hT', dc, tbi)])

    def post_norm_block(yblk, tbi, iG):
        t0, t1, grp = TBS[tbi]
        sumsq_rstd(lambda dc: yblk[:, dc, :], lambda dc: [('yblk', dc)], 8, 1.0 / D, tbi, 6)
        for dc in range(8):
            t = ntmp[dc % 2]
            P.op('dve', lambda e, t=t, dc=dc: e.scalar_tensor_tensor(
                out=t[:], in0=yblk[:, dc, :], scalar=mAB[:, iG, dc, grp:grp + 1], in1=rstd[:],
                op0=ALU.mult, op1=ALU.mult),
                r=[('yblk', dc), ('mAB', iG), ('rstd',)], w=[('ntmp', 0)])
            P.op('dve', lambda e, t=t, dc=dc: e.tensor_tensor(
                out=xT[:, dc, t0:t1], in0=xT[:, dc, t0:t1], in1=t[:], op=ALU.add),
                r=[('ntmp', 0), ('xT', dc, tbi)], w=[('xT', dc, tbi)])

    def conv_taps(dst, upsum, w0, w1, w2, bias, key_dst, key_src):
        if bias is None:
            P.op('act', lambda e: e.activation(out=dst[:], in_=upsum, func=AF.Identity, bias=0.0, scale=w1),
                 r=key_src, w=key_dst)
        else:
            P.op('act', lambda e: e.activation(out=dst[:], in_=upsum, func=AF.Identity, bias=bias, scale=w1),
                 r=key_src, w=key_dst)
        u3 = upsum[:, 0:512].rearrange("p (s t) -> p s t", s=2)
        d3 = dst[:, 0:512].rearrange("p (s t) -> p s t", s=2)
        for (wt, so, do) in ((w0, 0, 1), (w2, 1, 0)):
            P.op('dve', lambda e, wt=wt, so=so, do=do: e.scalar_tensor_tensor(
                out=d3[:, :, do:do + 255], in0=u3[:, :, so:so + 255], scalar=wt, in1=d3[:, :, do:do + 255],
                op0=ALU.mult, op1=ALU.add), r=key_src + key_dst, w=key_dst)
            P.op('dve', lambda e, wt=wt, so=so, do=do: e.scalar_tensor_tensor(
                out=dst[:, 512 + do:512 + do + 1023], in0=upsum[:, 512 + so:512 + so + 1023], scalar=wt,
                in1=dst[:, 512 + do:512 + do + 1023], op0=ALU.mult, op1=ALU.add),
                r=key_src + key_dst, w=key_dst)

    def ffn(L, pump=None):
        def pump1():
            if pump is not None:
                next(pump, None)
        with contextlib.ExitStack() as stf:
            gT = sb("gT", (128, 22, NT), BF16, stack=stf)
            if pump is not None:
                ada_ref[0] = sb("ada_slot", (128, SLOT), BF16, stack=stf)
            with contextlib.ExitStack() as stu:
                hT = sb("hT", (128, 8, NT), BF16, stack=stu)
                ya = [sb("ya%d" % i, (128, NT), stack=stu) for i in range(2)]
                yb = sb("yb", (128, NT), stack=stu)
                for k3 in range(3):
                    P.dma('sp', g_cwcb, cw_all[:, k3, :], ffn_conv_w[L][k3].rearrange("(c p) -> p c", p=128), w=[('cw',)], slow=True)
                P.dma('sp', g_cwcb, cb_all[:], ffn_conv_b[L].rearrange("(c p) -> p c", p=128), w=[('cb',)], slow=True)
                pre_norm(hT, 3, 4)
                hkeys = kk('hT', range(8), range(3))
                for pr in range(11):
                    pump1()
                    i = next_slot()
                    view = ring[i][:, 0:4096].rearrange("p (k n) -> p k n", k=8)
                    P.dma('pool', ring_g[i], view[:, :, 0:256],
                          ffn_w_up[L][:, pr * 256:(pr + 1) * 256].rearrange("(k p) n -> p k n", p=128),
                          w=[('ring', i)])
                    P.dma('pool', ring_g[i], view[:, :, 256:512],
                          ffn_w_up[L][:, FFN + pr * 256:FFN + (pr + 1) * 256].rearrange("(k p) n -> p k n", p=128),
                          w=[('ring', i)])
                    for c2 in range(2):
                        j = pr * 2 + c2
                        for half in range(2):
                            pb = half * 3
                            for tbi, (t0, t1, grp) in enumerate(TBS):
                                for kc in range(8):
                                    P.op('pe', lambda e, view=view, kc=kc, half=half, c2=c2, pb=pb, tbi=tbi, t0=t0, t1=t1: e.matmul(
                                        bank(pb + tbi), lhsT=view[:, kc, half * 256 + c2 * 128: half * 256 + c2 * 128 + 128],
                                        rhs=hT[:, kc, t0:t1], start=(kc == 0), stop=(kc == 7)),
                                        r=[('ring', i)] + kk('hT', kc, tbi), w=bk(pb + tbi))
                            f = j + 22 * half
                            dst = ya[j % 2] if half == 0 else yb
                            kd = [('ya', j % 2)] if half == 0 else [('yb',)]
                            conv_taps(dst, PS[:, pb * 512: pb * 512 + NT], cw_all[:, 0, f:f + 1], cw_all[:, 1, f:f + 1],
                                      cw_all[:, 2, f:f + 1], cb_all[:, f:f + 1], kd, bk(pb, pb + 1, pb + 2) + [('cw',), ('cb',)])
                        yaj = ya[j % 2]
                        P.op('act', lambda e, yaj=yaj: e.activation(out=yaj[:], in_=yaj[:], func=AF.Silu),
                             r=[('ya', j % 2)], w=[('ya', j % 2)])
                        P.op('dve', lambda e, yaj=yaj, j=j: e.tensor_tensor(out=gT[:, j, :], in0=yaj[:], in1=yb[:], op=ALU.mult),
                             r=[('ya', j % 2), ('yb',)], w=[('gT', j)])
                P.barrier()
            with contextlib.ExitStack() as std:
                yblk = sb("yblk", (128, 8, 512), stack=std)
                for tbi, (t0, t1, grp) in enumerate(TBS):
                    for oc2 in range(8):
                        if oc2 % 4 == 0:
                            pump1()
                        i = next_slot()
                        view = ring[i][:, 0:22 * 128].rearrange("p (k n) -> p k n", k=22)
                        P.dma('pool', ring_g[i], view,
                              ffn_w_down[L][:, oc2 * 128:(oc2 + 1) * 128].rearrange("(k p) n -> p k n", p=128),
                              w=[('ring', i)])
                        pb = oc2 % 2
                        for j in range(22):
                            P.op('pe', lambda e, view=view, j=j, pb=pb, t0=t0, t1=t1: e.matmul(
                                bank(pb), lhsT=view[:, j, :], rhs=gT[:, j, t0:t1], start=(j == 0), stop=(j == 21)),
                                r=[('ring', i), ('gT', j)], w=bk(pb))
                        P.op('act', lambda e, oc2=oc2, pb=pb: e.copy(out=yblk[:, oc2, :], in_=bank(pb)),
                             r=bk(pb), w=[('yblk', oc2)])
                    post_norm_block(yblk, tbi, 5)
                if pump is not None:
                    for _ in pump:
                        pass
                P.barrier()


    def out_proj(srcT, wsrc, iG):
        with contextlib.ExitStack() as sto:
            yblk = sb("yblk", (128, 8, 512), stack=sto)
            for tbi, (t0, t1, grp) in enumerate(TBS):
                for half in range(2):
                    i, view = load_w('pool', wsrc[:, half * 512:(half + 1) * 512], 8, 512, 'wo')
                    for o4 in range(4):
                        oc = half * 4 + o4
                        pb = oc % 2
                        for kc in range(8):
                            P.op('pe', lambda e, view=view, o4=o4, kc=kc, pb=pb: e.matmul(
                                bank(pb), lhsT=view[:, kc, o4 * 128:(o4 + 1) * 128], rhs=srcT[:, kc, t0:t1],
                                start=(kc == 0), stop=(kc == 7)),
                                r=[('ring', i), ('srcT', kc, tbi)], w=bk(pb))
                        P.op('act', lambda e, oc=oc, pb=pb: e.copy(out=yblk[:, oc, :], in_=bank(pb)),
                             r=bk(pb), w=[('yblk', oc)])
                post_norm_block(yblk, tbi, iG)
            P.barrier()

    def attention(L, j):
        with contextlib.ExitStack() as sta:
            qT = sb("qT", (128, 8, NT), BF16, stack=sta)
            kT = sb("kT", (128, 2, 2, NT + 512), BF16, stack=sta)
            Vd = sb("Vd", (128, 16, 4, 128), BF16, stack=sta)
            esink = sb("esink", (128, 16), stack=sta)
            P.op('dve', lambda e: e.memset(kT[:].rearrange("p a b t -> p (a b t)"), 0.0), w=[('kTz',)])
            P.dma('sp', g_esink, esink[:], at_sink[j:j + 1, :].partition_broadcast(128), w=[('esink',)])
            P.op('act', lambda e: e.activation(out=esink[:], in_=esink[:], func=AF.Exp), r=[('esink',)], w=[('esink',)])
            with contextlib.ExitStack() as stp:
                hT = sb("hT", (128, 8, NT), BF16, stack=stp)
                rope = sb("rope", (128, 2, 1024), stack=stp)
                qraw = sb("qraw", (128, 1024), stack=stp)
                t1 = sb("t1", (128, 1024), stack=stp)
                kvst = sb("kvst", (128, 2, 4, 256), stack=stp)
                ckst = sb("ckst", (128, 4, 256), stack=stp)
                P.dma('sp', g_att, rope[:], rope_t, w=[('rope',)])
                P.dma('sp', g_att, ckst[:], ck_in[j].rearrange("(b p) f -> p b f", p=128), w=[('ckst',)])
                for b in range(4):
                    for hf in range(2):
                        P.dma('pool', g_poolm, Vd[:, 12 + b, :, hf * 64:(hf + 1) * 64],
                              cv_in[j][b * 128:(b + 1) * 128, :].rearrange("p (g d) -> p g d", g=4), w=[('Vd', 12 + b)])
                pre_norm(hT, 0, 1)
                RF = cst_f[:, 10, :]

                def proj_chunk(view, col0, dstT, q8, pb):
                    U = PS[:, pb * 512: pb * 512 + NT]
                    for tbi, (t0, t1_, grp) in enumerate(TBS):
                        for kc in range(8):
                            P.op('pe', lambda e, kc=kc, tbi=tbi, t0=t0, t1_=t1_: e.matmul(
                                bank(pb + tbi), lhsT=view[:, kc, col0:col0 + 128], rhs=hT[:, kc, t0:t1_],
                                start=(kc == 0), stop=(kc == 7)),
                                r=[('ring', view_i[0])] + kk('hT', kc, tbi), w=bk(pb + tbi))
                    if dstT is kT:
                        for g2_ in range(2):
                            hp_ = slice(g2_ * 64, (g2_ + 1) * 64)
                            P.op('act', lambda e, g2_=g2_, hp_=hp_: e.copy(out=kT[hp_, q8, g2_, 0:512], in_=U[hp_, 0:512]),
                                 r=bk(pb) + [('kTz',)], w=[('qk', id(dstT), q8, 0, g2_)])
                    else:
                        P.op('act', lambda e: e.copy(out=dstT[:, q8, 0:512], in_=U[:, 0:512]),
                             r=bk(pb), w=[('qk', id(dstT), q8, 0)])
                    P.op('act', lambda e: e.copy(out=qraw[:], in_=U[:, 512:NT]), r=bk(pb + 1, pb + 2), w=[('qraw',)])
                    for hh in range(2):
                        P.op('pe', lambda e, hh=hh: e.matmul(bank(6 + hh), lhsT=RF, rhs=qraw[:, hh * 512:(hh + 1) * 512],
                                                            start=True, stop=True),
                             r=[('qraw',), ('cst',)], w=bk(6 + hh))
                    P.op('dve', lambda e: e.tensor_tensor(out=t1[:], in0=PS[:, 6 * 512: 6 * 512 + 1024], in1=rope[:, 1, :], op=ALU.mult),
                         r=bk(6, 7) + [('rope',)], w=[('t1',)])
                    P.op('dve', lambda e: e.tensor_tensor(out=qraw[:], in0=qraw[:], in1=rope[:, 0, :], op=ALU.mult),
                         r=[('qraw',), ('rope',)], w=[('qraw',)])
                    if dstT is kT:
                        for g2_ in range(2):
                            hp_ = slice(g2_ * 64, (g2_ + 1) * 64)
                            P.op('dve', lambda e, g2_=g2_, hp_=hp_: e.tensor_tensor(out=kT[hp_, q8, g2_, 512:NT], in0=qraw[hp_, :], in1=t1[hp_, :], op=ALU.add),
                                 r=[('qraw',), ('t1',), ('kTz',)], w=[('qk', id(dstT), q8, 1, g2_)])
                    else:
                        P.op('dve', lambda e: e.tensor_tensor(out=dstT[:, q8, 512:NT], in0=qraw[:], in1=t1[:], op=ALU.add),
                             r=[('qraw',), ('t1',)], w=[('qk', id(dstT), q8, 1)])

                view_i = [0]
                for c in range(2):
                    i = next_slot()
                    view_i[0] = i
                    view = ring[i][:, 0:4096].rearrange("p (k n) -> p k n", k=8)
                    for g2 in range(2):
                        for r4 in range(4):
                            col = ((2 * c + g2) * 4 + r4) * 64
                            P.dma('pool', ring_g[i], view[:, :, r4 * 128 + g2 * 64: r4 * 128 + g2 * 64 + 64],
                                  at_w_qkv[j][:, col:col + 64].rearrange("(k p) n -> p k n", p=128), w=[('ring', i)])
                    for r4 in range(4):
                        proj_chunk(view, r4 * 128, qT, c * 4 + r4, (r4 % 2) * 3)
                i, view = load_w('pool', at_w_qkv[j][:, 1024:1536], 8, 512, 'kv')
                view_i[0] = i
                for c in range(2):
                    proj_chunk(view, c * 128, kT, c, c * 3)
                for c in range(2):
                    for b in range(4):
                        P.op('pe', lambda e, c=c, b=b: e.matmul(PS[:, c * 512 + b * 128: c * 512 + b * 128 + 128],
                                                                lhsT=ckst[:, b, c * 128:(c + 1) * 128], rhs=IDF, start=True, stop=True),
                             r=[('ckst',), ('cst',)], w=bk(c))
                    for g2_ in range(2):
                        hp_ = slice(g2_ * 64, (g2_ + 1) * 64)
                        P.op('act', lambda e, c=c, g2_=g2_, hp_=hp_: e.copy(out=kT[hp_, c, g2_, NT:NT + 512], in_=bank(c)[hp_, :]),
                             r=bk(c) + [('kTz',)], w=[('qk', id(kT), c, 2, g2_)])
                for tt in range(12):
                    kinds = (0, 1) if tt < 4 else (1,)
                    for kind in kinds:
                        pb = 2 + kind
                        for kc in range(8):
                            P.op('pe', lambda e, kc=kc, kind=kind, pb=pb, tt=tt: e.matmul(
                                PS[:, pb * 512: pb * 512 + 256], lhsT=hT[:, kc, tt * 128:(tt + 1) * 128],
                                rhs=view[:, kc, kind * 256:(kind + 1) * 256], start=(kc == 0), stop=(kc == 7)),
                                r=[('ring', i)] + kk('hT', kc, tt // 4), w=bk(pb))
                        if tt < 4:
                            P.op('act', lambda e, kind=kind, pb=pb, tt=tt: e.copy(out=kvst[:, kind, tt, :], in_=PS[:, pb * 512: pb * 512 + 256]),
                                 r=bk(pb), w=[('kvst', kind, tt)])
                            dst = ck_out if kind == 0 else cv_out
                            P.dma('sp', g_kv, dst[tt // 2, j, (tt % 2) * 128:(tt % 2) * 128 + 128, :], kvst[:, kind, tt, :],
                                  r=[('kvst', kind, tt)])
                        if kind == 1:
                            src = PS[:, pb * 512: pb * 512 + 256].rearrange("p (g d) -> p g d", g=4)
                            P.op('act', lambda e, tt=tt, src=src: e.copy(out=Vd[:, tt, :, 0:64], in_=src), r=bk(pb), w=[('Vd', tt)])
                            P.op('dve', lambda e, tt=tt, src=src: e.tensor_copy(out=Vd[:, tt, :, 64:128], in_=src), r=bk(pb), w=[('Vd', tt)])
                P.barrier()
            with contextlib.ExitStack() as stc:
                oT = sb("oT", (128, 8, NT), BF16, stack=stc)
                PT = [sb("PT%d" % i, (128, 512), BF16, stack=stc) for i in range(4)]
                rden = sb("rden", (128, 512), stack=stc)
                pti = [0]
                sbi = [0]
                obi = [0]
                MPREV = cst_b[:, 3, :]
                MNEXT = cst_b[:, 2, :]
                jobs = []
                for sq_ in range(2):
                    for qb in range(2):
                        t0 = sq_ * 2
                        jobs.append((t0 + qb, [(t0, t0 * 128, None), (t0 + 1, (t0 + 1) * 128, None)]))
                for qb in range(8):
                    kbs = []
                    if qb > 0:
                        kbs.append((4 + qb - 1, (4 + qb - 1) * 128, MPREV))
                    kbs.append((4 + qb, (4 + qb) * 128, None))
                    if qb < 7:
                        kbs.append((4 + qb + 1, (4 + qb + 1) * 128, MNEXT))
                    for b in range(4):
                        kbs.append((12 + b, NT + b * 128, None))
                    jobs.append((4 + qb, kbs))
                steps = []
                for (qt, kbs) in jobs:
                    for g in range(4):
                        ob = 3 + obi[0] % 2
                        db = 5 + obi[0] % 2
                        obi[0] += 1
                        for ki, (vb, kcol, msk) in enumerate(kbs):
                            steps.append(dict(qt=qt, g=g, ob=ob, db=db, ki=ki, nk=len(kbs), vb=vb, kcol=kcol, msk=msk))
                LA = 2

                def emit_scores(st, idx):
                    g, qc0 = st['g'], st['qt'] * 128
                    c, g2 = g // 2, g % 2
                    sbk = idx % 3
                    pt = PT[idx % 4]
                    ptk = ('PT', idx % 4)
                    kcol, msk = st['kcol'], st['msk']
                    P.op('pe', lambda e: e.matmul(
                        bank(sbk), lhsT=kT[:, c, g2, kcol:kcol + 128], rhs=qT[:, c * 4:(c + 1) * 4, qc0:qc0 + 128],
                        start=True, stop=True), r=[], w=bk(sbk))
                    P.op('act', lambda e: e.activation(out=pt[:], in_=bank(sbk), func=AF.Exp, scale=0.125),
                         r=bk(sbk), w=[ptk])
                    if msk is not None:
                        P.op('dve', lambda e: e.tensor_tensor(
                            out=pt[:].rearrange("p (r q) -> p r q", r=4), in0=pt[:].rearrange("p (r q) -> p r q", r=4),
                            in1=msk.unsqueeze(1).to_broadcast([128, 4, 128]), op=ALU.mult), r=[ptk, ('cstb',)], w=[ptk])

                def emit_pv(st, idx):
                    g, qc0, ob, db, ki, nk, vb = st['g'], st['qt'] * 128, st['ob'], st['db'], st['ki'], st['nk'], st['vb']
                    pt = PT[idx % 4]
                    ptk = ('PT', idx % 4)
                    P.op('pe', lambda e: e.matmul(bank(ob), lhsT=Vd[:, vb, g, :], rhs=pt[:], start=(ki == 0), stop=(ki == nk - 1)),
                         r=[ptk, ('Vd', vb)], w=bk(ob))
                    P.op('pe', lambda e: e.matmul(bank(db), lhsT=ONESB, rhs=pt[:], start=(ki == 0), stop=(ki == nk - 1)),
                         r=[ptk, ('cstb',)], w=bk(db))
                    if ki != nk - 1:
                        return
                    P.op('dve', lambda e: e.tensor_tensor(
                        out=rden[:].rearrange("p (r q) -> p r q", r=4), in0=bank(db).rearrange("p (r q) -> p r q", r=4),
                        in1=esink[:, 4 * g:4 * g + 4].unsqueeze(2).to_broadcast([128, 4, 128]), op=ALU.add),
                        r=bk(db) + [('esink',)], w=[('rden',)])
                    P.op('act', lambda e: e.activation(out=rden[:], in_=rden[:], func=AF.Ln), r=[('rden',)], w=[('rden',)])
                    P.op('act', lambda e: e.activation(out=rden[:], in_=rden[:], func=AF.Exp, scale=-1.0), r=[('rden',)], w=[('rden',)])
                    for hf in range(2):
                        hq = slice(hf * 64, (hf + 1) * 64)
                        o4 = bank(ob).rearrange("p (a b q) -> p a b q", a=2, b=2)
                        r4 = rden[:].rearrange("p (a b q) -> p a b q", a=2, b=2)
                        P.op('dve', lambda e, hq=hq, hf=hf, o4=o4, r4=r4: e.tensor_tensor(
                            out=oT[hq, 2 * g:2 * g + 2, qc0:qc0 + 128], in0=o4[hq, :, hf, :], in1=r4[hq, :, hf, :], op=ALU.mult),
                            r=bk(ob) + [('rden',)], w=kk('srcT', (2 * g, 2 * g + 1), st['qt'] // 4))

                for idx in range(len(steps) + LA):
                    if idx < len(steps):
                        emit_scores(steps[idx], idx)
                    if idx - LA >= 0:
                        emit_pv(steps[idx - LA], idx - LA)
                P.barrier()
                out_proj(oT, at_w_o[j], 2)

    def deltanet(L, j2):
        with contextlib.ExitStack() as std0:
            ogT = sb("ogT", (128, 8, NT), BF16, stack=std0)
            deltanet_inner(L, j2, ogT)
            out_proj(ogT, dn_w_out[j2], 2)

    def deltanet_inner(L, j2, ogT):
        with contextlib.ExitStack() as std:
            hT = sb("hT", (128, 8, NT), BF16, stack=std)
            G = {n: sb("g_" + n, (128, 2, 12, 8), stack=std) for n in ("g", "b", "negb", "gc", "eg", "negeg", "egl", "bd")}
            dtb = sb("dtb", (128, 16), stack=std)
            alog = sb("alog", (128, 16), stack=std)
            ong = sb("ong", (128, 128), stack=std)
            cwd = sb("cwd", (128, 3, 24), stack=std)
            yq = sb("yq", (128, NT), stack=std)
            yq2 = sb("yq2", (128, NT), stack=std)
            yqs = [yq, yq2]
            qT = sb("dqT", (128, NT), BF16, stack=std)
            kT = sb("dkT", (128, NT), BF16, stack=std)
            k_tm = sb("k_tm", (128, 12, 128), BF16, stack=std)
            v_tm = sb("v_tm", (128, 12, 128), BF16, stack=std)
            sz = sb("sz", (128, 12, 128), BF16, stack=std)
            vT = sz[:].rearrange("p t d -> p (t d)")
            o_acc = sb("o_acc", (128, 12, 128), stack=std)
            abt = o_acc[:, 0:3, :].rearrange("p a d -> p (a d)").rearrange("p (t c) -> p t c", t=12)
            TT = sb("TT", (128, 2, 12, 128), BF16, stack=std)
            QKd = sb("QKd", (128, 2, 12, 128), BF16, stack=std)
            Drhs = [sb("Drhs%d" % i, (128, 2, 128), stack=std) for i in range(2)]
            DecT = [sb("DecT%d" % i, (128, 2, 128), BF16, stack=std) for i in range(2)]
            ZTs = [sb("ZT%d" % i, (128, 2, 128), BF16, stack=std) for i in range(2)]
            Zts = [sb("Zt%d" % i, (128, 2, 128), BF16, stack=std) for i in range(2)]
            ZdTs = [sb("ZdT%d" % i, (128, 2, 128), BF16, stack=std) for i in range(2)]
            Zds = [sb("Zd%d" % i, (128, 2, 128), BF16, stack=std) for i in range(2)]
            Ybs = [[sb("Yb%d_%d" % (i, k_), (128, 2, 128), BF16, stack=std) for k_ in range(2)] for i in range(2)]
            YTbs = [[sb("YTb%d_%d" % (i, k_), (128, 2, 128), BF16, stack=std) for k_ in range(2)] for i in range(2)]
            PTbs = [[sb("PTb%d_%d" % (i, k_), (128, 2, 128), BF16, stack=std) for k_ in range(2)] for i in range(2)]
            S32 = sb("S32", (128, 6, 128), stack=std)
            S32s = sb("S32s", (128, 6, 128), stack=std)
            Sbf = sb("Sbf", (128, 6, 128), BF16, stack=std)
            rr = sb("rr", (128, 6, 128), BF16, stack=std)
            vn = sb("vn", (128, 6, 128), BF16, stack=std)
            vn2 = sb("vn2", (128, 6, 128), BF16, stack=std)
            ssq = sb("ssq", (128, 12), stack=std)
            rst = sb("rst", (128, 12), stack=std)

            P.dma('sp', g_dnc, dtb[:], dn_dt_bias[j2:j2 + 1, :].partition_broadcast(128), w=[('dtb',)])
            P.dma('sp', g_dnc, alog[:], dn_a_log[j2:j2 + 1, :].partition_broadcast(128), w=[('alog',)])
            P.dma('sp', g_dnc, ong[:], dn_onorm_g[j2:j2 + 1, :].partition_broadcast(128), w=[('ong',)])
            for k3 in range(3):
                P.dma('sp', g_dnc, cwd[:, k3, :], dn_conv_w[j2][k3].rearrange("(c p) -> p c", p=128), w=[('cwd',)], slow=True)
            pre_norm(hT, 0, 1)
            hkeys = kk('hT', range(8), range(3))

            i, view = load_w('pool', dn_w_in[j2][:, 4096:4128], 8, 32, 'ab')
            for tt in range(12):
                for kc in range(8):
                    P.op('pe', lambda e, tt=tt, kc=kc: e.matmul(PS[:, 7 * 512 + tt * 32: 7 * 512 + tt * 32 + 32],
                                                                lhsT=hT[:, kc, tt * 128:(tt + 1) * 128], rhs=view[:, kc, :],
                                                                start=(kc == 0), stop=(kc == 7)),
                         r=[('ring', i)] + kk('hT', kc, tt // 4), w=bk(7))
            P.op('act', lambda e: e.copy(out=abt, in_=PS[:, 7 * 512: 7 * 512 + 384].rearrange("p (t c) -> p t c", t=12)),
                 r=bk(7), w=[('abt',)])

            def perm(t):
                return t[:].rearrange("p d t h -> p t d h")

            def bc16(t):
                return t[:].rearrange("p (d h) -> p d h", d=2).unsqueeze(1).to_broadcast([128, 12, 2, 8])
            P.op('act', lambda e: e.activation(out=alog[:], in_=alog[:], func=AF.Exp), r=[('alog',)], w=[('alog',)])
            P.op('dve', lambda e: e.tensor_scalar(out=alog[:], in0=alog[:], scalar1=-1.0, scalar2=None, op0=ALU.mult),
                 r=[('alog',)], w=[('alog',)])
            a4 = abt[:, :, 0:16].rearrange("p t (d h) -> p t d h", d=2)
            b4 = abt[:, :, 16:32].rearrange("p t (d h) -> p t d h", d=2)
            P.op('dve', lambda e: e.tensor_tensor(out=perm(G["g"]), in0=a4, in1=bc16(dtb), op=ALU.add),
                 r=[('abt',), ('dtb',)], w=[('G', 'g')])
            P.op('act', lambda e: e.activation(out=G["g"][:], in_=G["g"][:], func=AF.Exp), r=[('G', 'g')], w=[('G', 'g')])
            P.op('act', lambda e: e.activation(out=G["g"][:], in_=G["g"][:], func=AF.Ln, bias=1.0, scale=1.0), r=[('G', 'g')], w=[('G', 'g')])
            P.op('dve', lambda e: e.tensor_tensor(out=perm(G["g"]), in0=perm(G["g"]), in1=bc16(alog), op=ALU.mult),
                 r=[('G', 'g'), ('alog',)], w=[('G', 'g')])
            P.op('act', lambda e: e.activation(out=perm(G["b"]), in_=b4, func=AF.Sigmoid), r=[('abt',)], w=[('G', 'b')])
            P.op('dve', lambda e: e.tensor_scalar(out=G["negb"][:], in0=G["b"][:], scalar1=-1.0, scalar2=None, op0=ALU.mult),
                 r=[('G', 'b')], w=[('G', 'negb')])
            gflat = G["g"][:].rearrange("p d t h -> p (d t h)")
            P.op('pe', lambda e: e.matmul(PS[:, 7 * 512: 7 * 512 + 96], lhsT=cst_f[:, 2, :], rhs=gflat[:, 0:96], start=True, stop=True),
                 r=[('G', 'g'), ('cst',)], w=bk(7))
            P.op('pe', lambda e: e.matmul(PS[:, 7 * 512 + 96: 7 * 512 + 192], lhsT=cst_f[:, 3, :], rhs=gflat[:, 96:192], start=True, stop=True),
                 r=[('G', 'g'), ('cst',)], w=bk(7))
            P.op('pe', lambda e: e.matmul(PS[:, 7 * 512 + 192: 7 * 512 + 384], lhsT=cst_f[:, 1, :], rhs=gflat, start=True, stop=True),
                 r=[('G', 'g'), ('cst',)], w=bk(7))

            def fl(n):
                return G[n][:].rearrange("p d t h -> p (d t h)")
            P.op('act', lambda e: e.copy(out=fl("gc"), in_=PS[:, 7 * 512: 7 * 512 + 192]), r=bk(7), w=[('G', 'gc')])
            P.op('act', lambda e: e.activation(out=fl("eg"), in_=PS[:, 7 * 512: 7 * 512 + 192], func=AF.Exp), r=bk(7), w=[('G', 'eg')])
            P.op('act', lambda e: e.activation(out=fl("egl"), in_=PS[:, 7 * 512 + 192: 7 * 512 + 384], func=AF.Exp), r=bk(7), w=[('G', 'egl')])
            P.op('dve', lambda e: e.tensor_scalar(out=fl("negeg"), in0=fl("eg"), scalar1=-1.0, scalar2=None, op0=ALU.mult),
                 r=[('G', 'eg')], w=[('G', 'negeg')])
            P.op('dve', lambda e: e.tensor_tensor(out=fl("bd"), in0=PS[:, 7 * 512 + 192: 7 * 512 + 384], in1=fl("gc"), op=ALU.subtract),
                 r=bk(7) + [('G', 'gc')], w=[('G', 'bd')])
            P.op('act', lambda e: e.activation(out=fl("bd"), in_=fl("bd"), func=AF.Exp), r=[('G', 'bd')], w=[('G', 'bd')])
            P.op('dve', lambda e: e.tensor_tensor(out=fl("bd"), in0=fl("bd"), in1=fl("b"), op=ALU.mult),
                 r=[('G', 'bd'), ('G', 'b')], w=[('G', 'bd')])
            GK = [('G', n) for n in G]
            P.barrier()

            chains = []
            for d_ in range(2):
                for sidx, tiles in enumerate(([0, 1], [2, 3], list(range(4, 12)))):
                    chains.append((d_, sidx, tiles if d_ == 0 else tiles[::-1]))

            def head_gen(h):
                i = next_slot()
                view = ring[i][:, 0:4096].rearrange("p (k x n) -> p k x n", k=8, x=4)
                for X in range(4):
                    P.dma('pool', ring_g[i], view[:, :, X, :],
                          dn_w_in[j2][:, X * 1024 + h * 128: X * 1024 + (h + 1) * 128].rearrange("(k p) n -> p k n", p=128),
                          w=[('ring', i)])
                def projMM(X):
                    pb = (X % 2) * 3
                    for tbi, (t0, t1_, grp) in enumerate(TBS):
                        for kc in range(8):
                            P.op('pe', lambda e, X=X, kc=kc, tbi=tbi, t0=t0, t1_=t1_, pb=pb: e.matmul(
                                bank(pb + tbi), lhsT=view[:, kc, X, :], rhs=hT[:, kc, t0:t1_], start=(kc == 0), stop=(kc == 7)),
                                r=[('ring', i)] + kk('hT', kc, tbi), w=bk(pb + tbi))

                def projPost(X):
                    yb = yqs[X % 2]
                    ky = ('yq', X % 2)
                    pb = (X % 2) * 3
                    U = PS[:, pb * 512: pb * 512 + NT]
                    cf = X * 8 + h
                    conv_taps(yb, U, cwd[:, 0, cf:cf + 1], cwd[:, 1, cf:cf + 1], cwd[:, 2, cf:cf + 1], None,
                              [ky], bk(pb, pb + 1, pb + 2) + [('cwd',)])
                    yield
                    P.op('act', lambda e: e.activation(out=yb[:], in_=yb[:], func=AF.Silu), r=[ky], w=[ky])
                    yield
                    if X < 2:
                        dst = qT if X == 0 else kT
                        for tbi, (t0, t1_, grp) in enumerate(TBS):
                            sumsq_rstd(lambda dc, t0=t0, t1_=t1_: yb[:, t0:t1_], lambda dc: [ky], 1, 1.0, tbi, 6)
                            P.op('dve', lambda e, dst=dst, t0=t0, t1_=t1_, X=X: e.scalar_tensor_tensor(
                                out=dst[:, t0:t1_], in0=yb[:, t0:t1_], scalar=(128.0 ** -0.5 if X == 0 else 1.0), in1=rstd[:],
                                op0=ALU.mult, op1=ALU.mult), r=[ky, ('rstd',)], w=[('dq', X)])
                            yield
                    else:
                        P.op('act', lambda e: e.copy(out=vT, in_=yb[:]), r=[ky], w=[('sz',)])
                        yield

                def run_gens(gens):
                    gens = list(gens)
                    while gens:
                        for g_ in list(gens):
                            try:
                                next(g_)
                            except StopIteration:
                                gens.remove(g_)
                def run_gens_iter(gens):
                    gens = list(gens)
                    while gens:
                        for g_ in list(gens):
                            try:
                                next(g_)
                            except StopIteration:
                                gens.remove(g_)
                        yield 'A'
                projMM(0)
                yield 'A'
                projMM(1)
                yield 'A'
                yield from run_gens_iter([projPost(0), projPost(1)])
                yield 'A_pre2'
                projMM(2)
                yield 'A'
                yield from run_gens_iter([projPost(2)])
                for (src, dstm, X) in ((kT, k_tm, 1), (vT, v_tm, 2)):
                    for b4_ in range(3):
                        pb = 6 + (b4_ % 2)
                        for t4 in range(4):
                            tt = b4_ * 4 + t4
                            P.op('pe', lambda e, src=src, tt=tt, t4=t4, pb=pb: e.matmul(
                                PS[:, pb * 512 + t4 * 128: pb * 512 + t4 * 128 + 128], lhsT=src[:, tt * 128:(tt + 1) * 128], rhs=IDB,
                                start=True, stop=True), r=[(('dq', X) if X == 1 else ('sz',)), ('cstb',)], w=bk(pb))
                        P.op('act', lambda e, dstm=dstm, b4_=b4_, pb=pb: e.copy(
                            out=dstm[:, b4_ * 4:(b4_ + 1) * 4, :].rearrange("p t d -> p (t d)"), in_=bank(pb)),
                            r=bk(pb), w=[('tm', X)])
                        yield 'A'
                for b4_ in range(3):
                    pb = 6 + (b4_ % 2)
                    for t4 in range(4):
                        tt = b4_ * 4 + t4
                        for kc in range(8):
                            P.op('pe', lambda e, tt=tt, t4=t4, kc=kc, pb=pb: e.matmul(
                                PS[:, pb * 512 + t4 * 128: pb * 512 + t4 * 128 + 128], lhsT=hT[:, kc, tt * 128:(tt + 1) * 128],
                                rhs=view[:, kc, 3, :], start=(kc == 0), stop=(kc == 7)),
                                r=[('ring', i)] + kk('hT', kc, tt // 4), w=bk(pb))
                    P.op('act', lambda e, b4_=b4_, pb=pb: e.activation(
                        out=sz[:, b4_ * 4:(b4_ + 1) * 4, :].rearrange("p t d -> p (t d)"), in_=bank(pb), func=AF.Silu),
                        r=bk(pb), w=[('sz',)])
                    yield 'A'
                yield 'A_end'

                NB = 2

                def phaseC(d_, c0, bs):
                    MSK = cst_f[:, 4 + d_, :]
                    NBX = cst_f[:, 6 + d_, :]
                    LU = cst_f[:, 2 + d_, :]
                    SMT = cst_b[:, 8 + d_, :]
                    cs = [c0 + t_ for t_ in range(NB)]
                    B0 = bs * 3
                    W_ = NB * 128
                    ROLE = {0: (0, 0), 1: (1, 0), 2: (2, 0), 3: (1, 256)}

                    def bnk(i):
                        b_, o_ = ROLE[i]
                        return PS[:, (B0 + b_) * 512 + o_: (B0 + b_) * 512 + o_ + W_]

                    def bq(i, t_):
                        b_, o_ = ROLE[i]
                        return PS[:, (B0 + b_) * 512 + o_ + t_ * 128: (B0 + b_) * 512 + o_ + t_ * 128 + 128]

                    def bkr(i):
                        return bk(B0 + ROLE[i][0])

                    def b3(i):
                        return bnk(i).rearrange("p (t d) -> p t d", t=NB)

                    def fl3(t):
                        return t[:].rearrange("p t d -> p (t d)")

                    def K(n, *x):
                        return (n, bs) + x
                    Dr, De, ZT_, Zt_, ZdT_, Zd_ = Drhs[bs], DecT[bs], ZTs[bs], Zts[bs], ZdTs[bs], Zds[bs]
                    Yb_, YTb_, PTb_ = Ybs[bs], YTbs[bs], PTbs[bs]
                    BDb = cst_b[:, 11, :].unsqueeze(1).to_broadcast([128, NB, 128])
                    IDb4 = IDB.unsqueeze(1).to_broadcast([128, NB, 128])
                    for t_, cch in enumerate(cs):
                        P.op('dve', lambda e, t_=t_, cch=cch: e.scalar_tensor_tensor(
                            out=Dr[:, t_, :], in0=MSK, scalar=G["g"][:, d_, cch, h:h + 1], in1=NBX, op0=ALU.mult, op1=ALU.add),
                            r=[('cst',), ('G', 'g')], w=[K('Drhs')])
                    yield
                    for t_, cch in enumerate(cs):
                        cs_ = slice(cch * 128, (cch + 1) * 128)
                        P.op('pe', lambda e, t_=t_: e.matmul(bq(0, t_), lhsT=Dr[:, t_, :], rhs=LU, start=True, stop=True),
                             r=[K('Drhs'), ('cst',)], w=bkr(0))
                        P.op('pe', lambda e, t_=t_, cs_=cs_: e.matmul(bq(1, t_), lhsT=kT[:, cs_], rhs=kT[:, cs_], start=True, stop=True),
                             r=[('dq', 1)], w=bkr(1))
                        P.op('pe', lambda e, t_=t_, cs_=cs_: e.matmul(bq(2, t_), lhsT=kT[:, cs_], rhs=qT[:, cs_], start=True, stop=True),
                             r=[('dq', 1), ('dq', 0)], w=bkr(2))
                    yield
                    P.op('act', lambda e: e.activation(out=fl3(De), in_=bnk(0), func=AF.Exp), r=bkr(0), w=[K('DecT')])
                    yield
                    for t_, cch in enumerate(cs):
                        P.op('dve', lambda e, t_=t_, cch=cch: e.scalar_tensor_tensor(
                            out=ZT_[:, t_, :], in0=bq(1, t_), scalar=G["negb"][:, d_, cch, h:h + 1],
                            in1=De[:, t_, :], op0=ALU.mult, op1=ALU.mult), r=bkr(1) + [K('DecT'), ('G', 'negb')], w=[K('ZT')])
                    P.op('dve', lambda e: e.tensor_tensor(out=ZT_[:], in0=ZT_[:], in1=SMT.unsqueeze(1).to_broadcast([128, NB, 128]), op=ALU.mult),
                         r=[K('ZT'), ('cstb',)], w=[K('ZT')])
                    P.op('dve', lambda e: e.tensor_tensor(out=QKd[:, d_, c0:c0 + NB, :], in0=b3(2), in1=De[:], op=ALU.mult),
                         r=bkr(2) + [K('DecT')], w=[('QKd', d_, c0)])
                    yield
                    for t_ in range(NB):
                        P.op('pe', lambda e, t_=t_: e.matmul(bq(3, t_), lhsT=ZT_[:, t_, :], rhs=IDB, start=True, stop=True),
                             r=[K('ZT'), ('cstb',)], w=bkr(3))
                    yield
                    P.op('act', lambda e: e.copy(out=fl3(Zt_), in_=bnk(3)), r=bkr(3), w=[K('Zt')])
                    P.op('dve', lambda e: e.tensor_tensor(out=ZdT_[:], in0=ZT_[:], in1=BDb, op=ALU.mult), r=[K('ZT'), ('cstb',)], w=[K('ZdT')])
                    P.op('dve', lambda e: e.tensor_tensor(out=PTb_[0][:], in0=ZdT_[:], in1=IDb4, op=ALU.add), r=[K('ZdT'), ('cstb',)], w=[K('PTb', 0)])
                    yield
                    P.op('dve', lambda e: e.tensor_tensor(out=Zd_[:], in0=Zt_[:], in1=BDb, op=ALU.mult), r=[K('Zt'), ('cstb',)], w=[K('Zd')])
                    P.op('dve', lambda e: e.tensor_tensor(out=Zt_[:], in0=Zt_[:], in1=Zd_[:], op=ALU.subtract), r=[K('Zt'), K('Zd')], w=[K('Zt')])
                    yield

                    def mm4(pbk, lh, rh, kl, kr):
                        for t_ in range(NB):
                            P.op('pe', lambda e, t_=t_: e.matmul(bq(pbk, t_), lhsT=lh[:, t_, :], rhs=rh[:, t_, :], start=True, stop=True),
                                 r=[kl, kr], w=bkr(pbk))

                    def ev(eng, dst, kd, pbk):
                        if eng == 'act':
                            P.op('act', lambda e: e.copy(out=fl3(dst), in_=bnk(pbk)), r=bkr(pbk), w=[kd])
                        else:
                            P.op('dve', lambda e: e.tensor_copy(out=fl3(dst), in_=bnk(pbk)), r=bkr(pbk), w=[kd])

                    def padd(dst, kd, pbk, addend, ka, op=ALU.add):
                        P.op('dve', lambda e: e.tensor_tensor(out=dst, in0=b3(pbk), in1=addend, op=op), r=bkr(pbk) + [ka], w=[kd])
                    curY, curYT, kY, kYT = Zd_, ZdT_, K('Zd'), K('ZdT')
                    for lev in range(3):
                        ny, nyt = Yb_[lev % 2], YTb_[lev % 2]
                        mm4(0, curYT, curY, kYT, kY)
                        if lev < 2:
                            mm4(1, curY, curYT, kY, kYT)
                        yield
                        ev('act', ny, K('Yb', lev % 2), 0)
                        if lev < 2:
                            ev('act', nyt, K('YTb', lev % 2), 1)
                        yield
                        pbk = 2 + (lev % 2)
                        mm4(pbk, ny, PTb_[lev % 2], K('Yb', lev % 2), K('PTb', lev % 2))
                        yield
                        padd(PTb_[(lev + 1) % 2][:], K('PTb', (lev + 1) % 2), pbk, PTb_[lev % 2][:], K('PTb', lev % 2))
                        yield
                        curY, curYT, kY, kYT = ny, nyt, K('Yb', lev % 2), K('YTb', lev % 2)
                    TdT, kTdT = PTb_[1], K('PTb', 1)
                    for t_ in range(NB):
                        P.op('pe', lambda e, t_=t_: e.matmul(bq(0, t_), lhsT=TdT[:, t_, :], rhs=IDB, start=True, stop=True),
                             r=[kTdT, ('cstb',)], w=bkr(0))
                    mm4(1, TdT, Zt_, kTdT, K('Zt'))
                    mm4(2, Zt_, TdT, K('Zt'), kTdT)
                    yield
                    padd(ZdT_[:], K('ZdT'), 0, IDb4, ('cstb',), op=ALU.subtract)
                    ev('act', Yb_[0], K('Yb', 0), 1)
                    yield
                    ev('act', YTb_[0], K('YTb', 0), 2)
                    padd(PTb_[0][:], K('PTb', 0), 2, IDb4, ('cstb',))
                    yield
                    mm4(3, YTb_[0], Yb_[0], K('YTb', 0), K('Yb', 0))
                    mm4(0, Yb_[0], YTb_[0], K('Yb', 0), K('YTb', 0))
                    yield
                    ev('act', Yb_[1], K('Yb', 1), 3)
                    ev('act', YTb_[1], K('YTb', 1), 0)
                    yield
                    mm4(1, Yb_[1], PTb_[0], K('Yb', 1), K('PTb', 0))
                    mm4(2, YTb_[1], Yb_[1], K('YTb', 1), K('Yb', 1))
                    yield
                    padd(PTb_[1][:], K('PTb', 1), 1, PTb_[0][:], K('PTb', 0))
                    ev('act', Yb_[0], K('Yb', 0), 2)
                    yield
                    mm4(3, Yb_[0], PTb_[1], K('Yb', 0), K('PTb', 1))
                    yield
                    padd(PTb_[0][:], K('PTb', 0), 3, PTb_[1][:], K('PTb', 1))
                    yield
                    mm4(0, ZdT_, PTb_[0], K('ZdT'), K('PTb', 0))
                    yield
                    padd(TT[:, d_, c0:c0 + NB, :], ('TT', d_, c0), 0, PTb_[0][:], K('PTb', 0))
                    yield

                P.op('dve', lambda e: e.memset(o_acc[:], 0.0), w=kk('o_acc', range(12)))
                for ch, (d_, sidx, order) in enumerate(chains):
                    if sidx < 2:
                        P.op('dve', lambda e, ch=ch: e.memset(S32[:, ch, :], 0.0), w=[('S32', ch)])
                        P.op('dve', lambda e, ch=ch: e.memset(Sbf[:, ch, :], 0.0), w=[('Sbf', ch)])
                    else:
                        P.dma('sp', g_S[ch], S32[:, ch, :], state_in[j2, d_, h], w=[('S32', ch)])
                        P.op('act', lambda e, ch=ch: e.copy(out=Sbf[:, ch, :], in_=S32[:, ch, :]), r=[('S32', ch)], w=[('Sbf', ch)])

                lane_steps = [[(d_ * 3 + sidx, cch) for sidx in range(3) for cch in chains[d_ * 3 + sidx][2]] for d_ in range(2)]

                def scan_gen():
                    def q_(ln, qi):
                        return PS[:, (6 + ln) * 512 + qi * 128: (6 + ln) * 512 + qi * 128 + 128]

                    def qk_(ln):
                        return [('ps', 6 + ln)]
                    for k_ in range(12):
                        act = [(ln, lane_steps[ln][k_][0], lane_steps[ln][k_][1]) for ln in range(2)]
                        need = set((ln, (cch // 2) * 2) for ln, ch, cch in act)
                        yield need
                        for ln, ch, cch in act:
                            P.op('act', lambda e, ch=ch, cch=cch, ln=ln: e.activation(out=S32s[:, ch, :], in_=S32[:, ch, :], func=AF.Identity, bias=0.0,
                                                                                     scale=G["egl"][:, ln, cch, h:h + 1]),
                                 r=[('S32', ch), ('G', 'egl')], w=[('S32s', ch)])
                        for ln, ch, cch in act:
                            cs_ = slice(cch * 128, (cch + 1) * 128)
                            P.op('pe', lambda e, ch=ch, cs_=cs_, ln=ln: e.matmul(q_(ln, 0), lhsT=kT[:, cs_], rhs=Sbf[:, ch, :], start=True, stop=True),
                                 r=[('dq', 1), ('Sbf', ch)], w=qk_(ln))
                            P.op('pe', lambda e, ch=ch, cs_=cs_, ln=ln: e.matmul(q_(ln, 2), lhsT=qT[:, cs_], rhs=Sbf[:, ch, :], start=True, stop=True),
                                 r=[('dq', 0), ('Sbf', ch)], w=qk_(ln))
                        yield need
                        for ln, ch, cch in act:
                            P.op('dve', lambda e, ch=ch, cch=cch, ln=ln: e.scalar_tensor_tensor(
                                out=rr[:, ch, :], in0=q_(ln, 0), scalar=G["negeg"][:, ln, cch, h:h + 1], in1=v_tm[:, cch, :],
                                op0=ALU.mult, op1=ALU.add), r=qk_(ln) + [('G', 'negeg'), ('tm', 2)], w=[('rr', ch)])
                        yield need
                        for ln, ch, cch in act:
                            P.op('pe', lambda e, ch=ch, cch=cch, ln=ln: e.matmul(q_(ln, 1), lhsT=TT[:, ln, cch, :], rhs=rr[:, ch, :], start=True, stop=True),
                                 r=[('TT', ln, (cch // 2) * 2), ('rr', ch)], w=qk_(ln))
                        yield need
                        for ln, ch, cch in act:
                            P.op('act', lambda e, ch=ch, cch=cch, ln=ln: e.activation(out=vn2[:, ch, :], in_=q_(ln, 1), func=AF.Identity, bias=0.0,
                                                                                     scale=G["bd"][:, ln, cch, h:h + 1]),
                                 r=qk_(ln) + [('G', 'bd')], w=[('vn2', ch)])
                        for ln, ch, cch in act:
                            P.op('act', lambda e, ch=ch, cch=cch, ln=ln: e.activation(out=vn[:, ch, :], in_=q_(ln, 1), func=AF.Identity, bias=0.0,
                                                                                     scale=G["b"][:, ln, cch, h:h + 1]),
                                 r=qk_(ln) + [('G', 'b')], w=[('vn', ch)])
                        yield need
                        for ln, ch, cch in act:
                            P.op('pe', lambda e, ch=ch, cch=cch, ln=ln: e.matmul(q_(ln, 0), lhsT=k_tm[:, cch, :], rhs=vn2[:, ch, :], start=True, stop=True),
                                 r=[('tm', 1), ('vn2', ch)], w=qk_(ln))
                        for ln, ch, cch in act:
                            P.op('pe', lambda e, ch=ch, cch=cch, ln=ln: e.matmul(q_(ln, 3), lhsT=QKd[:, ln, cch, :], rhs=vn[:, ch, :], start=True, stop=True),
                                 r=[('QKd', ln, (cch // 2) * 2), ('vn', ch)], w=qk_(ln))
                        yield need
                        for ln, ch, cch in act:
                            P.op('dve', lambda e, ch=ch, ln=ln: e.tensor_tensor(out=Sbf[:, ch, :], in0=q_(ln, 0), in1=S32s[:, ch, :], op=ALU.add),
                                 r=qk_(ln) + [('S32s', ch)], w=[('Sbf', ch)])
                        yield need
                        for ln, ch, cch in act:
                            P.op('dve', lambda e, ch=ch, ln=ln: e.tensor_tensor(out=S32[:, ch, :], in0=q_(ln, 0), in1=S32s[:, ch, :], op=ALU.add),
                                 r=qk_(ln) + [('S32s', ch)], w=[('S32', ch)])
                            P.op('dve', lambda e, ch=ch, cch=cch, ln=ln: e.scalar_tensor_tensor(
                                out=o_acc[:, cch, :], in0=q_(ln, 2), scalar=G["eg"][:, ln, cch, h:h + 1], in1=o_acc[:, cch, :],
                                op0=ALU.mult, op1=ALU.add), r=qk_(ln) + [('o_acc', cch), ('G', 'eg')], w=[('o_acc', cch)])
                            P.op('dve', lambda e, ch=ch, cch=cch, ln=ln: e.tensor_tensor(out=o_acc[:, cch, :], in0=q_(ln, 3), in1=o_acc[:, cch, :], op=ALU.add),
                                 r=qk_(ln) + [('o_acc', cch)], w=[('o_acc', cch)])

                pending = [(0, 0), (1, 0), (0, 2), (1, 2), (0, 4), (1, 10), (0, 6), (1, 8), (0, 8), (1, 6), (0, 10), (1, 4)]
                active = [None, None]
                done = set()
                scan = scan_gen()
                scan_need = next(scan)
                while pending or any(g_ is not None for g_ in active) or scan is not None:
                    for bs in range(2):
                        if active[bs] is None and pending:
                            d_, c0 = pending.pop(0)
                            active[bs] = (phaseC(d_, c0, bs), (d_, c0))
                        if active[bs] is not None:
                            try:
                                next(active[bs][0])
                            except StopIteration:
                                done.add(active[bs][1])
                                active[bs] = None
                    if scan is not None and scan_need <= done:
                        try:
                            scan_need = next(scan)
                        except StopIteration:
                            scan = None
                for ch, (d_, sidx, order) in enumerate(chains):
                    if sidx < 2:
                        P.dma('sp', g_so, state_out[sidx, j2, d_, h], S32[:, ch, :], r=[('S32', ch)])
                yield 'C_end'

                okeys = kk('o_acc', range(12))
                yq3 = yq2[:].rearrange("p (t d) -> p t d", t=12)
                P.op('dve', lambda e: e.tensor_tensor(out=yq3, in0=o_acc[:], in1=o_acc[:], op=ALU.mult), r=okeys, w=[('yq', 1)])
                P.op('dve', lambda e: e.reduce_sum(out=ssq[:], in_=yq3, axis=AX.X), r=[('yq', 1)], w=[('ssq',)])
                yield 'E'
                P.op('act', lambda e: e.activation(out=ssq[:], in_=ssq[:], func=AF.Sqrt, bias=EPS, scale=1.0 / 128), r=[('ssq',)], w=[('ssq',)])
                yield 'E'
                P.op('dve', lambda e: e.reciprocal(out=rst[:], in_=ssq[:]), r=[('ssq',)], w=[('rst',)])
                yield 'E'
                P.op('dve', lambda e: e.tensor_tensor(out=o_acc[:], in0=o_acc[:], in1=rst[:].unsqueeze(2).to_broadcast([128, 12, 128]), op=ALU.mult),
                     r=okeys + [('rst',)], w=okeys)
                yield 'E'
                P.op('dve', lambda e: e.tensor_tensor(out=o_acc[:], in0=o_acc[:], in1=ong[:].unsqueeze(1).to_broadcast([128, 12, 128]), op=ALU.mult),
                     r=okeys + [('ong',)], w=okeys)
                yield 'E'
                P.op('dve', lambda e: e.tensor_tensor(out=v_tm[:], in0=o_acc[:], in1=sz[:], op=ALU.mult), r=okeys + [('sz',)], w=[('tm', 2)])
                yield 'E'
                for b4_ in range(3):
                    pb = 6 + (b4_ % 2)
                    for t4 in range(4):
                        tt = b4_ * 4 + t4
                        P.op('pe', lambda e, tt=tt, t4=t4, pb=pb: e.matmul(
                            PS[:, pb * 512 + t4 * 128: pb * 512 + t4 * 128 + 128], lhsT=v_tm[:, tt, :], rhs=IDB, start=True, stop=True),
                            r=[('tm', 2), ('cstb',)], w=bk(pb))
                    P.op('act', lambda e, b4_=b4_, pb=pb: e.copy(out=ogT[:, h, b4_ * 512:(b4_ + 1) * 512], in_=bank(pb)),
                         r=bk(pb), w=[('srcT', h, b4_)])
                    yield 'E'
            gens_h = [head_gen(h_) for h_ in range(8)]
            state_h = [None] * 8

            def step_h(h_):
                try:
                    state_h[h_] = next(gens_h[h_])
                except StopIteration:
                    state_h[h_] = 'done'
            while state_h[0] != 'A_end':
                step_h(0)
            for h_ in range(8):
                while state_h[h_] != 'C_end':
                    step_h(h_)
                while state_h[h_] != 'done' or (h_ + 1 < 8 and state_h[h_ + 1] != 'A_end'):
                    if state_h[h_] != 'done':
                        step_h(h_)
                    if h_ + 1 < 8 and state_h[h_ + 1] != 'A_end':
                        if not (state_h[h_ + 1] == 'A_pre2' and state_h[h_] != 'done'):
                            step_h(h_ + 1)
            P.barrier()

    with contextlib.ExitStack() as st0:
        ada_ref[0] = sb("ada_slot0", (128, SLOT), BF16, stack=st0)
        g0 = phase0_gen(st0)
        m0 = mod_stream(0)
        alive = [g0, m0]
        while alive:
            for g_ in list(alive):
                try:
                    next(g_)
                except StopIteration:
                    alive.remove(g_)
        P.barrier()
    for L in range(n_layers):
        modulation(L)
        if L % 2 == 0 and mixers in (2, 3):
            deltanet(L, L // 2)
        if L % 2 == 1 and mixers in (1, 3):
            attention(L, L // 2)
        if dbg:
            P.dma('sp', g_out, dbg_out[2 * L], xT[:], r=kk('xT', range(8), range(3)))
        pump = mod_stream(L + 1) if L + 1 < n_layers else None
        ffn(L, pump)
        if dbg:
            P.dma('sp', g_out, dbg_out[2 * L + 1], xT[:], r=kk('xT', range(8), range(3)))

    with contextlib.ExitStack() as st0:
        ostage = [sb("ostage%d" % i, (128, D), stack=st0) for i in range(2)]
        for tt in range(12):
            s = ostage[tt % 2]
            pb = (tt % 2) * 2
            for dc in range(8):
                b = pb + dc // 4
                P.op('pe', lambda e, tt=tt, dc=dc, b=b: e.matmul(
                    PS[:, b * 512 + (dc % 4) * 128: b * 512 + (dc % 4) * 128 + 128],
                    lhsT=xT[:, dc, tt * 128:(tt + 1) * 128], rhs=IDF, start=True, stop=True),
                    r=[('xT', dc, tt // 4), ('cst',)], w=bk(b))
            if tt % 2 == 0:
                P.op('act', lambda e, s=s, pb=pb: e.copy(out=s[:], in_=PS[:, pb * 512: pb * 512 + 1024]),
                     r=bk(pb, pb + 1), w=[('ostg', tt % 2)])
            else:
                P.op('dve', lambda e, s=s, pb=pb: e.tensor_copy(out=s[:], in_=PS[:, pb * 512: pb * 512 + 1024]),
                     r=bk(pb, pb + 1), w=[('ostg', tt % 2)])
            P.dma('sp', g_os[tt % 2], y_out[tt * 128:(tt + 1) * 128, :], s[:], r=[('ostg', tt % 2)])
        P.barrier()
    for g_ in out_groups:
        if g_.count > 0:
            P.eng['sp'].wait_ge(g_.sem, g_.count)
    es.close()
    return nc, P


def make_consts():
    c = np.zeros((128, 12, 128), np.float32)
    i = np.arange(128)
    m, j = np.meshgrid(i, i, indexing="ij")
    c[:, 0, :] = np.eye(128)
    c[:, 1, :] = 1.0
    c[:, 2, :] = (m <= j)
    c[:, 3, :] = (m >= j)
    c[:, 4, :] = (m > j)
    c[:, 5, :] = (m < j)
    xf = np.zeros((128, 128), np.float32)
    xf[0, 1:] = 1.0
    xf[i[1:], i[1:]] = -1.0
    c[:, 6, :] = -BIG * xf
    xb = np.zeros((128, 128), np.float32)
    xb[127, :127] = 1.0
    xb[i[:127], i[:127]] = -1.0
    c[:, 7, :] = -BIG * xb
    c[:, 8, :] = (j > m)
    c[:, 9, :] = (j < m)
    R = np.zeros((128, 128), np.float32)
    for base in range(0, 128, 32):
        for d in range(16):
            R[base + d + 16, base + d] = -1.0
            R[base + d, base + d + 16] = 1.0
    c[:, 10, :] = R
    c[:, 11, :] = (m // 16 == j // 16)
    return c


def make_rope():
    half = 32
    inv = np.power(10000.0, -np.arange(0, half, 2, dtype=np.float32) / half).astype(np.float32)
    pos = np.arange(1024)
    row = (pos // 64).astype(np.float32)
    col = (pos % 64).astype(np.float32)
    t = np.zeros((128, 2, 1024), np.float32)
    for h in range(2):
        for d in range(64):
            p = row if d < 32 else col
            f = inv[d % 16]
            ang = (p * f).astype(np.float32)
            t[h * 64 + d, 0, :] = np.cos(ang)
            t[h * 64 + d, 1, :] = np.sin(ang)
    return t


_CACHE = {}


def kernel(x_prompt, x_sample, state_delta, cache_k, cache_v, c, c_ctx,
           w_ada, b_ada, norm_g, dn_w_in, dn_conv_w, dn_a_log, dn_dt_bias, dn_onorm_g, dn_w_out,
           at_w_qkv, at_sink, at_w_o, ffn_w_up, ffn_conv_w, ffn_conv_b, ffn_w_down, _n_layers=DEPTH, _mixers=3, _dbg=False):
    f = lambda a: np.ascontiguousarray(np.asarray(a, dtype=np.float32))
    key = (_n_layers, _mixers, _dbg)
    if key not in _CACHE:
        _CACHE[key] = build_program(_n_layers, _mixers, _dbg)
    nc, _ = _CACHE[key]
    x_prompt, x_sample = f(x_prompt), f(x_sample)
    shared = {
        "w_ada": f(w_ada), "b_ada": f(b_ada), "norm_g": f(norm_g), "dn_w_in": f(dn_w_in),
        "dn_conv_w": f(dn_conv_w), "dn_a_log": f(dn_a_log).reshape(2, 16), "dn_dt_bias": f(dn_dt_bias).reshape(2, 16),
        "dn_onorm_g": f(dn_onorm_g), "dn_w_out": f(dn_w_out), "at_w_qkv": f(at_w_qkv), "at_sink": f(at_sink),
        "at_w_o": f(at_w_o), "ffn_w_up": f(ffn_w_up), "ffn_conv_w": f(ffn_conv_w), "ffn_conv_b": f(ffn_conv_b),
        "ffn_w_down": f(ffn_w_down), "cst": make_consts(), "rope_t": make_rope(),
    }
    state_delta, cache_k, cache_v, c, c_ctx = f(state_delta), f(cache_k), f(cache_v), f(c), f(c_ctx)
    in_maps = []
    for i in range(8):
        m = dict(shared)
        m["x_in"] = np.concatenate([x_prompt[2 * i], x_prompt[2 * i + 1], x_sample[i]], axis=0)
        m["state_in"] = np.ascontiguousarray(state_delta[i])
        m["ck_in"] = np.ascontiguousarray(cache_k[i].reshape(2, 512, 256))
        m["cv_in"] = np.ascontiguousarray(cache_v[i].reshape(2, 512, 256))
        m["c_in"] = np.stack([c_ctx, c[i]], axis=0)
        in_maps.append(m)
    res = run_bass_kernel_spmd(nc, in_maps, core_ids=list(range(8)))
    R = res.results
    y_prompt = np.stack([R[i]["y_out"][s * 256:(s + 1) * 256] for i in range(8) for s in range(2)], 0)
    y_sample = np.stack([R[i]["y_out"][512:] for i in range(8)], 0)
    st = np.concatenate([R[i]["state_out"] for i in range(8)], 0)
    ck = np.concatenate([R[i]["ck_out"] for i in range(8)], 0).reshape(16, 2, 256, 4, 64)
    cv = np.concatenate([R[i]["cv_out"] for i in range(8)], 0).reshape(16, 2, 256, 4, 64)
    outs = (y_prompt.astype(np.float32), y_sample.astype(np.float32), st.astype(np.float32),
            ck.astype(np.float32), cv.astype(np.float32))
    if _dbg:
        return outs, [R[i]["dbg_out"] for i in range(8)]
    return outs
```

```python
import contextlib
import os
DN_STAGE = int(os.environ.get('DN_STAGE', '5'))
import numpy as np
import concourse.bass as bass
import concourse.mybir as mybir
from concourse.bass_utils import run_bass_kernel_spmd

F32 = mybir.dt.float32
BF16 = mybir.dt.bfloat16
ALU = mybir.AluOpType
AF = mybir.ActivationFunctionType
AX = mybir.AxisListType

D = 1024
NT = 1536
DEPTH = 4
FFN = 2816
EPS = 1e-6
BIG = 200.0
SEGS = [(0, 256), (256, 512), (512, 1536)]
TBS = [(0, 512, 0), (512, 1024, 1), (1024, 1536, 1)]
SLOT = 4096
NRING = 2
FUSE_WAITS = True
FUSE_PE = True


class Grp:
    def __init__(self, name, sem):
        self.name = name
        self.sem = sem
        self.count = 0


class Prog:
    def __init__(self, nc, es, needed=None):
        self.nc = nc
        self.es = es
        self.needed = needed
        self.used = set()
        self.semval = {}
        self.inc = {}
        self.pending = None
        self.nfused = 0
        self.pe_cand = None
        self.cur_ps = False
        self.eng = {'pe': nc.tensor, 'dve': nc.vector, 'act': nc.scalar, 'pool': nc.gpsimd, 'sp': nc.sync}
        self.sem = {k: es.enter_context(nc.semaphore('s_' + k)) for k in self.eng}
        self.cnt = {k: 0 for k in self.eng}
        self.waited = {k: {} for k in self.eng}
        self.lastw = {}
        self.readers = {}
        self.groups = []
        self.nwait = 0

    def group(self, name):
        g = Grp(name, self.es.enter_context(self.nc.semaphore('g_' + name)))
        self.groups.append(g)
        return g

    def _wait(self, eng, ev):
        if ev[0] == 'c':
            src, val = ev[1], ev[2]
            if src == eng and eng in ('pe', 'sp'):
                return
            sem, name = self.sem[src], src
        else:
            g = ev[1]
            sem, name, val = g.sem, g.name, g.count
        if self.waited[eng].get(name, 0) >= val:
            return
        if self.pe_cand is not None and self.cur_ps and ev[0] == 'c':
            if self.pe_cand.get(name, 0) < val:
                self.pe_cand[name] = val
            return
        self.waited[eng][name] = val
        if ev[0] == 'c':
            self.used.add((src, val))
            if self.needed is not None:
                val = self.semval[(src, val)]
        if self.pending is not None:
            self.pending.append((sem, val))
        else:
            self.eng[eng].wait_ge(sem, val)
        self.nwait += 1

    def _deps(self, eng, r, w):
        for k in r:
            self.cur_ps = False
            ev = self.lastw.get(k)
            if ev is not None:
                self._wait(eng, ev)
            if k[0] == 'ps':
                for en2, ev2 in self.readers.get(k, {}).items():
                    if en2 != eng:
                        self._wait(eng, ev2)
        for k in w:
            self.cur_ps = (k[0] == 'ps')
            ev = self.lastw.get(k)
            if ev is not None:
                self._wait(eng, ev)
            for ev in self.readers.get(k, {}).values():
                self._wait(eng, ev)
        self.cur_ps = False

    def _record(self, ev, evname, r, w):
        for k in r:
            self.readers.setdefault(k, {})[evname] = ev
        for k in w:
            self.lastw[k] = ev
            self.readers[k] = {}

    def op(self, eng, fn, r=(), w=()):
        fuse = FUSE_WAITS and eng in ('dve', 'act')
        if fuse:
            self.pending = []
        pe_fuse = None
        if FUSE_PE and eng == 'pe':
            self.pe_cand = {}
            self._deps(eng, r, w)
            cand, self.pe_cand = self.pe_cand, None
            cand = {k_: v_ for k_, v_ in cand.items() if self.waited[eng].get(k_, 0) < v_}
            names = list(cand)
            for k_ in names[:-1]:
                self._wait(eng, ('c', k_, cand[k_]))
            if names:
                k_ = names[-1]
                self.used.add((k_, cand[k_]))
                v_ = cand[k_] if self.needed is None else self.semval[(k_, cand[k_])]
                pe_fuse = (self.sem[k_], v_)
                self.nwait += 1
        else:
            self._deps(eng, r, w)
        pend, self.pending = self.pending, None
        if fuse and pend:
            for (sem_, val_) in pend[:-1]:
                self.eng[eng].wait_ge(sem_, val_)
        ins = fn(self.eng[eng])
        if fuse and pend:
            ins._wait_ge(pend[-1][0], pend[-1][1])
            self.nfused += 1
        if pe_fuse is not None:
            ins._wait_ge(pe_fuse[0], pe_fuse[1])
            self.nfused += 1
        self.cnt[eng] += 1
        n = self.cnt[eng]
        if self.needed is None:
            ins.then_inc(self.sem[eng], 1)
        elif (eng, n) in self.needed:
            self.inc[eng] = self.inc.get(eng, 0) + 1
            self.semval[(eng, n)] = self.inc[eng]
            ins.then_inc(self.sem[eng], 1)
        self._record(('c', eng, n), eng, r, w)

    def dma(self, q, grp, out, in_, r=(), w=(), slow=False):
        self._deps(q, r, w)
        if slow:
            ins = self.eng[q].dma_start(out=out, in_=in_, allow_slow_non_contiguous=True)
        else:
            ins = self.eng[q].dma_start(out=out, in_=in_)
        ins.then_inc(grp.sem, 16)
        grp.count += 16
        self._record(('d', grp), grp.name, r, w)

    def barrier(self):
        for e in self.eng:
            for s in ('pe', 'dve', 'act', 'pool'):
                if s != e and self.cnt[s] > 0:
                    self._wait(e, ('c', s, self.cnt[s]))
            for g in self.groups:
                if g.count > 0:
                    self._wait(e, ('d', g))
        self.lastw = {}
        self.readers = {}


def kk(name, *idx):
    out = [(name,)]
    for i in idx:
        if isinstance(i, int):
            out = [o + (i,) for o in out]
        else:
            out = [o + (j,) for o in out for j in i]
    return out


def build_program(n_layers=DEPTH, mixers=3, dbg=False):
    _, P1 = build_pass(n_layers, mixers, dbg, None)
    return build_pass(n_layers, mixers, dbg, P1.used)


def build_pass(n_layers, mixers, dbg, needed):
    nc = bass.Bass("TRN2", target_bir_lowering=False)
    es = contextlib.ExitStack()
    P = Prog(nc, es, needed)

    def din(name, shape):
        return nc.dram_tensor(name, list(shape), F32, kind="ExternalInput").ap()

    def dout(name, shape):
        return nc.dram_tensor(name, list(shape), F32, kind="ExternalOutput").ap()

    xin = din("x_in", (NT, D))
    state_in = din("state_in", (2, 2, 8, 128, 128))
    ck_in = din("ck_in", (2, 512, 256))
    cv_in = din("cv_in", (2, 512, 256))
    c_in = din("c_in", (2, D))
    w_ada = din("w_ada", (DEPTH, D, 6 * D))
    b_ada = din("b_ada", (DEPTH, 6 * D))
    norm_g = din("norm_g", (DEPTH, 4, D))
    dn_w_in = din("dn_w_in", (2, D, 4128))
    dn_conv_w = din("dn_conv_w", (2, 3, 3072))
    dn_a_log = din("dn_a_log", (2, 16))
    dn_dt_bias = din("dn_dt_bias", (2, 16))
    dn_onorm_g = din("dn_onorm_g", (2, 128))
    dn_w_out = din("dn_w_out", (2, D, D))
    at_w_qkv = din("at_w_qkv", (2, D, 1536))
    at_sink = din("at_sink", (2, 16))
    at_w_o = din("at_w_o", (2, D, D))
    ffn_w_up = din("ffn_w_up", (DEPTH, D, 2 * FFN))
    ffn_conv_w = din("ffn_conv_w", (DEPTH, 3, 2 * FFN))
    ffn_conv_b = din("ffn_conv_b", (DEPTH, 2 * FFN))
    ffn_w_down = din("ffn_w_down", (DEPTH, FFN, D))
    cst = din("cst", (128, 12, 128))
    rope_t = din("rope_t", (128, 2, 1024))

    y_out = dout("y_out", (NT, D))
    state_out = dout("state_out", (2, 2, 2, 8, 128, 128))
    ck_out = dout("ck_out", (2, 2, 256, 256))
    cv_out = dout("cv_out", (2, 2, 256, 256))
    dbg_out = dout("dbg_out", (8, 128, 8, NT)) if dbg else None

    uid = [0]

    def sb(name, shape, dt=F32, stack=None):
        uid[0] += 1
        return (stack or es).enter_context(nc.sbuf_tensor("%s_%d" % (name, uid[0]), list(shape), dt))

    xT = sb("xT", (128, 8, NT))
    cst_f = sb("cst_f", (128, 12, 128))
    cst_b = sb("cst_b", (128, 12, 128), BF16)
    ring = [sb("ring%d" % i, (128, SLOT), BF16) for i in range(NRING)]
    ring_g = [P.group("ring%d" % i) for i in range(NRING)]
    ring_i = [0]
    ada_ref = [None]
    ada_g = P.group("ada")
    scT = sb("scT", (128, 8, 2), BF16)
    modT = sb("modT", (128, 48, 2))
    mAB = sb("mAB", (128, 6, 8, 2))
    ngT = sb("ngT", (128, 4, 8))
    badaT = sb("badaT", (128, 48))
    sq = [sb("sq%d" % i, (128, 512), BF16) for i in range(2)]
    rs_t = sb("rs_t", (128, 512))
    rstd = sb("rstd", (128, 512))
    ntmp = [sb("ntmp0", (128, 512))] * 2
    cw_all = sb("cw_all", (128, 3, 44))
    cb_all = sb("cb_all", (128, 44))
    PS = es.enter_context(nc.psum_tensor("PS", [128, 4096], F32))

    g_in = P.group("in")
    g_misc = P.group("misc")
    g_out = P.group("out")
    g_poolm = P.group("poolm")
    g_stg = [P.group("stg0"), P.group("stg1")]
    g_craw = P.group("craw")
    g_bada = P.group("bada")
    g_ng = P.group("ng")
    g_cwcb = P.group("cwcb")
    g_att = P.group("att")
    g_esink = P.group("esink")
    g_dnc = P.group("dnc")
    g_S = {2: P.group("S2"), 5: P.group("S5")}
    g_so = P.group("so")
    g_kv = P.group("kv")
    g_os = [P.group("os0"), P.group("os1")]
    out_groups = [g_out, g_so, g_kv, g_os[0], g_os[1]]

    IDF = cst_f[:, 0, :]
    IDB = cst_b[:, 0, :]
    ONESB = cst_b[:, 1, :]

    def bank(b):
        return PS[:, b * 512:(b + 1) * 512]

    def bk(*bs):
        return [('ps', b) for b in bs]

    def next_slot():
        i = ring_i[0] % NRING
        ring_i[0] += 1
        return i

    P.dma('sp', g_in, cst_f[:], cst, w=[('cst',)])
    P.dma('pool', g_poolm, cst_b[:], cst, w=[('cstb',)])

    with contextlib.ExitStack() as st0:
        craw = sb("craw", (128, 2, 8), stack=st0)
        for g in range(2):
            P.dma('sp', g_craw, craw[:, g, :], c_in[g:g + 1, :].rearrange("o (kc p) -> p (o kc)", p=128),
                  w=[('craw',)], slow=True)
        P.op('act', lambda e: e.activation(out=scT[:].rearrange("p k g -> p g k"), in_=craw[:], func=AF.Silu),
             r=[('craw',)], w=[('scT',)])
        P.barrier()

    def phase0_gen(st0):
        stage = [sb("xstage%d" % i, (128, D), stack=st0) for i in range(2)]
        for tt in range(12):
            s = stage[tt % 2]
            P.dma('sp', g_stg[tt % 2], s[:], xin[tt * 128:(tt + 1) * 128, :], w=[('stg', tt % 2)])
            pb = (tt % 2) * 2
            for dc in range(8):
                b = pb + dc // 4
                P.op('pe', lambda e, s=s, dc=dc, b=b: e.matmul(
                    PS[:, b * 512 + (dc % 4) * 128: b * 512 + (dc % 4) * 128 + 128],
                    lhsT=s[:, dc * 128:(dc + 1) * 128], rhs=IDF, start=True, stop=True),
                    r=[('stg', tt % 2), ('cst',)], w=bk(b))
            eng = 'act' if tt % 2 == 0 else 'dve'
            src = PS[:, pb * 512: pb * 512 + 1024].rearrange("p (c t) -> p c t", c=8)
            if eng == 'act':
                P.op('act', lambda e, tt=tt, src=src: e.copy(out=xT[:, :, tt * 128:(tt + 1) * 128], in_=src),
                     r=bk(pb, pb + 1), w=kk('xT', range(8), tt // 4))
            else:
                P.op('dve', lambda e, tt=tt, src=src: e.tensor_copy(out=xT[:, :, tt * 128:(tt + 1) * 128], in_=src),
                     r=bk(pb, pb + 1), w=kk('xT', range(8), tt // 4))
            yield

    def load_w(eng_q, src2d, kc, ncol, key):
        i = next_slot()
        view = ring[i][:, 0:kc * ncol].rearrange("p (k n) -> p k n", k=kc)
        P.dma('pool', ring_g[i], view, src2d.rearrange("(k p) n -> p k n", p=128), w=[('ring', i)])
        return i, view

    def mod_stream(L):
        mb = 7
        for piece in range(12):
            view = ada_ref[0][:, 0:4096].rearrange("p (k n) -> p k n", k=8)
            P.dma('pool', ada_g, view, w_ada[L][:, piece * 512:(piece + 1) * 512].rearrange("(k p) n -> p k n", p=128), w=[('adaslot',)])
            i = 'ada'
            for o4 in range(4):
                oc = piece * 4 + o4
                for kc in range(8):
                    P.op('pe', lambda e, view=view, o4=o4, kc=kc, oc=oc: e.matmul(
                        PS[:, mb * 512 + 2 * oc: mb * 512 + 2 * oc + 2],
                        lhsT=view[:, kc, o4 * 128:(o4 + 1) * 128], rhs=scT[:, kc, :],
                        start=(kc == 0), stop=(kc == 7)),
                        r=[('adaslot',), ('scT',)], w=bk(mb))
            yield

    def modulation(L):
        mb = 7
        P.dma('sp', g_bada, badaT[:], b_ada[L].rearrange("(oc p) -> p oc", p=128), w=[('bada',)], slow=True)
        for v in range(4):
            P.dma('sp', g_ng, ngT[:, v, :], norm_g[L][v].rearrange("(dc p) -> p dc", p=128), w=[('ngT',)], slow=True)
        P.op('dve', lambda e: e.tensor_tensor(
            out=modT[:], in0=PS[:, mb * 512: mb * 512 + 96].rearrange("p (o g) -> p o g", g=2),
            in1=badaT[:].unsqueeze(2).to_broadcast([128, 48, 2]), op=ALU.add),
            r=bk(mb) + [('bada',)], w=[('modT',)])

        def mv(v):
            return modT[:, v * 8:(v + 1) * 8, :]

        def ng(v):
            return ngT[:, v, :].unsqueeze(2).to_broadcast([128, 8, 2])
        for (dst, vs, vg) in ((0, 1, 0), (3, 4, 2)):
            P.op('dve', lambda e, dst=dst, vs=vs: e.tensor_scalar(
                out=mAB[:, dst], in0=mv(vs), scalar1=1.0, scalar2=None, op0=ALU.add),
                r=[('modT',)], w=[('mAB', dst)])
            P.op('dve', lambda e, dst=dst, vg=vg: e.tensor_tensor(
                out=mAB[:, dst], in0=mAB[:, dst], in1=ng(vg), op=ALU.mult),
                r=[('mAB', dst), ('ngT',)], w=[('mAB', dst)])
        for (dst, vs) in ((1, 0), (4, 3)):
            P.op('dve', lambda e, dst=dst, vs=vs: e.tensor_copy(out=mAB[:, dst], in_=mv(vs)),
                 r=[('modT',)], w=[('mAB', dst)])
        for (dst, vs, vg) in ((2, 2, 1), (5, 5, 3)):
            P.op('dve', lambda e, dst=dst, vs=vs, vg=vg: e.tensor_tensor(
                out=mAB[:, dst], in0=mv(vs), in1=ng(vg), op=ALU.mult),
                r=[('modT',), ('ngT',)], w=[('mAB', dst)])

    def sumsq_rstd(src_fn, src_keys_fn, nchunk, scale, tbi, psb, ncols=512):
        for dc in range(nchunk):
            s = sq[dc % 2]
            P.op('act', lambda e, s=s, dc=dc: e.activation(out=s[:, 0:ncols], in_=src_fn(dc), func=AF.Square),
                 r=src_keys_fn(dc), w=[('sq', dc % 2)])
            P.op('pe', lambda e, s=s, dc=dc: e.matmul(
                PS[:, psb * 512: psb * 512 + ncols], lhsT=ONESB, rhs=s[:, 0:ncols],
                start=(dc == 0), stop=(dc == nchunk - 1)),
                r=[('sq', dc % 2), ('cstb',)], w=bk(psb))
        P.op('act', lambda e: e.activation(out=rs_t[:, 0:ncols], in_=PS[:, psb * 512: psb * 512 + ncols],
                                           func=AF.Ln, bias=EPS, scale=scale),
             r=bk(psb), w=[('rs_t',)])
        P.op('act', lambda e: e.activation(out=rstd[:, 0:ncols], in_=rs_t[:, 0:ncols], func=AF.Exp, scale=-0.5),
             r=[('rs_t',)], w=[('rstd',)])

    def pre_norm(hT, iA, iB):
        for tbi, (t0, t1, grp) in enumerate(TBS):
            sumsq_rstd(lambda dc: xT[:, dc, t0:t1], lambda dc: [('xT', dc, tbi)], 8, 1.0 / D, tbi, 6)
            for dc in range(8):
                t = ntmp[dc % 2]
                P.op('dve', lambda e, t=t, dc=dc: e.scalar_tensor_tensor(
                    out=t[:], in0=xT[:, dc, t0:t1], scalar=mAB[:, iA, dc, grp:grp + 1], in1=rstd[:],
                    op0=ALU.mult, op1=ALU.mult),
                    r=[('xT', dc, tbi), ('mAB', iA), ('rstd',)], w=[('ntmp', 0)])
                P.op('act', lambda e, t=t, dc=dc: e.activation(
                    out=hT[:, dc, t0:t1], in_=t[:], func=AF.Identity, bias=mAB[:, iB, dc, grp:grp + 1], scale=1.0),
                    r=[('ntmp', 0), ('mAB', iB)], w=[('hT', dc, tbi)])

    def post_norm_block(yblk, tbi, iG):
        t0, t1, grp = TBS[tbi]
        sumsq_rstd(lambda dc: yblk[:, dc, :], lambda dc: [('yblk', dc)], 8, 1.0 / D, tbi, 6)
        for dc in range(8):
            t = ntmp[dc % 2]
            P.op('dve', lambda e, t=t, dc=dc: e.scalar_tensor_tensor(
                out=t[:], in0=yblk[:, dc, :], scalar=mAB[:, iG, dc, grp:grp + 1], in1=rstd[:],
                op0=ALU.mult, op1=ALU.mult),
                r=[('yblk', dc), ('mAB', iG), ('rstd',)], w=[('ntmp', 0)])
            P.op('dve', lambda e, t=t, dc=dc: e.tensor_tensor(
                out=xT[:, dc, t0:t1], in0=xT[:, dc, t0:t1], in1=t[:], op=ALU.add),
                r=[('ntmp', 0), ('xT', dc, tbi)], w=[('xT', dc, tbi)])

    def conv_taps(dst, upsum, w0, w1, w2, bias, key_dst, key_src):
        if bias is None:
            P.op('act', lambda e: e.activation(out=dst[:], in_=upsum, func=AF.Identity, bias=0.0, scale=w1),
                 r=key_src, w=key_dst)
        else:
            P.op('act', lambda e: e.activation(out=dst[:], in_=upsum, func=AF.Identity, bias=bias, scale=w1),
                 r=key_src, w=key_dst)
        u3 = upsum[:, 0:512].rearrange("p (s t) -> p s t", s=2)
        d3 = dst[:, 0:512].rearrange("p (s t) -> p s t", s=2)
        for (wt, so, do) in ((w0, 0, 1), (w2, 1, 0)):
            P.op('dve', lambda e, wt=wt, so=so, do=do: e.scalar_tensor_tensor(
                out=d3[:, :, do:do + 255], in0=u3[:, :, so:so + 255], scalar=wt, in1=d3[:, :, do:do + 255],
                op0=ALU.mult, op1=ALU.add), r=key_src + key_dst, w=key_dst)
            P.op('dve', lambda e, wt=wt, so=so, do=do: e.scalar_tensor_tensor(
                out=dst[:, 512 + do:512 + do + 1023], in0=upsum[:, 512 + so:512 + so + 1023], scalar=wt,
                in1=dst[:, 512 + do:512 + do + 1023], op0=ALU.mult, op1=ALU.add),
                r=key_src + key_dst, w=key_dst)

    def ffn(L, pump=None):
        def pump1():
            if pump is not None:
                next(pump, None)
        with contextlib.ExitStack() as stf:
            gT = sb("gT", (128, 22, NT), BF16, stack=stf)
            if pump is not None:
                ada_ref[0] = sb("ada_slot", (128, SLOT), BF16, stack=stf)
            with contextlib.ExitStack() as stu:
                hT = sb("hT", (128, 8, NT), BF16, stack=stu)
                ya = [sb("ya%d" % i, (128, NT), stack=stu) for i in range(2)]
                yb = sb("yb", (128, NT), stack=stu)
                for k3 in range(3):
                    P.dma('sp', g_cwcb, cw_all[:, k3, :], ffn_conv_w[L][k3].rearrange("(c p) -> p c", p=128), w=[('cw',)], slow=True)
                P.dma('sp', g_cwcb, cb_all[:], ffn_conv_b[L].rearrange("(c p) -> p c", p=128), w=[('cb',)], slow=True)
                pre_norm(hT, 3, 4)
                hkeys = kk('hT', range(8), range(3))
                for pr in range(11):
                    pump1()
                    i = next_slot()
                    view = ring[i][:, 0:4096].rearrange("p (k n) -> p k n", k=8)
                    P.dma('pool', ring_g[i], view[:, :, 0:256],
                          ffn_w_up[L][:, pr * 256:(pr + 1) * 256].rearrange("(k p) n -> p k n", p=128),
                          w=[('ring', i)])
                    P.dma('pool', ring_g[i], view[:, :, 256:512],
                          ffn_w_up[L][:, FFN + pr * 256:FFN + (pr + 1) * 256].rearrange("(k p) n -> p k n", p=128),
                          w=[('ring', i)])
                    for c2 in range(2):
                        j = pr * 2 + c2
                        for half in range(2):
                            pb = half * 3
                            for tbi, (t0, t1, grp) in enumerate(TBS):
                                for kc in range(8):
                                    P.op('pe', lambda e, view=view, kc=kc, half=half, c2=c2, pb=pb, tbi=tbi, t0=t0, t1=t1: e.matmul(
                                        bank(pb + tbi), lhsT=view[:, kc, half * 256 + c2 * 128: half * 256 + c2 * 128 + 128],
                                        rhs=hT[:, kc, t0:t1], start=(kc == 0), stop=(kc == 7)),
                                        r=[('ring', i)] + kk('hT', kc, tbi), w=bk(pb + tbi))
                            f = j + 22 * half
                            dst = ya[j % 2] if half == 0 else yb
                            kd = [('ya', j % 2)] if half == 0 else [('yb',)]
                            conv_taps(dst, PS[:, pb * 512: pb * 512 + NT], cw_all[:, 0, f:f + 1], cw_all[:, 1, f:f + 1],
                                      cw_all[:, 2, f:f + 1], cb_all[:, f:f + 1], kd, bk(pb, pb + 1, pb + 2) + [('cw',), ('cb',)])
                        yaj = ya[j % 2]
                        P.op('act', lambda e, yaj=yaj: e.activation(out=yaj[:], in_=yaj[:], func=AF.Silu),
                             r=[('ya', j % 2)], w=[('ya', j % 2)])
                        P.op('dve', lambda e, yaj=yaj, j=j: e.tensor_tensor(out=gT[:, j, :], in0=yaj[:], in1=yb[:], op=ALU.mult),
                             r=[('ya', j % 2), ('yb',)], w=[('gT', j)])
                P.barrier()
            with contextlib.ExitStack() as std:
                yblk = sb("yblk", (128, 8, 512), stack=std)
                for tbi, (t0, t1, grp) in enumerate(TBS):
                    for oc2 in range(8):
                        if oc2 % 4 == 0:
                            pump1()
                        i = next_slot()
                        view = ring[i][:, 0:22 * 128].rearrange("p (k n) -> p k n", k=22)
                        P.dma('pool', ring_g[i], view,
                              ffn_w_down[L][:, oc2 * 128:(oc2 + 1) * 128].rearrange("(k p) n -> p k n", p=128),
                              w=[('ring', i)])
                        pb = oc2 % 2
                        for j in range(22):
                            P.op('pe', lambda e, view=view, j=j, pb=pb, t0=t0, t1=t1: e.matmul(
                                bank(pb), lhsT=view[:, j, :], rhs=gT[:, j, t0:t1], start=(j == 0), stop=(j == 21)),
                                r=[('ring', i), ('gT', j)], w=bk(pb))
                        P.op('act', lambda e, oc2=oc2, pb=pb: e.copy(out=yblk[:, oc2, :], in_=bank(pb)),
                             r=bk(pb), w=[('yblk', oc2)])
                    post_norm_block(yblk, tbi, 5)
                if pump is not None:
                    for _ in pump:
                        pass
                P.barrier()


    def out_proj(srcT, wsrc, iG):
        with contextlib.ExitStack() as sto:
            yblk = sb("yblk", (128, 8, 512), stack=sto)
            for tbi, (t0, t1, grp) in enumerate(TBS):
                for half in range(2):
                    i, view = load_w('pool', wsrc[:, half * 512:(half + 1) * 512], 8, 512, 'wo')
                    for o4 in range(4):
                        oc = half * 4 + o4
                        pb = oc % 2
                        for kc in range(8):
                            P.op('pe', lambda e, view=view, o4=o4, kc=kc, pb=pb: e.matmul(
                                bank(pb), lhsT=view[:, kc, o4 * 128:(o4 + 1) * 128], rhs=srcT[:, kc, t0:t1],
                                start=(kc == 0), stop=(kc == 7)),
                                r=[('ring', i), ('srcT', kc, tbi)], w=bk(pb))
                        P.op('act', lambda e, oc=oc, pb=pb: e.copy(out=yblk[:, oc, :], in_=bank(pb)),
                             r=bk(pb), w=[('yblk', oc)])
                post_norm_block(yblk, tbi, iG)
            P.barrier()

    def attention(L, j):
        with contextlib.ExitStack() as sta:
            qT = sb("qT", (128, 8, NT), BF16, stack=sta)
            kT = sb("kT", (128, 2, 2, NT + 512), BF16, stack=sta)
            Vd = sb("Vd", (128, 16, 4, 128), BF16, stack=sta)
            esink = sb("esink", (128, 16), stack=sta)
            P.op('dve', lambda e: e.memset(kT[:].rearrange("p a b t -> p (a b t)"), 0.0), w=[('kTz',)])
            P.dma('sp', g_esink, esink[:], at_sink[j:j + 1, :].partition_broadcast(128), w=[('esink',)])
            P.op('act', lambda e: e.activation(out=esink[:], in_=esink[:], func=AF.Exp), r=[('esink',)], w=[('esink',)])
            with contextlib.ExitStack() as stp:
                hT = sb("hT", (128, 8, NT), BF16, stack=stp)
                rope = sb("rope", (128, 2, 1024), stack=stp)
                qraw = sb("qraw", (128, 1024), stack=stp)
                t1 = sb("t1", (128, 1024), stack=stp)
                kvst = sb("kvst", (128, 2, 4, 256), stack=stp)
                ckst = sb("ckst", (128, 4, 256), stack=stp)
                P.dma('sp', g_att, rope[:], rope_t, w=[('rope',)])
                P.dma('sp', g_att, ckst[:], ck_in[j].rearrange("(b p) f -> p b f", p=128), w=[('ckst',)])
                for b in range(4):
                    for hf in range(2):
                        P.dma('pool', g_poolm, Vd[:, 12 + b, :, hf * 64:(hf + 1) * 64],
                              cv_in[j][b * 128:(b + 1) * 128, :].rearrange("p (g d) -> p g d", g=4), w=[('Vd', 12 + b)])
                pre_norm(hT, 0, 1)
                RF = cst_f[:, 10, :]

                def proj_chunk(view, col0, dstT, q8, pb):
                    U = PS[:, pb * 512: pb * 512 + NT]
                    for tbi, (t0, t1_, grp) in enumerate(TBS):
                        for kc in range(8):
                            P.op('pe', lambda e, kc=kc, tbi=tbi, t0=t0, t1_=t1_: e.matmul(
                                bank(pb + tbi), lhsT=view[:, kc, col0:col0 + 128], rhs=hT[:, kc, t0:t1_],
                                start=(kc == 0), stop=(kc == 7)),
                                r=[('ring', view_i[0])] + kk('hT', kc, tbi), w=bk(pb + tbi))
                    if dstT is kT:
                        for g2_ in range(2):
                            hp_ = slice(g2_ * 64, (g2_ + 1) * 64)
                            P.op('act', lambda e, g2_=g2_, hp_=hp_: e.copy(out=kT[hp_, q8, g2_, 0:512], in_=U[hp_, 0:512]),
                                 r=bk(pb) + [('kTz',)], w=[('qk', id(dstT), q8, 0, g2_)])
                    else:
                        P.op('act', lambda e: e.copy(out=dstT[:, q8, 0:512], in_=U[:, 0:512]),
                             r=bk(pb), w=[('qk', id(dstT), q8, 0)])
                    P.op('act', lambda e: e.copy(out=qraw[:], in_=U[:, 512:NT]), r=bk(pb + 1, pb + 2), w=[('qraw',)])
                    for hh in range(2):
                        P.op('pe', lambda e, hh=hh: e.matmul(bank(6 + hh), lhsT=RF, rhs=qraw[:, hh * 512:(hh + 1) * 512],
                                                            start=True, stop=True),
                             r=[('qraw',), ('cst',)], w=bk(6 + hh))
                    P.op('dve', lambda e: e.tensor_tensor(out=t1[:], in0=PS[:, 6 * 512: 6 * 512 + 1024], in1=rope[:, 1, :], op=ALU.mult),
                         r=bk(6, 7) + [('rope',)], w=[('t1',)])
                    P.op('dve', lambda e: e.tensor_tensor(out=qraw[:], in0=qraw[:], in1=rope[:, 0, :], op=ALU.mult),
                         r=[('qraw',), ('rope',)], w=[('qraw',)])
                    if dstT is kT:
                        for g2_ in range(2):
                            hp_ = slice(g2_ * 64, (g2_ + 1) * 64)
                            P.op('dve', lambda e, g2_=g2_, hp_=hp_: e.tensor_tensor(out=kT[hp_, q8, g2_, 512:NT], in0=qraw[hp_, :], in1=t1[hp_, :], op=ALU.add),
                                 r=[('qraw',), ('t1',), ('kTz',)], w=[('qk', id(dstT), q8, 1, g2_)])
                    else:
                        P.op('dve', lambda e: e.tensor_tensor(out=dstT[:, q8, 512:NT], in0=qraw[:], in1=t1[:], op=ALU.add),
                             r=[('qraw',), ('t1',)], w=[('qk', id(dstT), q8, 1)])

                view_i = [0]
                for c in range(2):
                    i = next_slot()
                    view_i[0] = i
                    view = ring[i][:, 0:4096].rearrange("p (k n) -> p k n", k=8)
                    for g2 in range(2):
                        for r4 in range(4):
                            col = ((2 * c + g2) * 4 + r4) * 64
                            P.dma('pool', ring_g[i], view[:, :, r4 * 128 + g2 * 64: r4 * 128 + g2 * 64 + 64],
                                  at_w_qkv[j][:, col:col + 64].rearrange("(k p) n -> p k n", p=128), w=[('ring', i)])
                    for r4 in range(4):
                        proj_chunk(view, r4 * 128, qT, c * 4 + r4, (r4 % 2) * 3)
                i, view = load_w('pool', at_w_qkv[j][:, 1024:1536], 8, 512, 'kv')
                view_i[0] = i
                for c in range(2):
                    proj_chunk(view, c * 128, kT, c, c * 3)
                for c in range(2):
                    for b in range(4):
                        P.op('pe', lambda e, c=c, b=b: e.matmul(PS[:, c * 512 + b * 128: c * 512 + b * 128 + 128],
                                                                lhsT=ckst[:, b, c * 128:(c + 1) * 128], rhs=IDF, start=True, stop=True),
                             r=[('ckst',), ('cst',)], w=bk(c))
                    for g2_ in range(2):
                        hp_ = slice(g2_ * 64, (g2_ + 1) * 64)
                        P.op('act', lambda e, c=c, g2_=g2_, hp_=hp_: e.copy(out=kT[hp_, c, g2_, NT:NT + 512], in_=bank(c)[hp_, :]),
                             r=bk(c) + [('kTz',)], w=[('qk', id(kT), c, 2, g2_)])
                for tt in range(12):
                    kinds = (0, 1) if tt < 4 else (1,)
                    for kind in kinds:
                        pb = 2 + kind
                        for kc in range(8):
                            P.op('pe', lambda e, kc=kc, kind=kind, pb=pb, tt=tt: e.matmul(
                                PS[:, pb * 512: pb * 512 + 256], lhsT=hT[:, kc, tt * 128:(tt + 1) * 128],
                                rhs=view[:, kc, kind * 256:(kind + 1) * 256], start=(kc == 0), stop=(kc == 7)),
                                r=[('ring', i)] + kk('hT', kc, tt // 4), w=bk(pb))
                        if tt < 4:
                            P.op('act', lambda e, kind=kind, pb=pb, tt=tt: e.copy(out=kvst[:, kind, tt, :], in_=PS[:, pb * 512: pb * 512 + 256]),
                                 r=bk(pb), w=[('kvst', kind, tt)])
                            dst = ck_out if kind == 0 else cv_out
                            P.dma('sp', g_kv, dst[tt // 2, j, (tt % 2) * 128:(tt % 2) * 128 + 128, :], kvst[:, kind, tt, :],
                                  r=[('kvst', kind, tt)])
                        if kind == 1:
                            src = PS[:, pb * 512: pb * 512 + 256].rearrange("p (g d) -> p g d", g=4)
                            P.op('act', lambda e, tt=tt, src=src: e.copy(out=Vd[:, tt, :, 0:64], in_=src), r=bk(pb), w=[('Vd', tt)])
                            P.op('dve', lambda e, tt=tt, src=src: e.tensor_copy(out=Vd[:, tt, :, 64:128], in_=src), r=bk(pb), w=[('Vd', tt)])
                P.barrier()
            with contextlib.ExitStack() as stc:
                oT = sb("oT", (128, 8, NT), BF16, stack=stc)
                PT = [sb("PT%d" % i, (128, 512), BF16, stack=stc) for i in range(4)]
                rden = sb("rden", (128, 512), stack=stc)
                pti = [0]
                sbi = [0]
                obi = [0]
                MPREV = cst_b[:, 3, :]
                MNEXT = cst_b[:, 2, :]
                jobs = []
                for sq_ in range(2):
                    for qb in range(2):
                        t0 = sq_ * 2
                        jobs.append((t0 + qb, [(t0, t0 * 128, None), (t0 + 1, (t0 + 1) * 128, None)]))
                for qb in range(8):
                    kbs = []
                    if qb > 0:
                        kbs.append((4 + qb - 1, (4 + qb - 1) * 128, MPREV))
                    kbs.append((4 + qb, (4 + qb) * 128, None))
                    if qb < 7:
                        kbs.append((4 + qb + 1, (4 + qb + 1) * 128, MNEXT))
                    for b in range(4):
                        kbs.append((12 + b, NT + b * 128, None))
                    jobs.append((4 + qb, kbs))
                steps = []
                for (qt, kbs) in jobs:
                    for g in range(4):
                        ob = 3 + obi[0] % 2
                        db = 5 + obi[0] % 2
                        obi[0] += 1
                        for ki, (vb, kcol, msk) in enumerate(kbs):
                            steps.append(dict(qt=qt, g=g, ob=ob, db=db, ki=ki, nk=len(kbs), vb=vb, kcol=kcol, msk=msk))
                LA = 2

                def emit_scores(st, idx):
                    g, qc0 = st['g'], st['qt'] * 128
                    c, g2 = g // 2, g % 2
                    sbk = idx % 3
                    pt = PT[idx % 4]
                    ptk = ('PT', idx % 4)
                    kcol, msk = st['kcol'], st['msk']
                    P.op('pe', lambda e: e.matmul(
                        bank(sbk), lhsT=kT[:, c, g2, kcol:kcol + 128], rhs=qT[:, c * 4:(c + 1) * 4, qc0:qc0 + 128],
                        start=True, stop=True), r=[], w=bk(sbk))
                    P.op('act', lambda e: e.activation(out=pt[:], in_=bank(sbk), func=AF.Exp, scale=0.125),
                         r=bk(sbk), w=[ptk])
                    if msk is not None:
                        P.op('dve', lambda e: e.tensor_tensor(
                            out=pt[:].rearrange("p (r q) -> p r q", r=4), in0=pt[:].rearrange("p (r q) -> p r q", r=4),
                            in1=msk.unsqueeze(1).to_broadcast([128, 4, 128]), op=ALU.mult), r=[ptk, ('cstb',)], w=[ptk])

                def emit_pv(st, idx):
                    g, qc0, ob, db, ki, nk, vb = st['g'], st['qt'] * 128, st['ob'], st['db'], st['ki'], st['nk'], st['vb']
                    pt = PT[idx % 4]
                    ptk = ('PT', idx % 4)
                    P.op('pe', lambda e: e.matmul(bank(ob), lhsT=Vd[:, vb, g, :], rhs=pt[:], start=(ki == 0), stop=(ki == nk - 1)),
                         r=[ptk, ('Vd', vb)], w=bk(ob))
                    P.op('pe', lambda e: e.matmul(bank(db), lhsT=ONESB, rhs=pt[:], start=(ki == 0), stop=(ki == nk - 1)),
                         r=[ptk, ('cstb',)], w=bk(db))
                    if ki != nk - 1:
                        return
                    P.op('dve', lambda e: e.tensor_tensor(
                        out=rden[:].rearrange("p (r q) -> p r q", r=4), in0=bank(db).rearrange("p (r q) -> p r q", r=4),
                        in1=esink[:, 4 * g:4 * g + 4].unsqueeze(2).to_broadcast([128, 4, 128]), op=ALU.add),
                        r=bk(db) + [('esink',)], w=[('rden',)])
                    P.op('act', lambda e: e.activation(out=rden[:], in_=rden[:], func=AF.Ln), r=[('rden',)], w=[('rden',)])
                    P.op('act', lambda e: e.activation(out=rden[:], in_=rden[:], func=AF.Exp, scale=-1.0), r=[('rden',)], w=[('rden',)])
                    for hf in range(2):
                        hq = slice(hf * 64, (hf + 1) * 64)
                        o4 = bank(ob).rearrange("p (a b q) -> p a b q", a=2, b=2)
                        r4 = rden[:].rearrange("p (a b q) -> p a b q", a=2, b=2)
                        P.op('dve', lambda e, hq=hq, hf=hf, o4=o4, r4=r4: e.tensor_tensor(
                            out=oT[hq, 2 * g:2 * g + 2, qc0:qc0 + 128], in0=o4[hq, :, hf, :], in1=r4[hq, :, hf, :], op=ALU.mult),
                            r=bk(ob) + [('rden',)], w=kk('srcT', (2 * g, 2 * g + 1), st['qt'] // 4))

                for idx in range(len(steps) + LA):
                    if idx < len(steps):
                        emit_scores(steps[idx], idx)
                    if idx - LA >= 0:
                        emit_pv(steps[idx - LA], idx - LA)
                P.barrier()
                out_proj(oT, at_w_o[j], 2)

    def deltanet(L, j2):
        with contextlib.ExitStack() as std0:
            ogT = sb("ogT", (128, 8, NT), BF16, stack=std0)
            deltanet_inner(L, j2, ogT)
            out_proj(ogT, dn_w_out[j2], 2)

    def deltanet_inner(L, j2, ogT):
        with contextlib.ExitStack() as std:
            hT = sb("hT", (128, 8, NT), BF16, stack=std)
            G = {n: sb("g_" + n, (128, 2, 12, 8), stack=std) for n in ("g", "b", "negb", "gc", "eg", "negeg", "egl", "bd")}
            dtb = sb("dtb", (128, 16), stack=std)
            alog = sb("alog", (128, 16), stack=std)
            ong = sb("ong", (128, 128), stack=std)
            cwd = sb("cwd", (128, 3, 24), stack=std)
            yq = sb("yq", (128, NT), stack=std)
            yq2 = sb("yq2", (128, NT), stack=std)
            yqs = [yq, yq2]
            qT = sb("dqT", (128, NT), BF16, stack=std)
            kT = sb("dkT", (128, NT), BF16, stack=std)
            k_tm = sb("k_tm", (128, 12, 128), BF16, stack=std)
            v_tm = sb("v_tm", (128, 12, 128), BF16, stack=std)
            sz = sb("sz", (128, 12, 128), BF16, stack=std)
            vT = sz[:].rearrange("p t d -> p (t d)")
            o_acc = sb("o_acc", (128, 12, 128), stack=std)
            abt = o_acc[:, 0:3, :].rearrange("p a d -> p (a d)").rearrange("p (t c) -> p t c", t=12)
            TT = sb("TT", (128, 2, 12, 128), BF16, stack=std)
            QKd = sb("QKd", (128, 2, 12, 128), BF16, stack=std)
            Drhs = [sb("Drhs%d" % i, (128, 2, 128), stack=std) for i in range(2)]
            DecT = [sb("DecT%d" % i, (128, 2, 128), BF16, stack=std) for i in range(2)]
            ZTs = [sb("ZT%d" % i, (128, 2, 128), BF16, stack=std) for i in range(2)]
            Zts = [sb("Zt%d" % i, (128, 2, 128), BF16, stack=std) for i in range(2)]
            ZdTs = [sb("ZdT%d" % i, (128, 2, 128), BF16, stack=std) for i in range(2)]
            Zds = [sb("Zd%d" % i, (128, 2, 128), BF16, stack=std) for i in range(2)]
            Ybs = [[sb("Yb%d_%d" % (i, k_), (128, 2, 128), BF16, stack=std) for k_ in range(2)] for i in range(2)]
            YTbs = [[sb("YTb%d_%d" % (i, k_), (128, 2, 128), BF16, stack=std) for k_ in range(2)] for i in range(2)]
            PTbs = [[sb("PTb%d_%d" % (i, k_), (128, 2, 128), BF16, stack=std) for k_ in range(2)] for i in range(2)]
            S32 = sb("S32", (128, 6, 128), stack=std)
            S32s = sb("S32s", (128, 6, 128), stack=std)
            Sbf = sb("Sbf", (128, 6, 128), BF16, stack=std)
            rr = sb("rr", (128, 6, 128), BF16, stack=std)
            vn = sb("vn", (128, 6, 128), BF16, stack=std)
            vn2 = sb("vn2", (128, 6, 128), BF16, stack=std)
            ssq = sb("ssq", (128, 12), stack=std)
            rst = sb("rst", (128, 12), stack=std)

            P.dma('sp', g_dnc, dtb[:], dn_dt_bias[j2:j2 + 1, :].partition_broadcast(128), w=[('dtb',)])
            P.dma('sp', g_dnc, alog[:], dn_a_log[j2:j2 + 1, :].partition_broadcast(128), w=[('alog',)])
            P.dma('sp', g_dnc, ong[:], dn_onorm_g[j2:j2 + 1, :].partition_broadcast(128), w=[('ong',)])
            for k3 in range(3):
                P.dma('sp', g_dnc, cwd[:, k3, :], dn_conv_w[j2][k3].rearrange("(c p) -> p c", p=128), w=[('cwd',)], slow=True)
            pre_norm(hT, 0, 1)
            hkeys = kk('hT', range(8), range(3))

            i, view = load_w('pool', dn_w_in[j2][:, 4096:4128], 8, 32, 'ab')
            for tt in range(12):
                for kc in range(8):
                    P.op('pe', lambda e, tt=tt, kc=kc: e.matmul(PS[:, 7 * 512 + tt * 32: 7 * 512 + tt * 32 + 32],
                                                                lhsT=hT[:, kc, tt * 128:(tt + 1) * 128], rhs=view[:, kc, :],
                                                                start=(kc == 0), stop=(kc == 7)),
                         r=[('ring', i)] + kk('hT', kc, tt // 4), w=bk(7))
            P.op('act', lambda e: e.copy(out=abt, in_=PS[:, 7 * 512: 7 * 512 + 384].rearrange("p (t c) -> p t c", t=12)),
                 r=bk(7), w=[('abt',)])

            def perm(t):
                return t[:].rearrange("p d t h -> p t d h")

            def bc16(t):
                return t[:].rearrange("p (d h) -> p d h", d=2).unsqueeze(1).to_broadcast([128, 12, 2, 8])
            P.op('act', lambda e: e.activation(out=alog[:], in_=alog[:], func=AF.Exp), r=[('alog',)], w=[('alog',)])
            P.op('dve', lambda e: e.tensor_scalar(out=alog[:], in0=alog[:], scalar1=-1.0, scalar2=None, op0=ALU.mult),
                 r=[('alog',)], w=[('alog',)])
            a4 = abt[:, :, 0:16].rearrange("p t (d h) -> p t d h", d=2)
            b4 = abt[:, :, 16:32].rearrange("p t (d h) -> p t d h", d=2)
            P.op('dve', lambda e: e.tensor_tensor(out=perm(G["g"]), in0=a4, in1=bc16(dtb), op=ALU.add),
                 r=[('abt',), ('dtb',)], w=[('G', 'g')])
            P.op('act', lambda e: e.activation(out=G["g"][:], in_=G["g"][:], func=AF.Exp), r=[('G', 'g')], w=[('G', 'g')])
            P.op('act', lambda e: e.activation(out=G["g"][:], in_=G["g"][:], func=AF.Ln, bias=1.0, scale=1.0), r=[('G', 'g')], w=[('G', 'g')])
            P.op('dve', lambda e: e.tensor_tensor(out=perm(G["g"]), in0=perm(G["g"]), in1=bc16(alog), op=ALU.mult),
                 r=[('G', 'g'), ('alog',)], w=[('G', 'g')])
            P.op('act', lambda e: e.activation(out=perm(G["b"]), in_=b4, func=AF.Sigmoid), r=[('abt',)], w=[('G', 'b')])
            P.op('dve', lambda e: e.tensor_scalar(out=G["negb"][:], in0=G["b"][:], scalar1=-1.0, scalar2=None, op0=ALU.mult),
                 r=[('G', 'b')], w=[('G', 'negb')])
            gflat = G["g"][:].rearrange("p d t h -> p (d t h)")
            P.op('pe', lambda e: e.matmul(PS[:, 7 * 512: 7 * 512 + 96], lhsT=cst_f[:, 2, :], rhs=gflat[:, 0:96], start=True, stop=True),
                 r=[('G', 'g'), ('cst',)], w=bk(7))
            P.op('pe', lambda e: e.matmul(PS[:, 7 * 512 + 96: 7 * 512 + 192], lhsT=cst_f[:, 3, :], rhs=gflat[:, 96:192], start=True, stop=True),
                 r=[('G', 'g'), ('cst',)], w=bk(7))
            P.op('pe', lambda e: e.matmul(PS[:, 7 * 512 + 192: 7 * 512 + 384], lhsT=cst_f[:, 1, :], rhs=gflat, start=True, stop=True),
                 r=[('G', 'g'), ('cst',)], w=bk(7))

            def fl(n):
                return G[n][:].rearrange("p d t h -> p (d t h)")
            P.op('act', lambda e: e.copy(out=fl("gc"), in_=PS[:, 7 * 512: 7 * 512 + 192]), r=bk(7), w=[('G', 'gc')])
            P.op('act', lambda e: e.activation(out=fl("eg"), in_=PS[:, 7 * 512: 7 * 512 + 192], func=AF.Exp), r=bk(7), w=[('G', 'eg')])
            P.op('act', lambda e: e.activation(out=fl("egl"), in_=PS[:, 7 * 512 + 192: 7 * 512 + 384], func=AF.Exp), r=bk(7), w=[('G', 'egl')])
            P.op('dve', lambda e: e.tensor_scalar(out=fl("negeg"), in0=fl("eg"), scalar1=-1.0, scalar2=None, op0=ALU.mult),
                 r=[('G', 'eg')], w=[('G', 'negeg')])
            P.op('dve', lambda e: e.tensor_tensor(out=fl("bd"), in0=PS[:, 7 * 512 + 192: 7 * 512 + 384], in1=fl("gc"), op=ALU.subtract),
                 r=bk(7) + [('G', 'gc')], w=[('G', 'bd')])
            P.op('act', lambda e: e.activation(out=fl("bd"), in_=fl("bd"), func=AF.Exp), r=[('G', 'bd')], w=[('G', 'bd')])
            P.op('dve', lambda e: e.tensor_tensor(out=fl("bd"), in0=fl("bd"), in1=fl("b"), op=ALU.mult),
                 r=[('G', 'bd'), ('G', 'b')], w=[('G', 'bd')])
            GK = [('G', n) for n in G]
            P.barrier()

            chains = []
            for d_ in range(2):
                for sidx, tiles in enumerate(([0, 1], [2, 3], list(range(4, 12)))):
                    chains.append((d_, sidx, tiles if d_ == 0 else tiles[::-1]))

            def head_gen(h):
                i = next_slot()
                view = ring[i][:, 0:4096].rearrange("p (k x n) -> p k x n", k=8, x=4)
                for X in range(4):
                    P.dma('pool', ring_g[i], view[:, :, X, :],
                          dn_w_in[j2][:, X * 1024 + h * 128: X * 1024 + (h + 1) * 128].rearrange("(k p) n -> p k n", p=128),
                          w=[('ring', i)])
                def projMM(X):
                    pb = (X % 2) * 3
                    for tbi, (t0, t1_, grp) in enumerate(TBS):
                        for kc in range(8):
                            P.op('pe', lambda e, X=X, kc=kc, tbi=tbi, t0=t0, t1_=t1_, pb=pb: e.matmul(
                                bank(pb + tbi), lhsT=view[:, kc, X, :], rhs=hT[:, kc, t0:t1_], start=(kc == 0), stop=(kc == 7)),
                                r=[('ring', i)] + kk('hT', kc, tbi), w=bk(pb + tbi))

                def projPost(X):
                    yb = yqs[X % 2]
                    ky = ('yq', X % 2)
                    pb = (X % 2) * 3
                    U = PS[:, pb * 512: pb * 512 + NT]
                    cf = X * 8 + h
                    conv_taps(yb, U, cwd[:, 0, cf:cf + 1], cwd[:, 1, cf:cf + 1], cwd[:, 2, cf:cf + 1], None,
                              [ky], bk(pb, pb + 1, pb + 2) + [('cwd',)])
                    yield
                    P.op('act', lambda e: e.activation(out=yb[:], in_=yb[:], func=AF.Silu), r=[ky], w=[ky])
                    yield
                    if X < 2:
                        dst = qT if X == 0 else kT
                        for tbi, (t0, t1_, grp) in enumerate(TBS):
                            sumsq_rstd(lambda dc, t0=t0, t1_=t1_: yb[:, t0:t1_], lambda dc: [ky], 1, 1.0, tbi, 6)
                            P.op('dve', lambda e, dst=dst, t0=t0, t1_=t1_, X=X: e.scalar_tensor_tensor(
                                out=dst[:, t0:t1_], in0=yb[:, t0:t1_], scalar=(128.0 ** -0.5 if X == 0 else 1.0), in1=rstd[:],
                                op0=ALU.mult, op1=ALU.mult), r=[ky, ('rstd',)], w=[('dq', X)])
                            yield
                    else:
                        P.op('act', lambda e: e.copy(out=vT, in_=yb[:]), r=[ky], w=[('sz',)])
                        yield

                def run_gens(gens):
                    gens = list(gens)
                    while gens:
                        for g_ in list(gens):
                            try:
                                next(g_)
                            except StopIteration:
                                gens.remove(g_)
                def run_gens_iter(gens):
                    gens = list(gens)
                    while gens:
                        for g_ in list(gens):
                            try:
                                next(g_)
                            except StopIteration:
                                gens.remove(g_)
                        yield 'A'
                projMM(0)
                yield 'A'
                projMM(1)
                yield 'A'
                yield from run_gens_iter([projPost(0), projPost(1)])
                yield 'A_pre2'
                projMM(2)
                yield 'A'
                yield from run_gens_iter([projPost(2)])
                for (src, dstm, X) in ((kT, k_tm, 1), (vT, v_tm, 2)):
                    for b4_ in range(3):
                        pb = 6 + (b4_ % 2)
                        for t4 in range(4):
                            tt = b4_ * 4 + t4
                            P.op('pe', lambda e, src=src, tt=tt, t4=t4, pb=pb: e.matmul(
                                PS[:, pb * 512 + t4 * 128: pb * 512 + t4 * 128 + 128], lhsT=src[:, tt * 128:(tt + 1) * 128], rhs=IDB,
                                start=True, stop=True), r=[(('dq', X) if X == 1 else ('sz',)), ('cstb',)], w=bk(pb))
                        P.op('act', lambda e, dstm=dstm, b4_=b4_, pb=pb: e.copy(
                            out=dstm[:, b4_ * 4:(b4_ + 1) * 4, :].rearrange("p t d -> p (t d)"), in_=bank(pb)),
                            r=bk(pb), w=[('tm', X)])
                        yield 'A'
                for b4_ in range(3):
                    pb = 6 + (b4_ % 2)
                    for t4 in range(4):
                        tt = b4_ * 4 + t4
                        for kc in range(8):
                            P.op('pe', lambda e, tt=tt, t4=t4, kc=kc, pb=pb: e.matmul(
                                PS[:, pb * 512 + t4 * 128: pb * 512 + t4 * 128 + 128], lhsT=hT[:, kc, tt * 128:(tt + 1) * 128],
                                rhs=view[:, kc, 3, :], start=(kc == 0), stop=(kc == 7)),
                                r=[('ring', i)] + kk('hT', kc, tt // 4), w=bk(pb))
                    P.op('act', lambda e, b4_=b4_, pb=pb: e.activation(
                        out=sz[:, b4_ * 4:(b4_ + 1) * 4, :].rearrange("p t d -> p (t d)"), in_=bank(pb), func=AF.Silu),
                        r=bk(pb), w=[('sz',)])
                    yield 'A'
                yield 'A_end'

                NB = 2

                def phaseC(d_, c0, bs):
                    MSK = cst_f[:, 4 + d_, :]
                    NBX = cst_f[:, 6 + d_, :]
                    LU = cst_f[:, 2 + d_, :]
                    SMT = cst_b[:, 8 + d_, :]
                    cs = [c0 + t_ for t_ in range(NB)]
                    B0 = bs * 3
                    W_ = NB * 128
                    ROLE = {0: (0, 0), 1: (1, 0), 2: (2, 0), 3: (1, 256)}

                    def bnk(i):
                        b_, o_ = ROLE[i]
                        return PS[:, (B0 + b_) * 512 + o_: (B0 + b_) * 512 + o_ + W_]

                    def bq(i, t_):
                        b_, o_ = ROLE[i]
                        return PS[:, (B0 + b_) * 512 + o_ + t_ * 128: (B0 + b_) * 512 + o_ + t_ * 128 + 128]

                    def bkr(i):
                        return bk(B0 + ROLE[i][0])

                    def b3(i):
                        return bnk(i).rearrange("p (t d) -> p t d", t=NB)

                    def fl3(t):
                        return t[:].rearrange("p t d -> p (t d)")

                    def K(n, *x):
                        return (n, bs) + x
                    Dr, De, ZT_, Zt_, ZdT_, Zd_ = Drhs[bs], DecT[bs], ZTs[bs], Zts[bs], ZdTs[bs], Zds[bs]
                    Yb_, YTb_, PTb_ = Ybs[bs], YTbs[bs], PTbs[bs]
                    BDb = cst_b[:, 11, :].unsqueeze(1).to_broadcast([128, NB, 128])
                    IDb4 = IDB.unsqueeze(1).to_broadcast([128, NB, 128])
                    for t_, cch in enumerate(cs):
                        P.op('dve', lambda e, t_=t_, cch=cch: e.scalar_tensor_tensor(
                            out=Dr[:, t_, :], in0=MSK, scalar=G["g"][:, d_, cch, h:h + 1], in1=NBX, op0=ALU.mult, op1=ALU.add),
                            r=[('cst',), ('G', 'g')], w=[K('Drhs')])
                    yield
                    for t_, cch in enumerate(cs):
                        cs_ = slice(cch * 128, (cch + 1) * 128)
                        P.op('pe', lambda e, t_=t_: e.matmul(bq(0, t_), lhsT=Dr[:, t_, :], rhs=LU, start=True, stop=True),
                             r=[K('Drhs'), ('cst',)], w=bkr(0))
                        P.op('pe', lambda e, t_=t_, cs_=cs_: e.matmul(bq(1, t_), lhsT=kT[:, cs_], rhs=kT[:, cs_], start=True, stop=True),
                             r=[('dq', 1)], w=bkr(1))
                        P.op('pe', lambda e, t_=t_, cs_=cs_: e.matmul(bq(2, t_), lhsT=kT[:, cs_], rhs=qT[:, cs_], start=True, stop=True),
                             r=[('dq', 1), ('dq', 0)], w=bkr(2))
                    yield
                    P.op('act', lambda e: e.activation(out=fl3(De), in_=bnk(0), func=AF.Exp), r=bkr(0), w=[K('DecT')])
                    yield
                    for t_, cch in enumerate(cs):
                        P.op('dve', lambda e, t_=t_, cch=cch: e.scalar_tensor_tensor(
                            out=ZT_[:, t_, :], in0=bq(1, t_), scalar=G["negb"][:, d_, cch, h:h + 1],
                            in1=De[:, t_, :], op0=ALU.mult, op1=ALU.mult), r=bkr(1) + [K('DecT'), ('G', 'negb')], w=[K('ZT')])
                    P.op('dve', lambda e: e.tensor_tensor(out=ZT_[:], in0=ZT_[:], in1=SMT.unsqueeze(1).to_broadcast([128, NB, 128]), op=ALU.mult),
                         r=[K('ZT'), ('cstb',)], w=[K('ZT')])
                    P.op('dve', lambda e: e.tensor_tensor(out=QKd[:, d_, c0:c0 + NB, :], in0=b3(2), in1=De[:], op=ALU.mult),
                         r=bkr(2) + [K('DecT')], w=[('QKd', d_, c0)])
                    yield
                    for t_ in range(NB):
                        P.op('pe', lambda e, t_=t_: e.matmul(bq(3, t_), lhsT=ZT_[:, t_, :], rhs=IDB, start=True, stop=True),
                             r=[K('ZT'), ('cstb',)], w=bkr(3))
                    yield
                    P.op('act', lambda e: e.copy(out=fl3(Zt_), in_=bnk(3)), r=bkr(3), w=[K('Zt')])
                    P.op('dve', lambda e: e.tensor_tensor(out=ZdT_[:], in0=ZT_[:], in1=BDb, op=ALU.mult), r=[K('ZT'), ('cstb',)], w=[K('ZdT')])
                    P.op('dve', lambda e: e.tensor_tensor(out=PTb_[0][:], in0=ZdT_[:], in1=IDb4, op=ALU.add), r=[K('ZdT'), ('cstb',)], w=[K('PTb', 0)])
                    yield
                    P.op('dve', lambda e: e.tensor_tensor(out=Zd_[:], in0=Zt_[:], in1=BDb, op=ALU.mult), r=[K('Zt'), ('cstb',)], w=[K('Zd')])
                    P.op('dve', lambda e: e.tensor_tensor(out=Zt_[:], in0=Zt_[:], in1=Zd_[:], op=ALU.subtract), r=[K('Zt'), K('Zd')], w=[K('Zt')])
                    yield

                    def mm4(pbk, lh, rh, kl, kr):
                        for t_ in range(NB):
                            P.op('pe', lambda e, t_=t_: e.matmul(bq(pbk, t_), lhsT=lh[:, t_, :], rhs=rh[:, t_, :], start=True, stop=True),
                                 r=[kl, kr], w=bkr(pbk))

                    def ev(eng, dst, kd, pbk):
                        if eng == 'act':
                            P.op('act', lambda e: e.copy(out=fl3(dst), in_=bnk(pbk)), r=bkr(pbk), w=[kd])
                        else:
                            P.op('dve', lambda e: e.tensor_copy(out=fl3(dst), in_=bnk(pbk)), r=bkr(pbk), w=[kd])

                    def padd(dst, kd, pbk, addend, ka, op=ALU.add):
                        P.op('dve', lambda e: e.tensor_tensor(out=dst, in0=b3(pbk), in1=addend, op=op), r=bkr(pbk) + [ka], w=[kd])
                    curY, curYT, kY, kYT = Zd_, ZdT_, K('Zd'), K('ZdT')
                    for lev in range(3):
                        ny, nyt = Yb_[lev % 2], YTb_[lev % 2]
                        mm4(0, curYT, curY, kYT, kY)
                        if lev < 2:
                            mm4(1, curY, curYT, kY, kYT)
                        yield
                        ev('act', ny, K('Yb', lev % 2), 0)
                        if lev < 2:
                            ev('act', nyt, K('YTb', lev % 2), 1)
                        yield
                        pbk = 2 + (lev % 2)
                        mm4(pbk, ny, PTb_[lev % 2], K('Yb', lev % 2), K('PTb', lev % 2))
                        yield
                        padd(PTb_[(lev + 1) % 2][:], K('PTb', (lev + 1) % 2), pbk, PTb_[lev % 2][:], K('PTb', lev % 2))
                        yield
                        curY, curYT, kY, kYT = ny, nyt, K('Yb', lev % 2), K('YTb', lev % 2)
                    TdT, kTdT = PTb_[1], K('PTb', 1)
                    for t_ in range(NB):
                        P.op('pe', lambda e, t_=t_: e.matmul(bq(0, t_), lhsT=TdT[:, t_, :], rhs=IDB, start=True, stop=True),
                             r=[kTdT, ('cstb',)], w=bkr(0))
                    mm4(1, TdT, Zt_, kTdT, K('Zt'))
                    mm4(2, Zt_, TdT, K('Zt'), kTdT)
                    yield
                    padd(ZdT_[:], K('ZdT'), 0, IDb4, ('cstb',), op=ALU.subtract)
                    ev('act', Yb_[0], K('Yb', 0), 1)
                    yield
                    ev('act', YTb_[0], K('YTb', 0), 2)
                    padd(PTb_[0][:], K('PTb', 0), 2, IDb4, ('cstb',))
                    yield
                    mm4(3, YTb_[0], Yb_[0], K('YTb', 0), K('Yb', 0))
                    mm4(0, Yb_[0], YTb_[0], K('Yb', 0), K('YTb', 0))
                    yield
                    ev('act', Yb_[1], K('Yb', 1), 3)
                    ev('act', YTb_[1], K('YTb', 1), 0)
                    yield
                    mm4(1, Yb_[1], PTb_[0], K('Yb', 1), K('PTb', 0))
                    mm4(2, YTb_[1], Yb_[1], K('YTb', 1), K('Yb', 1))
                    yield
                    padd(PTb_[1][:], K('PTb', 1), 1, PTb_[0][:], K('PTb', 0))
                    ev('act', Yb_[0], K('Yb', 0), 2)
                    yield
                    mm4(3, Yb_[0], PTb_[1], K('Yb', 0), K('PTb', 1))
                    yield
                    padd(PTb_[0][:], K('PTb', 0), 3, PTb_[1][:], K('PTb', 1))
                    yield
                    mm4(0, ZdT_, PTb_[0], K('ZdT'), K('PTb', 0))
                    yield
                    padd(TT[:, d_, c0:c0 + NB, :], ('TT', d_, c0), 0, PTb_[0][:], K('PTb', 0))
                    yield

                P.op('dve', lambda e: e.memset(o_acc[:], 0.0), w=kk('o_acc', range(12)))
                for ch, (d_, sidx, order) in enumerate(chains):
                    if sidx < 2:
                        P.op('dve', lambda e, ch=ch: e.memset(S32[:, ch, :], 0.0), w=[('S32', ch)])
                        P.op('dve', lambda e, ch=ch: e.memset(Sbf[:, ch, :], 0.0), w=[('Sbf', ch)])
                    else:
                        P.dma('sp', g_S[ch], S32[:, ch, :], state_in[j2, d_, h], w=[('S32', ch)])
                        P.op('act', lambda e, ch=ch: e.copy(out=Sbf[:, ch, :], in_=S32[:, ch, :]), r=[('S32', ch)], w=[('Sbf', ch)])

                lane_steps = [[(d_ * 3 + sidx, cch) for sidx in range(3) for cch in chains[d_ * 3 + sidx][2]] for d_ in range(2)]

                def scan_gen():
                    def q_(ln, qi):
                        return PS[:, (6 + ln) * 512 + qi * 128: (6 + ln) * 512 + qi * 128 + 128]

                    def qk_(ln):
                        return [('ps', 6 + ln)]
                    for k_ in range(12):
                        act = [(ln, lane_steps[ln][k_][0], lane_steps[ln][k_][1]) for ln in range(2)]
                        need = set((ln, (cch // 2) * 2) for ln, ch, cch in act)
                        yield need
                        for ln, ch, cch in act:
                            P.op('act', lambda e, ch=ch, cch=cch, ln=ln: e.activation(out=S32s[:, ch, :], in_=S32[:, ch, :], func=AF.Identity, bias=0.0,
                                                                                     scale=G["egl"][:, ln, cch, h:h + 1]),
                                 r=[('S32', ch), ('G', 'egl')], w=[('S32s', ch)])
                        for ln, ch, cch in act:
                            cs_ = slice(cch * 128, (cch + 1) * 128)
                            P.op('pe', lambda e, ch=ch, cs_=cs_, ln=ln: e.matmul(q_(ln, 0), lhsT=kT[:, cs_], rhs=Sbf[:, ch, :], start=True, stop=True),
                                 r=[('dq', 1), ('Sbf', ch)], w=qk_(ln))
                            P.op('pe', lambda e, ch=ch, cs_=cs_, ln=ln: e.matmul(q_(ln, 2), lhsT=qT[:, cs_], rhs=Sbf[:, ch, :], start=True, stop=True),
                                 r=[('dq', 0), ('Sbf', ch)], w=qk_(ln))
                        yield need
                        for ln, ch, cch in act:
                            P.op('dve', lambda e, ch=ch, cch=cch, ln=ln: e.scalar_tensor_tensor(
                                out=rr[:, ch, :], in0=q_(ln, 0), scalar=G["negeg"][:, ln, cch, h:h + 1], in1=v_tm[:, cch, :],
                                op0=ALU.mult, op1=ALU.add), r=qk_(ln) + [('G', 'negeg'), ('tm', 2)], w=[('rr', ch)])
                        yield need
                        for ln, ch, cch in act:
                            P.op('pe', lambda e, ch=ch, cch=cch, ln=ln: e.matmul(q_(ln, 1), lhsT=TT[:, ln, cch, :], rhs=rr[:, ch, :], start=True, stop=True),
                                 r=[('TT', ln, (cch // 2) * 2), ('rr', ch)], w=qk_(ln))
                        yield need
                        for ln, ch, cch in act:
                            P.op('act', lambda e, ch=ch, cch=cch, ln=ln: e.activation(out=vn2[:, ch, :], in_=q_(ln, 1), func=AF.Identity, bias=0.0,
                                                                                     scale=G["bd"][:, ln, cch, h:h + 1]),
                                 r=qk_(ln) + [('G', 'bd')], w=[('vn2', ch)])
                        for ln, ch, cch in act:
                            P.op('act', lambda e, ch=ch, cch=cch, ln=ln: e.activation(out=vn[:, ch, :], in_=q_(ln, 1), func=AF.Identity, bias=0.0,
                                                                                     scale=G["b"][:, ln, cch, h:h + 1]),
                                 r=qk_(ln) + [('G', 'b')], w=[('vn', ch)])
                        yield need
                        for ln, ch, cch in act:
                            P.op('pe', lambda e, ch=ch, cch=cch, ln=ln: e.matmul(q_(ln, 0), lhsT=k_tm[:, cch, :], rhs=vn2[:, ch, :], start=True, stop=True),
                                 r=[('tm', 1), ('vn2', ch)], w=qk_(ln))
                        for ln, ch, cch in act:
                            P.op('pe', lambda e, ch=ch, cch=cch, ln=ln: e.matmul(q_(ln, 3), lhsT=QKd[:, ln, cch, :], rhs=vn[:, ch, :], start=True, stop=True),
                                 r=[('QKd', ln, (cch // 2) * 2), ('vn', ch)], w=qk_(ln))
                        yield need
                        for ln, ch, cch in act:
                            P.op('dve', lambda e, ch=ch, ln=ln: e.tensor_tensor(out=Sbf[:, ch, :], in0=q_(ln, 0), in1=S32s[:, ch, :], op=ALU.add),
                                 r=qk_(ln) + [('S32s', ch)], w=[('Sbf', ch)])
                        yield need
                        for ln, ch, cch in act:
                            P.op('dve', lambda e, ch=ch, ln=ln: e.tensor_tensor(out=S32[:, ch, :], in0=q_(ln, 0), in1=S32s[:, ch, :], op=ALU.add),
                                 r=qk_(ln) + [('S32s', ch)], w=[('S32', ch)])
                            P.op('dve', lambda e, ch=ch, cch=cch, ln=ln: e.scalar_tensor_tensor(
                                out=o_acc[:, cch, :], in0=q_(ln, 2), scalar=G["eg"][:, ln, cch, h:h + 1], in1=o_acc[:, cch, :],
                                op0=ALU.mult, op1=ALU.add), r=qk_(ln) + [('o_acc', cch), ('G', 'eg')], w=[('o_acc', cch)])
                            P.op('dve', lambda e, ch=ch, cch=cch, ln=ln: e.tensor_tensor(out=o_acc[:, cch, :], in0=q_(ln, 3), in1=o_acc[:, cch, :], op=ALU.add),
                                 r=qk_(ln) + [('o_acc', cch)], w=[('o_acc', cch)])

                pending = [(0, 0), (1, 0), (0, 2), (1, 2), (0, 4), (1, 10), (0, 6), (1, 8), (0, 8), (1, 6), (0, 10), (1, 4)]
                active = [None, None]
                done = set()
                scan = scan_gen()
                scan_need = next(scan)
                while pending or any(g_ is not None for g_ in active) or scan is not None:
                    for bs in range(2):
                        if active[bs] is None and pending:
                            d_, c0 = pending.pop(0)
                            active[bs] = (phaseC(d_, c0, bs), (d_, c0))
                        if active[bs] is not None:
                            try:
                                next(active[bs][0])
                            except StopIteration:
                                done.add(active[bs][1])
                                active[bs] = None
                    if scan is not None and scan_need <= done:
                        try:
                            scan_need = next(scan)
                        except StopIteration:
                            scan = None
                for ch, (d_, sidx, order) in enumerate(chains):
                    if sidx < 2:
                        P.dma('sp', g_so, state_out[sidx, j2, d_, h], S32[:, ch, :], r=[('S32', ch)])
                yield 'C_end'

                okeys = kk('o_acc', range(12))
                yq3 = yq2[:].rearrange("p (t d) -> p t d", t=12)
                P.op('dve', lambda e: e.tensor_tensor(out=yq3, in0=o_acc[:], in1=o_acc[:], op=ALU.mult), r=okeys, w=[('yq', 1)])
                P.op('dve', lambda e: e.reduce_sum(out=ssq[:], in_=yq3, axis=AX.X), r=[('yq', 1)], w=[('ssq',)])
                yield 'E'
                P.op('act', lambda e: e.activation(out=ssq[:], in_=ssq[:], func=AF.Sqrt, bias=EPS, scale=1.0 / 128), r=[('ssq',)], w=[('ssq',)])
                yield 'E'
                P.op('dve', lambda e: e.reciprocal(out=rst[:], in_=ssq[:]), r=[('ssq',)], w=[('rst',)])
                yield 'E'
                P.op('dve', lambda e: e.tensor_tensor(out=o_acc[:], in0=o_acc[:], in1=rst[:].unsqueeze(2).to_broadcast([128, 12, 128]), op=ALU.mult),
                     r=okeys + [('rst',)], w=okeys)
                yield 'E'
                P.op('dve', lambda e: e.tensor_tensor(out=o_acc[:], in0=o_acc[:], in1=ong[:].unsqueeze(1).to_broadcast([128, 12, 128]), op=ALU.mult),
                     r=okeys + [('ong',)], w=okeys)
                yield 'E'
                P.op('dve', lambda e: e.tensor_tensor(out=v_tm[:], in0=o_acc[:], in1=sz[:], op=ALU.mult), r=okeys + [('sz',)], w=[('tm', 2)])
                yield 'E'
                for b4_ in range(3):
                    pb = 6 + (b4_ % 2)
                    for t4 in range(4):
                        tt = b4_ * 4 + t4
                        P.op('pe', lambda e, tt=tt, t4=t4, pb=pb: e.matmul(
                            PS[:, pb * 512 + t4 * 128: pb * 512 + t4 * 128 + 128], lhsT=v_tm[:, tt, :], rhs=IDB, start=True, stop=True),
                            r=[('tm', 2), ('cstb',)], w=bk(pb))
                    P.op('act', lambda e, b4_=b4_, pb=pb: e.copy(out=ogT[:, h, b4_ * 512:(b4_ + 1) * 512], in_=bank(pb)),
                         r=bk(pb), w=[('srcT', h, b4_)])
                    yield 'E'
            gens_h = [head_gen(h_) for h_ in range(8)]
            state_h = [None] * 8

            def step_h(h_):
                try:
                    state_h[h_] = next(gens_h[h_])
                except StopIteration:
                    state_h[h_] = 'done'
            while state_h[0] != 'A_end':
                step_h(0)
            for h_ in range(8):
                while state_h[h_] != 'C_end':
                    step_h(h_)
                while state_h[h_] != 'done' or (h_ + 1 < 8 and state_h[h_ + 1] != 'A_end'):
                    if state_h[h_] != 'done':
                        step_h(h_)
                    if h_ + 1 < 8 and state_h[h_ + 1] != 'A_end':
                        if not (state_h[h_ + 1] == 'A_pre2' and state_h[h_] != 'done'):
                            step_h(h_ + 1)
            P.barrier()

    with contextlib.ExitStack() as st0:
        ada_ref[0] = sb("ada_slot0", (128, SLOT), BF16, stack=st0)
        g0 = phase0_gen(st0)
        m0 = mod_stream(0)
        alive = [g0, m0]
        while alive:
            for g_ in list(alive):
                try:
                    next(g_)
                except StopIteration:
                    alive.remove(g_)
        P.barrier()
    for L in range(n_layers):
        modulation(L)
        if L % 2 == 0 and mixers in (2, 3):
            deltanet(L, L // 2)
        if L % 2 == 1 and mixers in (1, 3):
            attention(L, L // 2)
        if dbg:
            P.dma('sp', g_out, dbg_out[2 * L], xT[:], r=kk('xT', range(8), range(3)))
        pump = mod_stream(L + 1) if L + 1 < n_layers else None
        ffn(L, pump)
        if dbg:
            P.dma('sp', g_out, dbg_out[2 * L + 1], xT[:], r=kk('xT', range(8), range(3)))

    with contextlib.ExitStack() as st0:
        ostage = [sb("ostage%d" % i, (128, D), stack=st0) for i in range(2)]
        for tt in range(12):
            s = ostage[tt % 2]
            pb = (tt % 2) * 2
            for dc in range(8):
                b = pb + dc // 4
                P.op('pe', lambda e, tt=tt, dc=dc, b=b: e.matmul(
                    PS[:, b * 512 + (dc % 4) * 128: b * 512 + (dc % 4) * 128 + 128],
                    lhsT=xT[:, dc, tt * 128:(tt + 1) * 128], rhs=IDF, start=True, stop=True),
                    r=[('xT', dc, tt // 4), ('cst',)], w=bk(b))
            if tt % 2 == 0:
                P.op('act', lambda e, s=s, pb=pb: e.copy(out=s[:], in_=PS[:, pb * 512: pb * 512 + 1024]),
                     r=bk(pb, pb + 1), w=[('ostg', tt % 2)])
            else:
                P.op('dve', lambda e, s=s, pb=pb: e.tensor_copy(out=s[:], in_=PS[:, pb * 512: pb * 512 + 1024]),
                     r=bk(pb, pb + 1), w=[('ostg', tt % 2)])
            P.dma('sp', g_os[tt % 2], y_out[tt * 128:(tt + 1) * 128, :], s[:], r=[('ostg', tt % 2)])
        P.barrier()
    for g_ in out_groups:
        if g_.count > 0:
            P.eng['sp'].wait_ge(g_.sem, g_.count)
    es.close()
    return nc, P


def make_consts():
    c = np.zeros((128, 12, 128), np.float32)
    i = np.arange(128)
    m, j = np.meshgrid(i, i, indexing="ij")
    c[:, 0, :] = np.eye(128)
    c[:, 1, :] = 1.0
    c[:, 2, :] = (m <= j)
    c[:, 3, :] = (m >= j)
    c[:, 4, :] = (m > j)
    c[:, 5, :] = (m < j)
    xf = np.zeros((128, 128), np.float32)
    xf[0, 1:] = 1.0
    xf[i[1:], i[1:]] = -1.0
    c[:, 6, :] = -BIG * xf
    xb = np.zeros((128, 128), np.float32)
    xb[127, :127] = 1.0
    xb[i[:127], i[:127]] = -1.0
    c[:, 7, :] = -BIG * xb
    c[:, 8, :] = (j > m)
    c[:, 9, :] = (j < m)
    R = np.zeros((128, 128), np.float32)
    for base in range(0, 128, 32):
        for d in range(16):
            R[base + d + 16, base + d] = -1.0
            R[base + d, base + d + 16] = 1.0
    c[:, 10, :] = R
    c[:, 11, :] = (m // 16 == j // 16)
    return c


def make_rope():
    half = 32
    inv = np.power(10000.0, -np.arange(0, half, 2, dtype=np.float32) / half).astype(np.float32)
    pos = np.arange(1024)
    row = (pos // 64).astype(np.float32)
    col = (pos % 64).astype(np.float32)
    t = np.zeros((128, 2, 1024), np.float32)
    for h in range(2):
        for d in range(64):
            p = row if d < 32 else col
            f = inv[d % 16]
            ang = (p * f).astype(np.float32)
            t[h * 64 + d, 0, :] = np.cos(ang)
            t[h * 64 + d, 1, :] = np.sin(ang)
    return t


_CACHE = {}


def kernel(x_prompt, x_sample, state_delta, cache_k, cache_v, c, c_ctx,
           w_ada, b_ada, norm_g, dn_w_in, dn_conv_w, dn_a_log, dn_dt_bias, dn_onorm_g, dn_w_out,
           at_w_qkv, at_sink, at_w_o, ffn_w_up, ffn_conv_w, ffn_conv_b, ffn_w_down, _n_layers=DEPTH, _mixers=3, _dbg=False):
    f = lambda a: np.ascontiguousarray(np.asarray(a, dtype=np.float32))
    key = (_n_layers, _mixers, _dbg)
    if key not in _CACHE:
        _CACHE[key] = build_program(_n_layers, _mixers, _dbg)
    nc, _ = _CACHE[key]
    x_prompt, x_sample = f(x_prompt), f(x_sample)
    shared = {
        "w_ada": f(w_ada), "b_ada": f(b_ada), "norm_g": f(norm_g), "dn_w_in": f(dn_w_in),
        "dn_conv_w": f(dn_conv_w), "dn_a_log": f(dn_a_log).reshape(2, 16), "dn_dt_bias": f(dn_dt_bias).reshape(2, 16),
        "dn_onorm_g": f(dn_onorm_g), "dn_w_out": f(dn_w_out), "at_w_qkv": f(at_w_qkv), "at_sink": f(at_sink),
        "at_w_o": f(at_w_o), "ffn_w_up": f(ffn_w_up), "ffn_conv_w": f(ffn_conv_w), "ffn_conv_b": f(ffn_conv_b),
        "ffn_w_down": f(ffn_w_down), "cst": make_consts(), "rope_t": make_rope(),
    }
    state_delta, cache_k, cache_v, c, c_ctx = f(state_delta), f(cache_k), f(cache_v), f(c), f(c_ctx)
    in_maps = []
    for i in range(8):
        m = dict(shared)
        m["x_in"] = np.concatenate([x_prompt[2 * i], x_prompt[2 * i + 1], x_sample[i]], axis=0)
        m["state_in"] = np.ascontiguousarray(state_delta[i])
        m["ck_in"] = np.ascontiguousarray(cache_k[i].reshape(2, 512, 256))
        m["cv_in"] = np.ascontiguousarray(cache_v[i].reshape(2, 512, 256))
        m["c_in"] = np.stack([c_ctx, c[i]], axis=0)
        in_maps.append(m)
    res = run_bass_kernel_spmd(nc, in_maps, core_ids=list(range(8)))
    R = res.results
    y_prompt = np.stack([R[i]["y_out"][s * 256:(s + 1) * 256] for i in range(8) for s in range(2)], 0)
    y_sample = np.stack([R[i]["y_out"][512:] for i in range(8)], 0)
    st = np.concatenate([R[i]["state_out"] for i in range(8)], 0)
    ck = np.concatenate([R[i]["ck_out"] for i in range(8)], 0).reshape(16, 2, 256, 4, 64)
    cv = np.concatenate([R[i]["cv_out"] for i in range(8)], 0).reshape(16, 2, 256, 4, 64)
    outs = (y_prompt.astype(np.float32), y_sample.astype(np.float32), st.astype(np.float32),
            ck.astype(np.float32), cv.astype(np.float32))
    if _dbg:
        return outs, [R[i]["dbg_out"] for i in range(8)]
    return outs
```
